# Optimizing a Trainium2 kernel written in Bass

```python
import math
import jax, jax.numpy as jnp
from jax import lax
import numpy as np

D_MODEL = 2048
BATCH = 2
SEQ = 8192
DEPTH = 4

MEM_LEN = 256
A_WIDTH = D_MODEL // 2
B_WIDTH = D_MODEL - A_WIDTH
DIFF_HEAD_DIM = 64
DIFF_HEADS = B_WIDTH // (2 * DIFF_HEAD_DIM)
CONF_KERNEL = 31
SHORT_KERNEL = 3
C_WIDTH = D_MODEL
X_HEADS = 4
X_HEAD_DIM = D_MODEL // X_HEADS
D_FF = 4 * D_MODEL
Q_BLOCK = 128
EPS = 1e-6
N_EVEN = (DEPTH + 1) // 2
N_ODD = DEPTH // 2
EVEN_IN = 2 * A_WIDTH + 3 * B_WIDTH
ODD_IN = 3 * C_WIDTH

kernel_name = "hybrid_conformer_diffattn_shortconv_trunk"


def rms_norm(x, g):
    x32 = x.astype(jnp.float32)
    y = x32 * lax.rsqrt(jnp.mean(jnp.square(x32), axis=-1, keepdims=True) + EPS)
    return (y * g.astype(jnp.float32)).astype(x.dtype)


def layer_norm(x, g, b):
    x32 = x.astype(jnp.float32)
    mu = jnp.mean(x32, axis=-1, keepdims=True)
    var = jnp.mean(jnp.square(x32 - mu), axis=-1, keepdims=True)
    y = (x32 - mu) * lax.rsqrt(var + EPS)
    return (y * g.astype(jnp.float32) + b.astype(jnp.float32)).astype(x.dtype)


def causal_dwconv(x, w):
    k_len, ch = w.shape
    return lax.conv_general_dilated(
        x, w[:, None, :].astype(x.dtype), window_strides=(1,), padding=[(k_len - 1, 0)],
        dimension_numbers=("NWC", "WIO", "NWC"), feature_group_count=ch)


def diff_attention(q, k, v, lam):
    bn, s_len, h, _, d = q.shape
    nb = s_len // Q_BLOCK
    qb = q.reshape(bn, nb, Q_BLOCK, h, 2, d).transpose(1, 0, 2, 3, 4, 5)
    k_pos = jnp.arange(s_len)
    scale = d ** -0.5

    def block(args):
        q_blk, i = args
        s = jnp.einsum('bqhcd,bkhcd->bhcqk', q_blk, k).astype(jnp.float32) * scale
        q_pos = i * Q_BLOCK + jnp.arange(Q_BLOCK)
        mask = k_pos[None, :] <= q_pos[:, None]
        s = jnp.where(mask, s, -jnp.inf)
        p = jax.nn.softmax(s, axis=-1)
        a = p[:, :, 0] - lam * p[:, :, 1]
        return jnp.einsum('bhqk,bkhe->bqhe', a.astype(v.dtype), v)

    o = lax.map(block, (qb, jnp.arange(nb)))
    return o.transpose(1, 0, 2, 3, 4).reshape(bn, s_len, h, 2 * d)


def even_mixer(h, w_in, conv_w, conv_b, ln_g, ln_b, lq1, lk1, lq2, lk2, subln_g, w_out, lambda_init):
    bn, s_len, _ = h.shape
    z = h @ w_in
    a_val, a_gate, q, k, v = jnp.split(
        z, [A_WIDTH, 2 * A_WIDTH, 2 * A_WIDTH + B_WIDTH, 2 * A_WIDTH + 2 * B_WIDTH], axis=-1)
    a = a_val * jax.nn.sigmoid(a_gate)
    a = causal_dwconv(a, conv_w) + conv_b
    a = jax.nn.silu(layer_norm(a, ln_g, ln_b))
    q = q.reshape(bn, s_len, DIFF_HEADS, 2, DIFF_HEAD_DIM)
    k = k.reshape(bn, s_len, DIFF_HEADS, 2, DIFF_HEAD_DIM)
    v = v.reshape(bn, s_len, DIFF_HEADS, 2 * DIFF_HEAD_DIM)
    f32 = jnp.float32
    lam = (jnp.exp(jnp.sum(lq1.astype(f32) * lk1.astype(f32)))
           - jnp.exp(jnp.sum(lq2.astype(f32) * lk2.astype(f32))) + lambda_init)
    o = diff_attention(q, k, v, lam)
    o = rms_norm(o, subln_g) * (1.0 - lambda_init)
    o = o.reshape(bn, s_len, B_WIDTH)
    return jnp.concatenate([a, o], axis=-1) @ w_out


def odd_mixer(h, w_in, conv_w, w_out):
    z = h @ w_in
    b_gate, c_gate, u = jnp.split(z, 3, axis=-1)
    y = b_gate * causal_dwconv(c_gate * u, conv_w)
    return y @ w_out


def cross_attention(h, mem_n, wq, wk, wv, wo):
    bn, s_len, _ = h.shape
    m_len = mem_n.shape[1]
    q = (h @ wq).reshape(bn, s_len, X_HEADS, X_HEAD_DIM)
    k = (mem_n @ wk).reshape(bn, m_len, X_HEADS, X_HEAD_DIM)
    v = (mem_n @ wv).reshape(bn, m_len, X_HEADS, X_HEAD_DIM)
    s = jnp.einsum('bshd,bmhd->bhsm', q, k).astype(jnp.float32) * (X_HEAD_DIM ** -0.5)
    p = jax.nn.softmax(s, axis=-1)
    o = jnp.einsum('bhsm,bmhd->bshd', p.astype(v.dtype), v).reshape(bn, s_len, D_MODEL)
    return o @ wo


def sq_relu_mlp(h, w1, w2):
    return jnp.square(jax.nn.relu(h @ w1)) @ w2


def setup_inputs(seed: int = 0) -> dict:
    key = jax.random.key(seed)
    ks = iter(jax.random.split(key, 32))
    f32 = jnp.float32

    def dense(shape):
        return jax.random.normal(next(ks), shape, f32) * (shape[-2] ** -0.5)

    def gain(shape):
        return 1.0 + 0.02 * jax.random.normal(next(ks), shape, f32)

    def small(shape, scale=0.02):
        return scale * jax.random.normal(next(ks), shape, f32)

    return {
        "x": jax.random.normal(next(ks), (BATCH, SEQ, D_MODEL), f32),
        "mem": jax.random.normal(next(ks), (BATCH, MEM_LEN, D_MODEL), f32),
        "mem_norm_g": gain((D_MODEL,)),
        "final_norm_g": gain((D_MODEL,)),
        "mix_norm_g": gain((DEPTH, D_MODEL)),
        "xattn_norm_g": gain((DEPTH, D_MODEL)),
        "mlp_norm_g": gain((DEPTH, D_MODEL)),
        "even_w_in": dense((N_EVEN, D_MODEL, EVEN_IN)),
        "conv_a_w": jax.random.normal(next(ks), (N_EVEN, CONF_KERNEL, A_WIDTH), f32) * (CONF_KERNEL ** -0.5),
        "conv_a_b": small((N_EVEN, A_WIDTH)),
        "ln_a_g": gain((N_EVEN, A_WIDTH)),
        "ln_a_b": small((N_EVEN, A_WIDTH)),
        "lambda_q1": small((N_EVEN, DIFF_HEAD_DIM), 0.1),
        "lambda_k1": small((N_EVEN, DIFF_HEAD_DIM), 0.1),
        "lambda_q2": small((N_EVEN, DIFF_HEAD_DIM), 0.1),
        "lambda_k2": small((N_EVEN, DIFF_HEAD_DIM), 0.1),
        "subln_g": gain((N_EVEN, 2 * DIFF_HEAD_DIM)),
        "even_w_out": dense((N_EVEN, D_MODEL, D_MODEL)),
        "odd_w_in": dense((N_ODD, D_MODEL, ODD_IN)),
        "conv_c_w": jax.random.normal(next(ks), (N_ODD, SHORT_KERNEL, C_WIDTH), f32) * (SHORT_KERNEL ** -0.5),
        "odd_w_out": dense((N_ODD, C_WIDTH, D_MODEL)),
        "xq_w": dense((DEPTH, D_MODEL, D_MODEL)),
        "xk_w": dense((DEPTH, D_MODEL, D_MODEL)),
        "xv_w": dense((DEPTH, D_MODEL, D_MODEL)),
        "xo_w": dense((DEPTH, D_MODEL, D_MODEL)),
        "mlp_w1": dense((DEPTH, D_MODEL, D_FF)),
        "mlp_w2": dense((DEPTH, D_FF, D_MODEL)),
    }


def reference(x, mem, mem_norm_g, final_norm_g, mix_norm_g, xattn_norm_g, mlp_norm_g,
              even_w_in, conv_a_w, conv_a_b, ln_a_g, ln_a_b, lambda_q1, lambda_k1,
              lambda_q2, lambda_k2, subln_g, even_w_out, odd_w_in, conv_c_w, odd_w_out,
              xq_w, xk_w, xv_w, xo_w, mlp_w1, mlp_w2):
    mem_n = rms_norm(mem, mem_norm_g)
    for l in range(DEPTH):
        hn = rms_norm(x, mix_norm_g[l])
        if l % 2 == 0:
            e = l // 2
            lambda_init = 0.8 - 0.6 * math.exp(-0.3 * l)
            x = x + even_mixer(hn, even_w_in[e], conv_a_w[e], conv_a_b[e], ln_a_g[e], ln_a_b[e],
                               lambda_q1[e], lambda_k1[e], lambda_q2[e], lambda_k2[e],
                               subln_g[e], even_w_out[e], lambda_init)
        else:
            o = l // 2
            x = x + odd_mixer(hn, odd_w_in[o], conv_c_w[o], odd_w_out[o])
        x = x + cross_attention(rms_norm(x, xattn_norm_g[l]), mem_n,
                                xq_w[l], xk_w[l], xv_w[l], xo_w[l])
        x = x + sq_relu_mlp(rms_norm(x, mlp_norm_g[l]), mlp_w1[l], mlp_w2[l])
    return rms_norm(x, final_norm_g)
```

```python
import math
from contextlib import ExitStack

import numpy as np
import ml_dtypes

import concourse.bass as bass
import concourse.mybir as mybir
from concourse.bass_utils import run_bass_kernel_spmd

F32 = mybir.dt.float32
BF16 = mybir.dt.bfloat16
AF = mybir.ActivationFunctionType
ALU = mybir.AluOpType
EPS = 1e-6


class Cfg:
    def __init__(self, D=2048, TT=512, MEM=256, WBW=512, depth=4):
        self.D = D
        self.TT = TT
        self.MEM = MEM
        self.WBW = WBW
        self.depth = depth
        self.NS = 4
        self.G = 4
        self.NB = 2
        self.NCORES = self.G * self.NB
        self.KD = D // 128
        self.TOK = self.NS * TT
        self.SEQ = self.G * self.NS * TT
        self.AW = D // 2
        self.BW = D - self.AW
        self.KA = self.AW // 128
        self.NH = self.BW // 128
        self.XH = 4
        self.XHD = D // 4
        self.XKC = self.XHD // 128
        self.DFF = 4 * D
        self.NG = self.DFF // D
        self.SUB = TT // 128
        self.MC = MEM // 128
        self.CK = 31
        self.HA = 30
        self.SK = 3
        self.HC = 2
        self.EIN = 2 * self.AW + 3 * self.BW
        self.OIN = 3 * D
        self.NEVEN = (depth + 1) // 2
        self.NODD = depth // 2
        self.TLA = self.KA * self.NS * self.HA
        self.TLC = self.KD * self.NS * self.HC
        self.TL = max(self.TLA, self.TLC)
        c = 0
        self.c_gain = c; c += 14 * self.KD
        self.c_convaw = c; c += 2 * self.KA * self.CK
        self.c_convab = c; c += 2 * self.KA
        self.c_lnag = c; c += 2 * self.KA
        self.c_lnab = c; c += 2 * self.KA
        self.c_subg = c; c += 2
        self.c_lam = c; c += 2 * 4
        self.c_convcw = c; c += 2 * self.KD * self.SK
        self.c_sel = c; c += 4
        self.NCONST = c

    def gain_col(self, kind, l=0):
        idx = {"mem": 0, "final": 1, "mix": 2, "xattn": 6, "mlp": 10}[kind] + (l if kind in ("mix", "xattn", "mlp") else 0)
        return self.c_gain + idx * self.KD


BIG = Cfg()


class Eng:
    def __init__(self, e, sem, name):
        self.e = e
        self.sem = sem
        self.name = name
        self.n = 0
        self.seen = {}

    def wait(self, *tickets):
        for t in tickets:
            if t is None:
                continue
            if isinstance(t, (list, tuple)) and len(t) > 0 and isinstance(t[0], (list, tuple)):
                self.wait(*t)
                continue
            if isinstance(t, list):
                self.wait(*t)
                continue
            key, sem, val = t
            if self.seen.get(key, 0) >= val:
                continue
            self.e.wait_ge(sem, val)
            self.seen[key] = val

    def tick(self, ins):
        self.n += 1
        ins.then_inc(self.sem, 1)
        return (self.name, self.sem, self.n)

    def last(self):
        return (self.name, self.sem, self.n) if self.n > 0 else None


class DSem:
    def __init__(self, sem, name):
        self.sem = sem
        self.name = name
        self.n = 0

    def issue(self, ins):
        self.n += 16
        ins.then_inc(self.sem, 16)
        return (self.name, self.sem, self.n)

    def last(self):
        return (self.name, self.sem, self.n) if self.n > 0 else None


class Ring:
    def __init__(self, n):
        self.n = n
        self.i = 0
        self.rel = [[] for _ in range(n)]

    def next(self):
        k = self.i % self.n
        self.i += 1
        rel = self.rel[k]
        self.rel[k] = []
        return k, rel

    def release(self, k, *tickets):
        self.rel[k].extend([t for t in tickets if t is not None])


class WJob:
    def __init__(self, pieces, fn):
        self.pieces = pieces
        self.fn = fn


class FJob:
    def __init__(self, fn):
        self.fn = fn


class Prog:
    def __init__(self, cfg, debug_stop=None):
        self.cfg = cfg
        self.debug_stop = debug_stop
        self.items = []

    def build(self):
        cfg = self.cfg
        nc = bass.Bass("TRN2", target_bir_lowering=False)
        self.nc = nc
        D, TOK, TT, KD, MEM = cfg.D, cfg.TOK, cfg.TT, cfg.KD, cfg.MEM
        dt = nc.dram_tensor
        self.xT_in = dt("xT", [D, TOK], F32, kind="ExternalInput").ap()
        self.memT_in = dt("memT", [D, MEM], F32, kind="ExternalInput").ap()
        self.consts_in = dt("consts", [128, cfg.NCONST], F32, kind="ExternalInput").ap()
        self.cmask_in = dt("cmask", [128, 4 * cfg.SUB * TT + 128], BF16, kind="ExternalInput").ap()
        ne, no, dp = cfg.NEVEN, cfg.NODD, cfg.depth
        self.w_even_in = dt("even_w_in", [ne * D, cfg.EIN], F32, kind="ExternalInput").ap()
        self.w_even_out = dt("even_w_out", [ne * D, D], F32, kind="ExternalInput").ap()
        self.w_odd_in = dt("odd_w_in", [max(no, 1) * D, cfg.OIN], F32, kind="ExternalInput").ap()
        self.w_odd_out = dt("odd_w_out", [max(no, 1) * D, D], F32, kind="ExternalInput").ap()
        self.w_xq = dt("xq_w", [dp * D, D], F32, kind="ExternalInput").ap()
        self.w_xk = dt("xk_w", [dp * D, D], F32, kind="ExternalInput").ap()
        self.w_xv = dt("xv_w", [dp * D, D], F32, kind="ExternalInput").ap()
        self.w_xo = dt("xo_w", [dp * D, D], F32, kind="ExternalInput").ap()
        self.w_m1 = dt("mlp_w1", [dp * D, cfg.DFF], F32, kind="ExternalInput").ap()
        self.w_m2 = dt("mlp_w2", [dp * cfg.DFF, D], F32, kind="ExternalInput").ap()
        self.yT = dt("yT", [D, TOK], F32, kind="ExternalOutput").ap()
        self.xres = dt("xres", [D, TOK], F32).ap()
        self.qT = dt("qT_s", [cfg.BW, TOK], BF16).ap()
        self.cuT = dt("cuT_s", [D, TOK], F32).ap()
        self.memnT = dt("memnT_s", [D, MEM], BF16).ap()
        self.HPC = 2 if cfg.NH >= 2 else 1
        self.KR = 128 * self.HPC
        self.nKc = cfg.NH // self.HPC
        self.agK_in = [dt(f"agK_in{c}", [self.KR, TOK], BF16).ap() for c in range(self.nKc)]
        self.agK_out = [dt(f"agK_out{c}", [cfg.G * self.KR, TOK], BF16).ap() for c in range(self.nKc)]
        self.agV_in = [dt(f"agV_in{c}", [TOK, self.KR], BF16).ap() for c in range(self.nKc)]
        self.agV_out = [dt(f"agV_out{c}", [cfg.G * TOK, self.KR], BF16).ap() for c in range(self.nKc)]
        self.agT_in = dt("agT_in", [128, cfg.TL], F32).ap()
        self.agT_out = dt("agT_out", [cfg.G * 128, cfg.TL], F32).ap()

        with ExitStack() as es:
            sb = lambda name, shape, dty: es.enter_context(nc.sbuf_tensor(name, shape, dty))
            self.A = sb("bufA", [128, KD * TOK], BF16)
            self.B = sb("bufB", [128, KD * TOK], BF16)
            self.NWS = 2
            self.W = [sb(f"wslot{i}", [128, KD, cfg.WBW], BF16) for i in range(self.NWS)]
            self.consts = sb("consts_sb", [128, cfg.NCONST], F32)
            self.ones_f = sb("ones_f", [128, 128], F32)
            self.ones_b = sb("ones_b", [128, 128], BF16)
            self.ident_b = sb("ident_b", [128, 128], BF16)
            self.lamv = sb("lamv", [128, 8], F32)
            self.subg = sb("subg", [128, 2], F32)
            self.nscr = 10
            self.scrT = sb("scrT", [128, self.nscr, TT], F32)
            self.scr = [self.scrT[:, i, :] for i in range(self.nscr)]
            self.epsv = sb("epsv", [128, 1], F32)
            self.tailbuf = sb("tailbuf", [128, cfg.TLC], F32)
            r0w = max(2 * cfg.TL, cfg.CK * 128 // 2)
            n_conv = r0w + cfg.TL + 4 * (cfg.HA + TT)
            n_attn = TOK + 4 * TT
            n_x = 3 * cfg.MC * TT // 2
            self.phase_mem = sb("phase_mem", [128, max(n_conv, n_attn, n_x)], F32)
            pm = self.phase_mem
            self.tails2 = [pm[:, i * cfg.TL:(i + 1) * cfg.TL] for i in range(2)]
            self.halo = pm[:, r0w:r0w + cfg.TL]
            self.dg = pm[:, 0:cfg.CK * 128 // 2].bitcast(BF16).rearrange("p (j m) -> p j m", m=128)
            e0 = r0w + cfg.TL
            self.ext = [pm[:, e0 + i * (cfg.HA + TT):e0 + (i + 1) * (cfg.HA + TT)] for i in range(4)]
            n_am = 3 * cfg.SEQ + 4 * cfg.SUB * TT
            if n_am <= KD * TOK:
                self.attn_mem = self.A
            else:
                self.attn_mem = sb("attn_mem", [128, n_am], BF16)
            self.ps = es.enter_context(nc.psum_tensor("ps", [128, 8, 512], F32))
            sem = lambda name: es.enter_context(nc.semaphore(name))
            self.PE = Eng(nc.tensor, sem("s_pe"), "pe")
            self.ACT = Eng(nc.scalar, sem("s_act"), "act")
            self.DVE = Eng(nc.vector, sem("s_dve"), "dve")
            self.POOL = Eng(nc.gpsimd, sem("s_pool"), "pool")
            self.SP = Eng(nc.sync, sem("s_sp"), "sp")
            self.wsem = [DSem(sem(f"s_w{i}"), f"w{i}") for i in range(self.NWS)]
            self.ldsem = [DSem(sem(f"s_ld{i}"), f"ld{i}") for i in range(8)]
            self.stsem = [DSem(sem(f"s_st{i}"), f"st{i}") for i in range(8)]
            self.ccsem = sem("s_cc")
            self.ccn = 0
            self.misc_sem = DSem(sem("s_misc"), "misc")
            self.outsem = DSem(sem("s_out"), "out")
            self.all_dsems = self.ldsem + self.stsem + [self.misc_sem, self.outsem]
            self.pending_bar = []
            self.emit_program()
        return nc

    def Abf(self):
        return self.A[:].rearrange("p (k t) -> p k t", k=self.cfg.KD)

    def Bbf(self):
        return self.B[:].rearrange("p (k t) -> p k t", k=self.cfg.KD)

    def Af32(self):
        return self.A[:].bitcast(F32).rearrange("p (k t) -> p k t", k=self.cfg.KA)

    def Bf32(self):
        return self.B[:].bitcast(F32).rearrange("p (k t) -> p k t", k=self.cfg.KA)

    def cst(self, col, n=1):
        return self.consts[:, col:col + n]

    def barrier(self, engines=None):
        ts = [e.last() for e in (self.PE, self.ACT, self.DVE, self.POOL)]
        ts += [d.last() for d in self.all_dsems]
        if self.ccn > 0:
            ts.append(("cc", self.ccsem, self.ccn))
        ts = [t for t in ts if t is not None]
        for e in (engines or (self.PE, self.ACT, self.DVE, self.SP)):
            e.wait(*ts)
        self.pending_bar = ts

    def pool_sync(self):
        self.POOL.wait(*self.pending_bar)

    def dma(self, q, dsem, out, in_):
        ins = q.e.dma_start(out=out, in_=in_)
        return dsem.issue(ins)

    def emit_program(self):
        cfg = self.cfg
        items = self.items
        items.append(FJob(self.setup))
        x_src = self.xT_in
        stop = False
        for l in range(cfg.depth):
            subs = []
            if l % 2 == 0:
                self.even_layer(l, x_src)
            else:
                self.odd_layer(l, x_src)
            x_src = self.xres
            if self.debug_stop == (l, 0):
                stop = True
                break
            self.xattn_layer(l)
            if self.debug_stop == (l, 1):
                stop = True
                break
            self.mlp_layer(l)
            if self.debug_stop == (l, 2):
                stop = True
                break
        if stop:
            items.append(FJob(self.dump_x))
        else:
            items.append(FJob(lambda: self.norm_phase(x_src, cfg.gain_col("final"), final=True)))
        items.append(FJob(self.finish))
        self.execute()

    def execute(self):
        wjobs = [it for it in self.items if isinstance(it, WJob)]
        for j, w in enumerate(wjobs):
            w.idx = j
        R = self.NWS
        ready = {}
        rel = {}

        def issue(j):
            if j >= len(wjobs):
                return
            slot = j % R
            if j - R >= 0:
                self.POOL.wait(*rel[j - R])
            t = None
            for (src, off, n) in wjobs[j].pieces:
                t = self.dma(self.POOL, self.wsem[slot], self.W[slot][:, :, off:off + n], src)
            ready[j] = t

        for j in range(min(R, len(wjobs))):
            issue(j)
        for it in self.items:
            if isinstance(it, FJob):
                it.fn()
            else:
                j = it.idx
                r = it.fn(self.W[j % R], ready[j])
                rel[j] = r if isinstance(r, list) else [r]
                issue(j + R)

    def wview(self, w2d, row0, c0, n):
        D = self.cfg.D
        return w2d[row0:row0 + D, c0:c0 + n].rearrange("(k p) n -> p k n", p=128)

    def setup(self):
        cfg, nc = self.cfg, self.nc
        SP, DVE, ACT, PE = self.SP, self.DVE, self.ACT, self.PE
        self.dma(SP, self.misc_sem, self.consts[:], self.consts_in[:, :])
        t_c = self.dma(SP, self.misc_sem, self.ident_b[:], self.cmask_in[:, 4 * cfg.SUB * cfg.TT:4 * cfg.SUB * cfg.TT + 128])
        t1 = DVE.tick(nc.vector.memset(self.ones_f[:], 1.0))
        t2 = DVE.tick(nc.vector.memset(self.ones_b[:], 1.0))
        t3 = DVE.tick(nc.vector.memset(self.epsv[:], EPS))
        DVE.wait(t_c)
        for e in range(cfg.NEVEN):
            l = 2 * e
            lam_init = 0.8 - 0.6 * math.exp(-0.3 * l)
            c0 = cfg.c_lam + 4 * e
            prod = self.scr[0][0:64, 0:2]
            ta = DVE.tick(nc.vector.tensor_tensor(out=self.scr[0][0:64, 0:1], in0=self.consts[0:64, c0:c0 + 1],
                                                  in1=self.consts[0:64, c0 + 1:c0 + 2], op=ALU.mult))
            tb = DVE.tick(nc.vector.tensor_tensor(out=self.scr[0][0:64, 1:2], in0=self.consts[0:64, c0 + 2:c0 + 3],
                                                  in1=self.consts[0:64, c0 + 3:c0 + 4], op=ALU.mult))
            PE.wait(ta, tb, t1)
            tp = PE.tick(nc.tensor.matmul(self.ps[:, 0, 0:2], lhsT=self.ones_f[0:64, :], rhs=prod, start=True, stop=True))
            ACT.wait(tp)
            te = ACT.tick(nc.scalar.activation(out=self.scr[1][:, 0:2], in_=self.ps[:, 0, 0:2], func=AF.Exp))
            DVE.wait(te)
            td = DVE.tick(nc.vector.tensor_tensor(out=self.scr[1][:, 2:3], in0=self.scr[1][:, 1:2],
                                                  in1=self.scr[1][:, 0:1], op=ALU.subtract))
            DVE.wait(td)
            tl = DVE.tick(nc.vector.tensor_scalar(out=self.lamv[:, e:e + 1], in0=self.scr[1][:, 2:3],
                                                  scalar1=-lam_init, scalar2=None, op0=ALU.add))
            tg = DVE.tick(nc.vector.tensor_scalar(out=self.subg[:, e:e + 1], in0=self.cst(cfg.c_subg + e),
                                                  scalar1=1.0 - lam_init, scalar2=None, op0=ALU.mult))
            PE.wait(tl)
            ACT.wait(tl)
        self.barrier()
        self.norm_phase(self.memT_in, cfg.gain_col("mem"), mem=True)

    def norm_phase(self, src, gcol, final=False, mem=False):
        cfg, nc = self.cfg, self.nc
        SP, DVE, ACT, PE = self.SP, self.DVE, self.ACT, self.PE
        KD, TT, D = cfg.KD, cfg.TT, cfg.D
        self.barrier()
        ntok = cfg.MEM if mem else cfg.TOK
        W = min(TT, ntok)
        ntile = ntok // W
        srcv = src.rearrange("(k p) t -> p k t", p=128)
        xs_all = self.B[:].bitcast(F32).rearrange("p (b k t) -> p b k t", b=2, k=KD)
        xring = Ring(2)
        psring = Ring(2)
        sqring = Ring(2)
        oring = Ring(2)
        rring = Ring(2)
        Abf = self.Abf()
        for ti in range(ntile):
            off = ti * W
            b, relx = xring.next()
            xs = xs_all[:, b, :, 0:W]
            SP.wait(*relx)
            tld = self.dma(SP, self.ldsem[b], xs, srcv[:, :, off:off + W])
            pb, relp = psring.next()
            pst = self.ps[:, pb, 0:W]
            PE.wait(*relp)
            tpe = None
            for k in range(KD):
                sb_, rels = sqring.next()
                sq = self.scr[sb_][:, 0:W]
                ACT.wait(tld, *rels)
                ta = ACT.tick(nc.scalar.activation(out=sq, in_=xs[:, k, :], func=AF.Square))
                PE.wait(ta)
                tpe = PE.tick(nc.tensor.matmul(pst, lhsT=self.ones_f[:], rhs=sq, start=(k == 0), stop=(k == KD - 1)))
                sqring.release(sb_, tpe)
            rb, relr = rring.next()
            sd = self.scr[2 + 4 * rb][:, 0:W]
            rstd = self.scr[3 + 4 * rb][:, 0:W]
            ACT.wait(tpe, *relr)
            ts = ACT.tick(nc.scalar.activation(out=sd, in_=pst, func=AF.Sqrt, bias=self.eps_col(), scale=1.0 / D))
            psring.release(pb, ts)
            DVE.wait(ts)
            tr = DVE.tick(nc.vector.reciprocal(out=rstd, in_=sd))
            DVE.wait(tr)
            tlast = None
            for k in range(KD):
                if final or mem:
                    ob, relo = oring.next()
                    DVE.wait(*relo)
                    if final:
                        dst = self.scr[4 + ob][:, 0:W]
                    else:
                        dst = self.scr[4 + ob][:, 0:W].bitcast(BF16)[:, 0:W]
                else:
                    dst = Abf[:, k, off:off + W]
                tlast = DVE.tick(nc.vector.scalar_tensor_tensor(out=dst, in0=xs[:, k, :], scalar=self.cst(gcol + k),
                                                                in1=rstd, op0=ALU.mult, op1=ALU.mult))
                if final:
                    SP.wait(tlast)
                    tst = self.dma(SP, self.outsem, self.yT[k * 128:(k + 1) * 128, off:off + W], dst)
                    oring.release(ob, tst)
                elif mem:
                    SP.wait(tlast)
                    tst = self.dma(SP, self.stsem[ob], self.memnT[k * 128:(k + 1) * 128, off:off + W], dst)
                    oring.release(ob, tst)
            xring.release(b, tlast)
            rring.release(rb, tlast)
        self.barrier()

    def mm_group(self, pst, wslot, mloc, in_buf, t0, w, ready, extra_wait=()):
        nc, PE, KD = self.nc, self.PE, self.cfg.KD
        PE.wait(ready, *extra_wait)
        ins = None
        for k in range(KD):
            ins = nc.tensor.matmul(pst, lhsT=wslot[:, k, mloc * 128:(mloc + 1) * 128], rhs=in_buf[:, k, t0:t0 + w],
                                   start=(k == 0), stop=(k == KD - 1))
        return PE.tick(ins)

    def gemm_rmw(self, in_buf, w2d, row0, x_src):
        cfg, nc = self.cfg, self.nc
        SP, DVE, PE = self.SP, self.DVE, self.PE
        D, TT, NS, WBW = cfg.D, cfg.TT, cfg.NS, cfg.WBW
        nblk = D // WBW
        mper = WBW // 128
        tiles = [(blk, ml, tt) for blk in range(nblk) for ml in range(mper) for tt in range(NS)]
        st = {"ld": 0, "tld": {}, "inring": Ring(3), "outring": Ring(2), "psring": Ring(8), "slotmap": {}}
        PRE = 2

        def load(n):
            if n >= len(tiles) or n in st["tld"]:
                return
            blk, ml, tt = tiles[n]
            m = blk * mper + ml
            s, rel = st["inring"].next()
            st["slotmap"][n] = s
            SP.wait(*rel)
            st["tld"][n] = self.dma(SP, self.ldsem[s], self.scr[s][:, 0:TT],
                                    x_src[m * 128:(m + 1) * 128, tt * TT:(tt + 1) * TT])

        def job(blk):
            def fn(wslot, ready):
                tp = None
                for n, (b2, ml, tt) in enumerate(tiles):
                    if b2 != blk:
                        continue
                    m = blk * mper + ml
                    for q in range(n, n + PRE + 1):
                        load(q)
                    pb, relp = st["psring"].next()
                    pst = self.ps[:, pb, 0:TT]
                    tp = self.mm_group(pst, wslot, ml, in_buf, tt * TT, TT, ready, relp)
                    ob, relo = st["outring"].next()
                    xo = self.scr[3 + ob][:, 0:TT]
                    s = st["slotmap"][n]
                    DVE.wait(tp, st["tld"][n], *relo)
                    td = DVE.tick(nc.vector.tensor_tensor(out=xo, in0=pst, in1=self.scr[s][:, 0:TT], op=ALU.add))
                    st["psring"].release(pb, td)
                    st["inring"].release(s, td)
                    SP.wait(td)
                    tst = self.dma(SP, self.stsem[ob], self.xres[m * 128:(m + 1) * 128, tt * TT:(tt + 1) * TT], xo)
                    st["outring"].release(ob, tst)
                return tp
            return fn

        self.items.append(FJob(self.barrier))
        for blk in range(nblk):
            self.items.append(WJob([(self.wview(w2d, row0, blk * WBW, WBW), 0, WBW)], job(blk)))
        self.items.append(FJob(self.barrier))

    def gemm_to_sbuf(self, in_buf, w2d, row0, col0, ncols, dst_fn, mode, tok_tiles=None):
        cfg, nc = self.cfg, self.nc
        DVE, ACT = self.DVE, self.ACT
        TT, NS, WBW = cfg.TT, cfg.NS, cfg.WBW
        nblk = ncols // WBW
        mper = WBW // 128
        st = {"psring": Ring(8), "tring": Ring(2), "cnt": 0}

        def job(blk):
            def fn(wslot, ready):
                tp = None
                for ml in range(mper):
                    m = blk * mper + ml
                    for tt, (tk0, tkw) in enumerate(tok_tiles or [(t_ * TT, TT) for t_ in range(NS)]):
                        pb, relp = st["psring"].next()
                        pst = self.ps[:, pb, 0:tkw]
                        tp = self.mm_group(pst, wslot, ml, in_buf, tk0, tkw, ready, relp)
                        dst = dst_fn(m, tt)
                        if mode == "copy":
                            st["cnt"] += 1
                            if st["cnt"] % 2 == 0:
                                ACT.wait(tp)
                                tc_ = ACT.tick(nc.scalar.copy(out=dst, in_=pst))
                            else:
                                DVE.wait(tp)
                                tc_ = DVE.tick(nc.vector.tensor_copy(out=dst, in_=pst))
                            st["psring"].release(pb, tc_)
                        else:
                            tb, relt = st["tring"].next()
                            tmp = self.scr[tb][:, 0:TT]
                            ACT.wait(tp, *relt)
                            ta = ACT.tick(nc.scalar.activation(out=tmp, in_=pst, func=AF.Relu))
                            st["psring"].release(pb, ta)
                            DVE.wait(ta)
                            td = DVE.tick(nc.vector.tensor_tensor(out=dst, in0=tmp, in1=tmp, op=ALU.mult))
                            st["tring"].release(tb, td)
                return tp
            return fn

        self.items.append(FJob(self.barrier))
        for blk in range(nblk):
            self.items.append(WJob([(self.wview(w2d, row0, col0 + blk * WBW, WBW), 0, WBW)], job(blk)))
        self.items.append(FJob(self.barrier))

    def gemm_to_dram(self, in_buf, w2d, row0, col0, ncols, dram_dst):
        cfg, nc = self.cfg, self.nc
        DVE, ACT, SP = self.DVE, self.ACT, self.SP
        TT, NS, WBW = cfg.TT, cfg.NS, cfg.WBW
        nblk = ncols // WBW
        mper = WBW // 128
        st = {"psring": Ring(8), "oring": Ring(4), "cnt": 0}

        def job(blk):
            def fn(wslot, ready):
                tp = None
                for ml in range(mper):
                    m = blk * mper + ml
                    for tt in range(NS):
                        pb, relp = st["psring"].next()
                        pst = self.ps[:, pb, 0:TT]
                        tp = self.mm_group(pst, wslot, ml, in_buf, tt * TT, TT, ready, relp)
                        ob, relo = st["oring"].next()
                        dst = self.scr[ob][:, 0:TT].bitcast(BF16)[:, 0:TT]
                        st["cnt"] += 1
                        if st["cnt"] % 2 == 0:
                            ACT.wait(tp, *relo)
                            tc_ = ACT.tick(nc.scalar.copy(out=dst, in_=pst))
                        else:
                            DVE.wait(tp, *relo)
                            tc_ = DVE.tick(nc.vector.tensor_copy(out=dst, in_=pst))
                        st["psring"].release(pb, tc_)
                        SP.wait(tc_)
                        tst = self.dma(SP, self.stsem[ob], dram_dst(m, tt), dst)
                        st["oring"].release(ob, tst)
                return tp
            return fn

        self.items.append(FJob(self.barrier))
        for blk in range(nblk):
            self.items.append(WJob([(self.wview(w2d, row0, col0 + blk * WBW, WBW), 0, WBW)], job(blk)))
        self.items.append(FJob(self.barrier))

    def gemm_tokmajor(self, in_buf, ntok, w2d, row0, col0, ncols, dst_kind, dst):
        cfg, nc = self.cfg, self.nc
        DVE, ACT, SP, PE = self.DVE, self.ACT, self.SP, self.PE
        KD, WBW = cfg.KD, cfg.WBW
        nblk = ncols // WBW
        ntb = ntok // 128
        st = {"psring": Ring(8), "oring": Ring(2), "cnt": 0}

        def job(blk):
            def fn(wslot, ready):
                tp = None
                for tb in range(ntb):
                    pb, relp = st["psring"].next()
                    pst = self.ps[:, pb, 0:WBW]
                    PE.wait(ready, *relp)
                    ins = None
                    for k in range(KD):
                        ins = nc.tensor.matmul(pst, lhsT=in_buf[:, k, tb * 128:(tb + 1) * 128], rhs=wslot[:, k, 0:WBW],
                                               start=(k == 0), stop=(k == KD - 1))
                    tp = PE.tick(ins)
                    st["cnt"] += 1
                    eng, fnc = (ACT, nc.scalar.copy) if st["cnt"] % 2 == 0 else (DVE, nc.vector.tensor_copy)
                    if dst_kind == "dram":
                        ob, relo = st["oring"].next()
                        o = self.scr[ob][:, 0:WBW // 2].bitcast(BF16)[:, 0:WBW] if WBW // 2 <= cfg.TT else None
                        eng.wait(tp, *relo)
                        tc_ = eng.tick(fnc(out=o, in_=pst))
                        st["psring"].release(pb, tc_)
                        SP.wait(tc_)
                        for (dap, c0, c1) in dst(tb, blk):
                            tst = self.dma(SP, self.stsem[ob], dap, o[:, c0:c1])
                        st["oring"].release(ob, tst)
                    else:
                        eng.wait(tp)
                        tc_ = eng.tick(fnc(out=dst(tb, blk), in_=pst))
                        st["psring"].release(pb, tc_)
                return tp
            return fn

        self.items.append(FJob(self.barrier))
        for blk in range(nblk):
            self.items.append(WJob([(self.wview(w2d, row0, col0 + blk * WBW, WBW), 0, WBW)], job(blk)))
        self.items.append(FJob(self.barrier))

    def gemm_gated(self, in_buf, w2d, row0, colA, colB, ncols, kind, dst):
        cfg, nc = self.cfg, self.nc
        DVE, ACT, SP = self.DVE, self.ACT, self.SP
        TT, NS, WBW = cfg.TT, cfg.NS, cfg.WBW
        half = WBW // 2
        nblk = ncols // half
        mper = half // 128
        st = {"psring": Ring(4), "tring": Ring(2), "oring": Ring(2)}
        tb_view = self.tailbuf[:, 0:cfg.TLC].rearrange("p (k i t) -> p k i t", k=cfg.KD, i=NS)

        def job(blk):
            def fn(wslot, ready):
                tp = None
                for ml in range(mper):
                    m = blk * mper + ml
                    for tt in range(NS):
                        pb, relp = st["psring"].next()
                        pA = self.ps[:, 2 * pb, 0:TT]
                        pB = self.ps[:, 2 * pb + 1, 0:TT]
                        tpa = self.mm_group(pA, wslot, ml, in_buf, tt * TT, TT, ready, relp)
                        tp = self.mm_group(pB, wslot, mper + ml, in_buf, tt * TT, TT, ready)
                        tb, relt = st["tring"].next()
                        tmp = self.scr[tb][:, 0:TT]
                        ACT.wait(tp, *relt)
                        if kind == "glu":
                            ta = ACT.tick(nc.scalar.activation(out=tmp, in_=pB, func=AF.Sigmoid))
                        else:
                            ta = ACT.tick(nc.scalar.copy(out=tmp, in_=pB))
                        DVE.wait(ta, tpa)
                        if kind == "glu":
                            td = DVE.tick(nc.vector.tensor_tensor(out=dst(m, tt), in0=pA, in1=tmp, op=ALU.mult))
                            st["tring"].release(tb, td)
                            st["psring"].release(pb, td)
                        else:
                            ob, relo = st["oring"].next()
                            o = self.scr[2 + ob][:, 0:TT]
                            DVE.wait(*relo)
                            td = DVE.tick(nc.vector.tensor_tensor(out=o, in0=pA, in1=tmp, op=ALU.mult))
                            st["tring"].release(tb, td)
                            st["psring"].release(pb, td)
                            DVE.wait(td)
                            tt2 = DVE.tick(nc.vector.tensor_copy(out=tb_view[:, m, tt, :], in_=o[:, TT - cfg.HC:TT]))
                            SP.wait(td)
                            tst = self.dma(SP, self.stsem[ob], self.cuT[m * 128:(m + 1) * 128, tt * TT:(tt + 1) * TT], o)
                            st["oring"].release(ob, tst, tt2)
                return tp
            return fn

        self.items.append(FJob(self.barrier))
        for blk in range(nblk):
            pieces = [(self.wview(w2d, row0, colA + blk * half, half), 0, half),
                      (self.wview(w2d, row0, colB + blk * half, half), half, half)]
            self.items.append(WJob(pieces, job(blk)))
        self.items.append(FJob(self.barrier))

    def allgather(self, src, dst, wait_tickets):
        nc, POOL, cfg = self.nc, self.POOL, self.cfg
        POOL.wait(*wait_tickets)
        groups = [[b * cfg.G + r for r in range(cfg.G)] for b in range(cfg.NB)]
        ins = nc.gpsimd.collective_compute("AllGather", ALU.bypass, replica_groups=groups,
                                           ins=[src], outs=[dst])
        ins.then_inc(self.ccsem, 1)
        self.ccn += 1
        return ("cc", self.ccsem, self.ccn)

    def even_layer(self, l, x_src):
        cfg = self.cfg
        e = l // 2
        D = cfg.D
        items = self.items
        items.append(FJob(lambda: self.norm_phase(x_src, cfg.gain_col("mix", l))))
        Abf, Bbf, Bf32 = self.Abf(), self.Bbf(), self.Bf32()
        TT = cfg.TT
        w_in = self.w_even_in
        r0 = e * D
        self.gemm_gated(Abf, w_in, r0, 0, cfg.AW, cfg.AW, "glu", lambda m, tt: Bf32[:, m, tt * TT:(tt + 1) * TT])
        items.append(FJob(self.send_tails_even))
        KR, WBW = self.KR, cfg.WBW
        self.gemm_to_dram(Abf, w_in, r0, 2 * cfg.AW, cfg.BW,
                          lambda m, tt: self.qT[m * 128:(m + 1) * 128, tt * TT:(tt + 1) * TT])
        self.gemm_to_dram(Abf, w_in, r0, 2 * cfg.AW + cfg.BW, cfg.BW,
                          lambda m, tt: self.agK_in[(m * 128) // KR][(m * 128) % KR:(m * 128) % KR + 128,
                                                                      tt * TT:(tt + 1) * TT])
        def vdst(tb, blk):
            out = []
            for c0 in range(0, WBW, KR):
                col = blk * WBW + c0
                out.append((self.agV_in[col // KR][tb * 128:(tb + 1) * 128, 0:KR], c0, c0 + KR))
            return out
        self.gemm_tokmajor(Abf, cfg.TOK, w_in, r0, 2 * cfg.AW + 2 * cfg.BW, cfg.BW, "dram", vdst)
        items.append(FJob(lambda: self.even_exchange_conv(e)))
        items.append(FJob(lambda: self.even_ln_silu(e)))
        items.append(FJob(lambda: self.diff_attention(e)))
        self.gemm_rmw(Bbf, self.w_even_out, r0, x_src)

    def conv_taps(self, eng, ne, chains, wcol, bcol, ntap, base):
        TT = self.cfg.TT
        for j in range(ntap):
            for c in chains:
                eng.wait(c["t"])
                src = c["ext"][:, base + j:base + j + TT]
                if j == 0:
                    if bcol is not None:
                        ins = ne.tensor_scalar(out=c["dst"], in0=src, scalar1=self.cst(wcol(c["k"], 0)),
                                               scalar2=self.cst(bcol(c["k"])), op0=ALU.mult, op1=ALU.add)
                    else:
                        ins = ne.tensor_scalar(out=c["dst"], in0=src, scalar1=self.cst(wcol(c["k"], 0)),
                                               scalar2=None, op0=ALU.mult)
                else:
                    ins = ne.scalar_tensor_tensor(out=c["dst"], in0=src, scalar=self.cst(wcol(c["k"], j)),
                                                  in1=c["dst"], op0=ALU.mult, op1=ALU.add)
                c["t"] = eng.tick(ins)

    def exchange_tails(self, tl, per, cT):
        cfg, nc = self.cfg, self.nc
        SP, DVE = self.SP, self.DVE
        NS = cfg.NS
        h = self.halo[:, 0:tl]
        nk = tl // (NS * per)
        hv = h.rearrange("p (k i t) -> p k i t", k=nk, i=NS)
        sel = lambda i: self.cst(cfg.c_sel + i)
        SP.wait(cT)
        t = None
        for r in range(cfg.G):
            tb = self.tails2[r % 2][:, 0:tl]
            SP.wait(t)
            tld = self.dma(SP, self.ldsem[r % 2], tb, self.agT_out[r * 128:(r + 1) * 128, 0:tl])
            DVE.wait(tld, t)
            if r == 0:
                t = DVE.tick(nc.vector.tensor_scalar(out=h, in0=tb, scalar1=sel(0), scalar2=None, op0=ALU.mult))
            elif r < cfg.G - 1:
                t = DVE.tick(nc.vector.scalar_tensor_tensor(out=h, in0=tb, scalar=sel(r), in1=h,
                                                            op0=ALU.mult, op1=ALU.add))
            else:
                tv = tb.rearrange("p (k i t) -> p k i t", k=nk, i=NS)
                t = DVE.tick(nc.vector.scalar_tensor_tensor(out=hv[:, :, 1:NS, :], in0=tv[:, :, 0:NS - 1, :],
                                                            scalar=sel(3), in1=hv[:, :, 1:NS, :],
                                                            op0=ALU.mult, op1=ALU.add))
        return t, hv

    def send_tails_even(self):
        cfg = self.cfg
        TT, NS, KA, HA = cfg.TT, cfg.NS, cfg.KA, cfg.HA
        Bf32 = self.Bf32()
        self.barrier(engines=(self.SP,))
        tin = self.agT_in[:, 0:cfg.TLA].rearrange("p (k i t) -> p k i t", k=KA, i=NS)
        tt_ = None
        for k in range(KA):
            src = Bf32[:, k, :].rearrange("p (i t) -> p i t", t=TT)[:, :, TT - HA:TT]
            tt_ = self.dma(self.SP, self.stsem[0], tin[:, k, :, :], src)
        self.cc_tails = self.allgather(self.agT_in[:, :], self.agT_out[:, :], [tt_])

    def send_tails_odd(self):
        cfg = self.cfg
        self.barrier(engines=(self.SP,))
        tt_ = self.dma(self.SP, self.stsem[0], self.agT_in[:, 0:cfg.TLC], self.tailbuf[:, 0:cfg.TLC])
        self.cc_tails = self.allgather(self.agT_in[:, :], self.agT_out[:, :], [tt_])

    def even_exchange_conv(self, e):
        cfg, nc = self.cfg, self.nc
        SP, DVE, ACT, POOL = self.SP, self.DVE, self.ACT, self.POOL
        TT, NS, KA, HA, CK = cfg.TT, cfg.NS, cfg.KA, cfg.HA, cfg.CK
        Bf32, Af32 = self.Bf32(), self.Af32()
        self.barrier()
        self.pool_sync()
        th, hv = self.exchange_tails(cfg.TLA, HA, self.cc_tails)
        self.cc_pair = []
        for c in range(self.nKc):
            self.allgather(self.agK_in[c][:, :], self.agK_out[c][:, :], [])
            self.allgather(self.agV_in[c][:, :], self.agV_out[c][:, :], [])
            self.cc_pair.append(("cc", self.ccsem, self.ccn))
        PE = self.PE
        wcol = lambda k, j: cfg.c_convaw + (e * KA + k) * CK + j
        bcol = lambda k: cfg.c_convab + e * KA + k
        extb = [x_.bitcast(BF16)[:, 0:HA + TT] for x_ in self.ext]
        extring = Ring(4)
        psring = Ring(4)
        dg_rel = []
        for k in range(KA):
            DVE.wait(th, *dg_rel)
            td = None
            for j in range(CK):
                td = DVE.tick(nc.vector.tensor_scalar(out=self.dg[:, j, :], in0=self.ident_b[:],
                                                      scalar1=self.cst(wcol(k, j)), scalar2=None, op0=ALU.mult))
            tpl = None
            for i in range(NS):
                xb, relx = extring.next()
                ACT.wait(th, *relx)
                ACT.tick(nc.scalar.copy(out=extb[xb][:, 0:HA], in_=hv[:, k, i, :]))
                t2 = ACT.tick(nc.scalar.copy(out=extb[xb][:, HA:HA + TT], in_=Bf32[:, k, i * TT:(i + 1) * TT]))
                pb, relp = psring.next()
                PE.wait(td, t2, *relp)
                ins = None
                for j in range(CK):
                    ins = nc.tensor.matmul(self.ps[:, pb, 0:TT], lhsT=self.dg[:, j, :], rhs=extb[xb][:, j:j + TT],
                                           start=(j == 0), stop=(j == CK - 1))
                tpl = PE.tick(ins)
                extring.release(xb, tpl)
                ACT.wait(tpl)
                ta = ACT.tick(nc.scalar.activation(out=Af32[:, k, i * TT:(i + 1) * TT], in_=self.ps[:, pb, 0:TT],
                                                   func=AF.Identity, bias=self.cst(bcol(k)), scale=1.0))
                psring.release(pb, ta)
            dg_rel = [tpl]
        self.barrier()

    def even_ln_silu(self, e):
        cfg, nc = self.cfg, self.nc
        DVE, ACT, PE = self.DVE, self.ACT, self.PE
        TT, NS, KA, AW = cfg.TT, cfg.NS, cfg.KA, cfg.AW
        Af32, Bbf = self.Af32(), self.Bbf()
        self.barrier()
        psring = Ring(2)
        sqring = Ring(2)
        tring = Ring(2)
        prev = None
        for i in range(NS):
            sl = slice(i * TT, (i + 1) * TT)
            pb, relp = psring.next()
            S1 = self.ps[:, 2 * pb, 0:TT]
            S2 = self.ps[:, 2 * pb + 1, 0:TT]
            PE.wait(*relp)
            tp = None
            for k in range(KA):
                nc.tensor.matmul(S1, lhsT=self.ones_f[:], rhs=Af32[:, k, sl], start=(k == 0), stop=(k == KA - 1))
                sb_, rels = sqring.next()
                sq = self.scr[sb_][:, 0:TT]
                ACT.wait(*rels)
                ta = ACT.tick(nc.scalar.activation(out=sq, in_=Af32[:, k, sl], func=AF.Square))
                PE.wait(ta)
                tp = PE.tick(nc.tensor.matmul(S2, lhsT=self.ones_f[:], rhs=sq, start=(k == 0), stop=(k == KA - 1)))
                sqring.release(sb_, tp)
            mean, msq, sd, rstd = (self.scr[j][:, 0:TT] for j in (4, 5, 6, 7))
            DVE.wait(tp, prev)
            t = DVE.tick(nc.vector.tensor_scalar(out=mean, in0=S1, scalar1=1.0 / AW, scalar2=None, op0=ALU.mult))
            DVE.wait(t)
            t = DVE.tick(nc.vector.tensor_tensor(out=msq, in0=mean, in1=mean, op=ALU.mult))
            DVE.wait(t)
            t = DVE.tick(nc.vector.scalar_tensor_tensor(out=msq, in0=S2, scalar=1.0 / AW, in1=msq,
                                                        op0=ALU.mult, op1=ALU.subtract))
            psring.release(pb, t)
            ACT.wait(t, prev)
            ts = ACT.tick(nc.scalar.activation(out=sd, in_=msq, func=AF.Sqrt, bias=self.eps_col(), scale=1.0))
            DVE.wait(ts)
            tr = DVE.tick(nc.vector.reciprocal(out=rstd, in_=sd))
            for k in range(KA):
                tb, relt = tring.next()
                tmp = self.scr[8 + tb][:, 0:TT]
                DVE.wait(tr, *relt)
                t = DVE.tick(nc.vector.tensor_tensor(out=tmp, in0=Af32[:, k, sl], in1=mean, op=ALU.subtract))
                DVE.wait(t)
                t = DVE.tick(nc.vector.tensor_tensor(out=tmp, in0=tmp, in1=rstd, op=ALU.mult))
                ACT.wait(t)
                ta = ACT.tick(nc.scalar.activation(out=Bbf[:, k, sl], in_=tmp, func=AF.Silu,
                                                   bias=self.cst(cfg.c_lnab + e * KA + k),
                                                   scale=self.cst(cfg.c_lnag + e * KA + k)))
                tring.release(tb, ta)
                prev = ta
        self.barrier()

    def eps_col(self):
        return self.epsv[:, 0:1]

    def diff_attention(self, e):
        cfg, nc = self.cfg, self.nc
        SP, DVE, ACT, PE, POOL = self.SP, self.DVE, self.ACT, self.PE, self.POOL
        TT, NS, G, SUB, NH, KA, TOK, SEQ = cfg.TT, cfg.NS, cfg.G, cfg.SUB, cfg.NH, cfg.KA, cfg.TOK, cfg.SEQ
        Bbf = self.Bbf()
        self.barrier()
        self.pool_sync()
        am = self.attn_mem
        Kb = [am[:, kb * SEQ:(kb + 1) * SEQ] for kb in range(2)]
        Vb = am[:, 2 * SEQ:3 * SEQ].rearrange("p (c e) -> p c e", e=128)
        mask = am[:, 3 * SEQ:3 * SEQ + 4 * SUB * TT].rearrange("p (m t) -> p m t", t=TT)
        pm = self.phase_mem
        qh = [pm[:, j * (TOK // 2):(j + 1) * (TOK // 2)].bitcast(BF16) for j in range(2)]
        e0 = TOK
        Er = [pm[:, e0 + j * TT:e0 + (j + 1) * TT].bitcast(BF16).rearrange("p (c t) -> p c t", c=2) for j in range(4)]
        tmask = self.dma(SP, self.ldsem[7], mask, self.cmask_in[:, 0:4 * SUB * TT].rearrange("p (m t) -> p m t", t=TT))
        scale = 64 ** -0.5
        nkv_tiles = G * NS

        def load_k(h):
            SP.wait(self.cc_pair[(h * 128) // self.KR])
            kb = h % 2
            kv = Kb[kb].rearrange("p (i r t) -> p i r t", i=NS, r=G)
            t = None
            for r in range(G):
                kc_, ko_ = (h * 128) // self.KR, (h * 128) % self.KR
                src = self.agK_out[kc_][r * self.KR + ko_:r * self.KR + ko_ + 128, :].rearrange("p (i t) -> p i t", t=TT)
                t = self.dma(SP, self.ldsem[kb], kv[:, :, r, :], src)
            return t

        def load_q(h):
            return self.dma(SP, self.ldsem[2 + h % 2], qh[h % 2], self.qT[h * 128:(h + 1) * 128, :])

        def load_v(h):
            vv = Vb.rearrange("p (i r s) e -> p i r s e", i=NS, r=G)
            t = None
            for r in range(G):
                for i2 in range(NS):
                    vo_ = (h * 128) % self.KR
                    src = self.agV_out[(h * 128) // self.KR][r * TOK + i2 * TT:r * TOK + (i2 + 1) * TT, vo_:vo_ + 128]
                    src = src.rearrange("(s p) e -> p s e", p=128)
                    t = self.dma(SP, self.ldsem[4], vv[:, i2, r, :, :], src)
            return t

        tk = {0: load_k(0)}
        tq = {0: load_q(0)}
        tv = {0: load_v(0)}
        if NH > 1:
            tk[1] = load_k(1)
            tq[1] = load_q(1)
        spair = Ring(2)
        ering = Ring(len(Er))
        seq = [(h, i, kc, G * (i + 1) * SUB) for h in range(NH) for i in range(NS) for kc in range(G * (i + 1) * SUB)]
        stA = {}
        state = {"oz_rel": [], "eb": 0, "tsum": None, "tz0": None}
        esring = Ring(2)
        LOOK = 2

        def stage_a(n):
            h, i, kc, nk = seq[n]
            Kh, Q = Kb[h % 2], qh[h % 2]
            t = kc // SUB
            s = kc % SUB
            sp, rels = spair.next()
            PE.wait(tk[h], tq[h], *rels)
            tps = None
            for c in range(2):
                tps = nc.tensor.matmul(self.ps[:, 2 * sp + c, 0:TT],
                                       lhsT=Kh[c * 64:(c + 1) * 64, t * TT + s * 128:t * TT + (s + 1) * 128],
                                       rhs=Q[c * 64:(c + 1) * 64, i * TT:(i + 1) * TT], start=True, stop=True)
            tps = PE.tick(tps)
            er, rele = ering.next()
            E = Er[er]
            ACT.wait(tps, *rele)
            ta = ACT.tick(nc.scalar.activation(out=E, in_=self.ps[:, 2 * sp:2 * sp + 2, 0:TT], func=AF.Exp, scale=scale))
            spair.release(sp, ta)
            te = ta
            if t >= G * i:
                r = t - G * i
                DVE.wait(ta, tmask)
                for c in range(2):
                    te = DVE.tick(nc.vector.tensor_tensor(out=E[:, c, :], in0=E[:, c, :],
                                                          in1=mask[:, r * SUB + s, :], op=ALU.mult))
            if kc == 0:
                state["eb"], rel_es = esring.next()
                state["rel_es"] = rel_es
                state["tsum"] = [None, None]
            par = kc % 2
            eng, ne = (DVE, nc.vector)
            Es = self.scr[4 + 2 * state["eb"] + par][:, 0:TT]
            if kc < 2:
                eng.wait(te, *state["rel_es"])
                tsum = eng.tick(ne.tensor_copy(out=Es, in_=E[:, 1, :]))
            else:
                eng.wait(te, state["tsum"][par])
                tsum = eng.tick(ne.tensor_tensor(out=Es, in0=Es, in1=E[:, 1, :], op=ALU.add))
            state["tsum"][par] = tsum
            if kc == nk - 1:
                state["tsum_final"] = list(state["tsum"])
            tz0 = None
            if h >= 1 and kc % 2 == 1:
                Es0 = self.scr[8 + state["eb"]][:, 0:TT]
                if kc == 1:
                    POOL.wait(te, *state["rel_es"])
                    tz0 = POOL.tick(nc.gpsimd.tensor_copy(out=Es0, in_=E[:, 0, :]))
                else:
                    POOL.wait(te, state["tz0"])
                    tz0 = POOL.tick(nc.gpsimd.tensor_tensor(out=Es0, in0=Es0, in1=E[:, 0, :], op=ALU.add))
                state["tz0"] = tz0
            stA[n] = (er, te, tsum, state["eb"], state.get("tsum_final") if kc == nk - 1 else None, tz0)

        def stage_b(n):
            h, i, kc, nk = seq[n]
            er, te, tsum, eb, tfin, tz0 = stA.pop(n)
            E = Er[er]
            PE.wait(te, tv[h])
            if kc == 0:
                PE.wait(*state["oz_rel"])
                state["oz_rel"] = []
            ins = None
            for c in range(2):
                ins = nc.tensor.matmul(self.ps[:, 4 + c, 0:TT], lhsT=Vb[:, kc, :], rhs=E[:, c, :],
                                       start=(kc == 0), stop=(kc == nk - 1))
            split0 = h >= 1
            if not split0:
                ins = nc.tensor.matmul(self.ps[:, 6, 0:TT], lhsT=self.ones_b[:], rhs=E[:, 0, :],
                                       start=(kc == 0), stop=(kc == nk - 1))
            elif kc % 2 == 0:
                ins = nc.tensor.matmul(self.ps[:, 6, 0:TT], lhsT=self.ones_b[:], rhs=E[:, 0, :],
                                       start=(kc == 0), stop=False)
            tp_last = PE.tick(ins)
            ering.release(er, tp_last, tsum, tz0)
            if kc == nk - 1 and split0:
                PE.wait(tz0)
                ins = nc.tensor.matmul(self.ps[:, 6, 0:TT], lhsT=self.ones_f[:], rhs=self.scr[8 + eb][:, 0:TT],
                                       start=False, stop=True)
                tp_last = PE.tick(ins)
            if kc == nk - 1:
                PE.wait(*tfin)
                nc.tensor.matmul(self.ps[:, 7, 0:TT], lhsT=self.ones_f[:], rhs=self.scr[4 + 2 * eb][:, 0:TT],
                                 start=True, stop=False)
                ins = nc.tensor.matmul(self.ps[:, 7, 0:TT], lhsT=self.ones_f[:], rhs=self.scr[5 + 2 * eb][:, 0:TT],
                                       start=False, stop=True)
                tp_last = PE.tick(ins)
                esring.release(eb, tp_last)
                rz0, t0, rz1, t1 = (self.scr[j][:, 0:TT] for j in (0, 1, 2, 3))
                DVE.wait(tp_last)
                a = DVE.tick(nc.vector.reciprocal(out=rz0, in_=self.ps[:, 6, 0:TT]))
                b_ = DVE.tick(nc.vector.reciprocal(out=rz1, in_=self.ps[:, 7, 0:TT]))
                DVE.wait(a)
                c_ = DVE.tick(nc.vector.tensor_tensor(out=t0, in0=self.ps[:, 4, 0:TT], in1=rz0, op=ALU.mult))
                DVE.wait(b_)
                d = DVE.tick(nc.vector.tensor_tensor(out=t1, in0=self.ps[:, 5, 0:TT], in1=rz1, op=ALU.mult))
                state["oz_rel"] = [d]
                DVE.wait(c_, d)
                DVE.tick(nc.vector.scalar_tensor_tensor(out=Bbf[:, KA + h, i * TT:(i + 1) * TT], in0=t1,
                                                        scalar=self.lamv[:, e:e + 1], in1=t0,
                                                        op0=ALU.mult, op1=ALU.add))
                if i == NS - 1:
                    SP.wait(tp_last)
                    if h + 1 < NH:
                        tv[h + 1] = load_v(h + 1)
                    if h + 2 < NH:
                        tk[h + 2] = load_k(h + 2)
                        tq[h + 2] = load_q(h + 2)

        N = len(seq)
        for n in range(N + LOOK):
            if n < N:
                stage_a(n)
            if n - LOOK >= 0:
                stage_b(n - LOOK)
        self.barrier()
        psring = Ring(2)
        sqring = Ring(2)
        for h in range(NH):
            for i in range(NS):
                o = Bbf[:, KA + h, i * TT:(i + 1) * TT]
                sb_, rels = sqring.next()
                sq = self.scr[sb_][:, 0:TT]
                ACT.wait(*rels)
                ta = ACT.tick(nc.scalar.activation(out=sq, in_=o, func=AF.Square))
                pb, relp = psring.next()
                PE.wait(ta, *relp)
                tp = PE.tick(nc.tensor.matmul(self.ps[:, pb, 0:TT], lhsT=self.ones_f[:], rhs=sq, start=True, stop=True))
                sd = self.scr[2 + sb_][:, 0:TT]
                rstd = self.scr[4 + sb_][:, 0:TT]
                ACT.wait(tp)
                ts = ACT.tick(nc.scalar.activation(out=sd, in_=self.ps[:, pb, 0:TT], func=AF.Sqrt,
                                                   bias=self.eps_col(), scale=1.0 / 128))
                psring.release(pb, ts)
                DVE.wait(ts)
                tr = DVE.tick(nc.vector.reciprocal(out=rstd, in_=sd))
                DVE.wait(tr)
                td = DVE.tick(nc.vector.scalar_tensor_tensor(out=o, in0=o, scalar=self.subg[:, e:e + 1], in1=rstd,
                                                             op0=ALU.mult, op1=ALU.mult))
                sqring.release(sb_, td)
        self.barrier()

    def odd_layer(self, l, x_src):
        cfg = self.cfg
        o = l // 2
        D, TT = cfg.D, cfg.TT
        items = self.items
        items.append(FJob(lambda: self.norm_phase(x_src, cfg.gain_col("mix", l))))
        Abf, Bbf = self.Abf(), self.Bbf()
        r0 = o * D
        self.gemm_gated(Abf, self.w_odd_in, r0, D, 2 * D, D, "mul", None)
        items.append(FJob(self.send_tails_odd))
        self.gemm_to_sbuf(Abf, self.w_odd_in, r0, 0, D, lambda m, tt: Bbf[:, m, tt * TT:(tt + 1) * TT], "copy")
        items.append(FJob(lambda: self.odd_exchange_conv(o)))
        self.gemm_rmw(Abf, self.w_odd_out, r0, x_src)

    def odd_exchange_conv(self, o):
        cfg, nc = self.cfg, self.nc
        SP, DVE, ACT, POOL = self.SP, self.DVE, self.ACT, self.POOL
        TT, NS, KD, HC, HA, SK = cfg.TT, cfg.NS, cfg.KD, cfg.HC, cfg.HA, cfg.SK
        Abf, Bbf = self.Abf(), self.Bbf()
        self.barrier()
        self.pool_sync()
        th, hv = self.exchange_tails(cfg.TLC, HC, self.cc_tails)
        PE = self.PE
        wcol = lambda k, j: cfg.c_convcw + (o * KD + k) * SK + j
        base = HA - HC
        extb = [x_.bitcast(BF16)[:, 0:HA + TT] for x_ in self.ext]
        extring = Ring(4)
        ldring = Ring(4)
        psring = Ring(4)
        dg_rel = [[], []]
        tiles = [(k, i) for k in range(KD) for i in range(NS)]
        tld = {}

        def load(n):
            if n >= len(tiles) or n in tld:
                return
            k, i = tiles[n]
            s, rel = ldring.next()
            SP.wait(*rel)
            tld[n] = (s, self.dma(SP, self.ldsem[s], self.scr[s][:, 0:TT],
                                  self.cuT[k * 128:(k + 1) * 128, i * TT:(i + 1) * TT]))

        for n, (k, i) in enumerate(tiles):
            for q in range(n, n + 3):
                load(q)
            db = k % 2
            if i == 0:
                DVE.wait(th, *dg_rel[db])
                td = None
                for j in range(SK):
                    td = DVE.tick(nc.vector.tensor_scalar(out=self.dg[:, db * SK + j, :], in0=self.ident_b[:],
                                                          scalar1=self.cst(wcol(k, j)), scalar2=None, op0=ALU.mult))
                tdg = td
            s, tl = tld[n]
            xb, relx = extring.next()
            ACT.wait(th, tl, *relx)
            ACT.tick(nc.scalar.copy(out=extb[xb][:, base:HA], in_=hv[:, k, i, :]))
            t2 = ACT.tick(nc.scalar.copy(out=extb[xb][:, HA:HA + TT], in_=self.scr[s][:, 0:TT]))
            ldring.release(s, t2)
            pb, relp = psring.next()
            PE.wait(tdg, t2, *relp)
            ins = None
            for j in range(SK):
                ins = nc.tensor.matmul(self.ps[:, pb, 0:TT], lhsT=self.dg[:, db * SK + j, :],
                                       rhs=extb[xb][:, base + j:base + j + TT], start=(j == 0), stop=(j == SK - 1))
            tpl = PE.tick(ins)
            extring.release(xb, tpl)
            if i == NS - 1:
                dg_rel[db] = [tpl]
            DVE.wait(tpl)
            tg = DVE.tick(nc.vector.tensor_tensor(out=Abf[:, k, i * TT:(i + 1) * TT], in0=self.ps[:, pb, 0:TT],
                                                  in1=Bbf[:, k, i * TT:(i + 1) * TT], op=ALU.mult))
            psring.release(pb, tg)
        self.barrier()

    def xattn_layer(self, l):
        cfg = self.cfg
        D, TT, KD, MEM, MC, WBW = cfg.D, cfg.TT, cfg.KD, cfg.MEM, cfg.MC, cfg.WBW
        items = self.items
        Abf, Bbf = self.Abf(), self.Bbf()
        items.append(FJob(lambda: self.norm_phase(self.xres, cfg.gain_col("xattn", l))))
        self.gemm_to_sbuf(Abf, self.w_xq, l * D, 0, D, lambda m, tt: Bbf[:, m, tt * TT:(tt + 1) * TT], "copy")
        n1 = KD * MEM
        memn = self.A[:, 0:n1].rearrange("p (k t) -> p k t", k=KD)
        kmem = self.A[:, n1:2 * n1].rearrange("p (k t) -> p k t", k=KD)
        vmem = self.A[:, 2 * n1:3 * n1].rearrange("p (c d) -> p c d", c=MC)

        def load_memn():
            self.barrier()
            self.dma(self.SP, self.ldsem[0], memn, self.memnT.rearrange("(k p) t -> p k t", p=128))
            self.barrier()
        items.append(FJob(load_memn))
        self.gemm_to_sbuf(memn, self.w_xk, l * D, 0, D, lambda m, tt: kmem[:, m, :], "copy", tok_tiles=[(0, MEM)])
        self.gemm_tokmajor(memn, MEM, self.w_xv, l * D, 0, D, "sbuf",
                           lambda tb, blk: vmem[:, tb, blk * WBW:(blk + 1) * WBW])
        items.append(FJob(lambda: self.xattn_core(kmem, vmem)))
        self.gemm_rmw(Bbf, self.w_xo, l * D, self.xres)

    def xattn_core(self, kmem, vmem):
        cfg, nc = self.cfg, self.nc
        DVE, ACT, PE = self.DVE, self.ACT, self.PE
        TT, NS, XH, XKC, MC = cfg.TT, cfg.NS, cfg.XH, cfg.XKC, cfg.MC
        Bbf = self.Bbf()
        self.barrier()
        pm = self.phase_mem
        Er = [pm[:, j * (MC * TT // 2):(j + 1) * (MC * TT // 2)].bitcast(BF16).rearrange("p (c t) -> p c t", c=MC)
              for j in range(3)]
        scale = cfg.XHD ** -0.5
        sring = Ring(2)
        ering = Ring(len(Er))
        oring = Ring(2)
        seq = [(h, tt) for h in range(XH) for tt in range(NS)]
        stA = {}
        state = {"zrel": []}

        def stage_a(n):
            h, tt = seq[n]
            sl = slice(tt * TT, (tt + 1) * TT)
            sp, rels = sring.next()
            PE.wait(*rels)
            tps = None
            for mc in range(MC):
                for dc in range(XKC):
                    tps = nc.tensor.matmul(self.ps[:, sp * MC + mc, 0:TT],
                                           lhsT=kmem[:, h * XKC + dc, mc * 128:(mc + 1) * 128],
                                           rhs=Bbf[:, h * XKC + dc, sl], start=(dc == 0), stop=(dc == XKC - 1))
            tps = PE.tick(tps)
            er, rele = ering.next()
            E = Er[er]
            ACT.wait(tps, *rele)
            ta = ACT.tick(nc.scalar.activation(out=E, in_=self.ps[:, sp * MC:(sp + 1) * MC, 0:TT], func=AF.Exp,
                                               scale=scale))
            sring.release(sp, ta)
            stA[n] = (er, ta)

        def stage_b(n):
            h, tt = seq[n]
            sl = slice(tt * TT, (tt + 1) * TT)
            er, ta = stA.pop(n)
            E = Er[er]
            PE.wait(ta, *state["zrel"])
            tz = None
            for mc in range(MC):
                tz = nc.tensor.matmul(self.ps[:, 4, 0:TT], lhsT=self.ones_b[:], rhs=E[:, mc, :],
                                      start=(mc == 0), stop=(mc == MC - 1))
            tz = PE.tick(tz)
            rz = self.scr[er][:, 0:TT]
            DVE.wait(tz)
            trz = DVE.tick(nc.vector.reciprocal(out=rz, in_=self.ps[:, 4, 0:TT]))
            state["zrel"] = [trz]
            tpo = td = None
            for ec in range(XKC):
                ob, relo = oring.next()
                PE.wait(*relo)
                for mc in range(MC):
                    tpo = nc.tensor.matmul(self.ps[:, 5 + ob, 0:TT],
                                           lhsT=vmem[:, mc, (h * XKC + ec) * 128:(h * XKC + ec + 1) * 128],
                                           rhs=E[:, mc, :], start=(mc == 0), stop=(mc == MC - 1))
                tpo = PE.tick(tpo)
                DVE.wait(tpo, trz)
                td = DVE.tick(nc.vector.tensor_tensor(out=Bbf[:, h * XKC + ec, sl], in0=self.ps[:, 5 + ob, 0:TT],
                                                      in1=rz, op=ALU.mult))
                oring.release(ob, td)
            ering.release(er, tpo, td)

        N = len(seq)
        LOOK = 2
        for n in range(N + LOOK):
            if n < N:
                stage_a(n)
            if n - LOOK >= 0:
                stage_b(n - LOOK)
        self.barrier()

    def mlp_layer(self, l):
        cfg = self.cfg
        D, TT = cfg.D, cfg.TT
        Abf, Bbf = self.Abf(), self.Bbf()
        self.items.append(FJob(lambda: self.norm_phase(self.xres, cfg.gain_col("mlp", l))))
        for g in range(cfg.NG):
            self.gemm_to_sbuf(Abf, self.w_m1, l * D, g * D, D, lambda m, tt: Bbf[:, m, tt * TT:(tt + 1) * TT], "relu2")
            self.gemm_rmw(Bbf, self.w_m2, l * cfg.DFF + g * D, self.xres)

    def dump_x(self):
        self.barrier()
        n = self.cfg.D // 128
        for k in range(n):
            self.dma(self.SP, self.outsem, self.yT[k * 128:(k + 1) * 128, :], self.xres[k * 128:(k + 1) * 128, :])

    def finish(self):
        self.barrier()
        self.POOL.wait(*self.pending_bar)


def pack_consts(cfg, p, j):
    c = np.zeros((128, cfg.NCONST), np.float32)
    KD, KA = cfg.KD, cfg.KA

    def put_vec(col, v):
        v = np.asarray(v, np.float32)
        n = v.shape[0] // 128
        c[:, col:col + n] = v.reshape(n, 128).T

    put_vec(cfg.gain_col("mem"), p["mem_norm_g"])
    put_vec(cfg.gain_col("final"), p["final_norm_g"])
    for l in range(cfg.depth):
        put_vec(cfg.gain_col("mix", l), p["mix_norm_g"][l])
        put_vec(cfg.gain_col("xattn", l), p["xattn_norm_g"][l])
        put_vec(cfg.gain_col("mlp", l), p["mlp_norm_g"][l])
    for e in range(cfg.NEVEN):
        w = np.asarray(p["conv_a_w"][e], np.float32)
        blk = w.T.reshape(KA, 128, cfg.CK).transpose(1, 0, 2).reshape(128, KA * cfg.CK)
        c[:, cfg.c_convaw + e * KA * cfg.CK: cfg.c_convaw + (e + 1) * KA * cfg.CK] = blk
        put_vec(cfg.c_convab + e * KA, p["conv_a_b"][e])
        put_vec(cfg.c_lnag + e * KA, p["ln_a_g"][e])
        put_vec(cfg.c_lnab + e * KA, p["ln_a_b"][e])
        c[:, cfg.c_subg + e] = np.asarray(p["subln_g"][e], np.float32)
        for n, name in enumerate(("lambda_q1", "lambda_k1", "lambda_q2", "lambda_k2")):
            c[0:64, cfg.c_lam + 4 * e + n] = np.asarray(p[name][e], np.float32)
    for o in range(cfg.NODD):
        w = np.asarray(p["conv_c_w"][o], np.float32)
        blk = w.T.reshape(KD, 128, cfg.SK).transpose(1, 0, 2).reshape(128, KD * cfg.SK)
        c[:, cfg.c_convcw + o * KD * cfg.SK: cfg.c_convcw + (o + 1) * KD * cfg.SK] = blk
    sel = np.zeros(4, np.float32)
    if j > 0:
        sel[j - 1] = 1.0
    else:
        sel[3] = 1.0
    c[:, cfg.c_sel:cfg.c_sel + 4] = sel[None, :]
    return c


def make_mask(cfg, j):
    TT, SUB = cfg.TT, cfg.SUB
    m = np.zeros((128, 4 * SUB, TT), np.float32)
    pidx = np.arange(128)[:, None]
    cidx = np.arange(TT)[None, :]
    for r in range(4):
        for s in range(SUB):
            m[:, r * SUB + s, :] = ((r * TT + s * 128 + pidx) <= (j * TT + cidx)).astype(np.float32)
    m = np.concatenate([m.reshape(128, 4 * SUB * TT), np.eye(128, dtype=np.float32)], axis=1)
    return m.astype(ml_dtypes.bfloat16)


_PROG_CACHE = {}


def run(cfg, inputs, debug_stop=None):
    key = (cfg.D, cfg.TT, cfg.depth, debug_stop)
    if key not in _PROG_CACHE:
        _PROG_CACHE[key] = Prog(cfg, debug_stop).build()
    nc = _PROG_CACHE[key]
    p = {k: np.asarray(v) for k, v in inputs.items()}
    D, TT, NS, G = cfg.D, cfg.TT, cfg.NS, cfg.G
    x, mem = p["x"], p["mem"]
    shared = {
        "even_w_in": p["even_w_in"].reshape(-1, cfg.EIN), "even_w_out": p["even_w_out"].reshape(-1, D),
        "odd_w_in": p["odd_w_in"].reshape(-1, cfg.OIN), "odd_w_out": p["odd_w_out"].reshape(-1, D),
        "xq_w": p["xq_w"].reshape(-1, D), "xk_w": p["xk_w"].reshape(-1, D),
        "xv_w": p["xv_w"].reshape(-1, D), "xo_w": p["xo_w"].reshape(-1, D),
        "mlp_w1": p["mlp_w1"].reshape(-1, cfg.DFF), "mlp_w2": p["mlp_w2"].reshape(-1, D),
    }
    in_maps = []
    for c in range(cfg.NCORES):
        b, j = divmod(c, G)
        xb = x[b].reshape(NS, G, TT, D)[:, j].reshape(NS * TT, D)
        m = dict(shared)
        m["xT"] = np.ascontiguousarray(xb.T)
        m["memT"] = np.ascontiguousarray(mem[b].T)
        m["consts"] = pack_consts(cfg, p, j)
        m["cmask"] = make_mask(cfg, j)
        in_maps.append(m)
    res = run_bass_kernel_spmd(nc, in_maps, core_ids=list(range(cfg.NCORES)))
    out = np.empty((cfg.NB, cfg.SEQ, D), np.float32)
    ov = out.reshape(cfg.NB, NS, G, TT, D)
    for c in range(cfg.NCORES):
        b, j = divmod(c, G)
        yT = np.asarray(res.results[c]["yT"])
        ov[b, :, j] = yT.T.reshape(NS, TT, D)
    return out


def kernel(**inputs):
    return run(BIG, inputs)
```

```python
import math
from contextlib import ExitStack

import numpy as np
import ml_dtypes

import concourse.bass as bass
import concourse.mybir as mybir
from concourse.bass_utils import run_bass_kernel_spmd

F32 = mybir.dt.float32
BF16 = mybir.dt.bfloat16
AF = mybir.ActivationFunctionType
ALU = mybir.AluOpType
EPS = 1e-6


class Cfg:
    def __init__(self, D=2048, TT=512, MEM=256, WBW=512, depth=4):
        self.D = D
        self.TT = TT
        self.MEM = MEM
        self.WBW = WBW
        self.depth = depth
        self.NS = 4
        self.G = 4
        self.NB = 2
        self.NCORES = self.G * self.NB
        self.KD = D // 128
        self.TOK = self.NS * TT
        self.SEQ = self.G * self.NS * TT
        self.AW = D // 2
        self.BW = D - self.AW
        self.KA = self.AW // 128
        self.NH = self.BW // 128
        self.XH = 4
        self.XHD = D // 4
        self.XKC = self.XHD // 128
        self.DFF = 4 * D
        self.NG = self.DFF // D
        self.SUB = TT // 128
        self.MC = MEM // 128
        self.CK = 31
        self.HA = 30
        self.SK = 3
        self.HC = 2
        self.EIN = 2 * self.AW + 3 * self.BW
        self.OIN = 3 * D
        self.NEVEN = (depth + 1) // 2
        self.NODD = depth // 2
        self.TLA = self.KA * self.NS * self.HA
        self.TLC = self.KD * self.NS * self.HC
        self.TL = max(self.TLA, self.TLC)
        c = 0
        self.c_gain = c; c += 14 * self.KD
        self.c_convaw = c; c += 2 * self.KA * self.CK
        self.c_convab = c; c += 2 * self.KA
        self.c_lnag = c; c += 2 * self.KA
        self.c_lnab = c; c += 2 * self.KA
        self.c_subg = c; c += 2
        self.c_lam = c; c += 2 * 4
        self.c_convcw = c; c += 2 * self.KD * self.SK
        self.c_sel = c; c += 4
        self.NCONST = c

    def gain_col(self, kind, l=0):
        idx = {"mem": 0, "final": 1, "mix": 2, "xattn": 6, "mlp": 10}[kind] + (l if kind in ("mix", "xattn", "mlp") else 0)
        return self.c_gain + idx * self.KD


BIG = Cfg()


class Eng:
    def __init__(self, e, sem, name):
        self.e = e
        self.sem = sem
        self.name = name
        self.n = 0
        self.seen = {}

    def wait(self, *tickets):
        for t in tickets:
            if t is None:
                continue
            if isinstance(t, (list, tuple)) and len(t) > 0 and isinstance(t[0], (list, tuple)):
                self.wait(*t)
                continue
            if isinstance(t, list):
                self.wait(*t)
                continue
            key, sem, val = t
            if self.seen.get(key, 0) >= val:
                continue
            self.e.wait_ge(sem, val)
            self.seen[key] = val

    def tick(self, ins):
        self.n += 1
        ins.then_inc(self.sem, 1)
        return (self.name, self.sem, self.n)

    def last(self):
        return (self.name, self.sem, self.n) if self.n > 0 else None


class DSem:
    def __init__(self, sem, name):
        self.sem = sem
        self.name = name
        self.n = 0

    def issue(self, ins):
        self.n += 16
        ins.then_inc(self.sem, 16)
        return (self.name, self.sem, self.n)

    def last(self):
        return (self.name, self.sem, self.n) if self.n > 0 else None


class Ring:
    def __init__(self, n):
        self.n = n
        self.i = 0
        self.rel = [[] for _ in range(n)]

    def next(self):
        k = self.i % self.n
        self.i += 1
        rel = self.rel[k]
        self.rel[k] = []
        return k, rel

    def release(self, k, *tickets):
        self.rel[k].extend([t for t in tickets if t is not None])


class WJob:
    def __init__(self, pieces, fn):
        self.pieces = pieces
        self.fn = fn


class FJob:
    def __init__(self, fn):
        self.fn = fn


class Prog:
    def __init__(self, cfg, debug_stop=None):
        self.cfg = cfg
        self.debug_stop = debug_stop
        self.items = []

    def build(self):
        cfg = self.cfg
        nc = bass.Bass("TRN2", target_bir_lowering=False)
        self.nc = nc
        D, TOK, TT, KD, MEM = cfg.D, cfg.TOK, cfg.TT, cfg.KD, cfg.MEM
        dt = nc.dram_tensor
        self.xT_in = dt("xT", [D, TOK], F32, kind="ExternalInput").ap()
        self.memT_in = dt("memT", [D, MEM], F32, kind="ExternalInput").ap()
        self.consts_in = dt("consts", [128, cfg.NCONST], F32, kind="ExternalInput").ap()
        self.cmask_in = dt("cmask", [128, 4 * cfg.SUB * TT + 128], BF16, kind="ExternalInput").ap()
        ne, no, dp = cfg.NEVEN, cfg.NODD, cfg.depth
        self.w_even_in = dt("even_w_in", [ne * D, cfg.EIN], F32, kind="ExternalInput").ap()
        self.w_even_out = dt("even_w_out", [ne * D, D], F32, kind="ExternalInput").ap()
        self.w_odd_in = dt("odd_w_in", [max(no, 1) * D, cfg.OIN], F32, kind="ExternalInput").ap()
        self.w_odd_out = dt("odd_w_out", [max(no, 1) * D, D], F32, kind="ExternalInput").ap()
        self.w_xq = dt("xq_w", [dp * D, D], F32, kind="ExternalInput").ap()
        self.w_xk = dt("xk_w", [dp * D, D], F32, kind="ExternalInput").ap()
        self.w_xv = dt("xv_w", [dp * D, D], F32, kind="ExternalInput").ap()
        self.w_xo = dt("xo_w", [dp * D, D], F32, kind="ExternalInput").ap()
        self.w_m1 = dt("mlp_w1", [dp * D, cfg.DFF], F32, kind="ExternalInput").ap()
        self.w_m2 = dt("mlp_w2", [dp * cfg.DFF, D], F32, kind="ExternalInput").ap()
        self.yT = dt("yT", [D, TOK], F32, kind="ExternalOutput").ap()
        self.xres = dt("xres", [D, TOK], F32).ap()
        self.qT = dt("qT_s", [cfg.BW, TOK], BF16).ap()
        self.cuT = dt("cuT_s", [D, TOK], F32).ap()
        self.memnT = dt("memnT_s", [D, MEM], BF16).ap()
        self.HPC = 2 if cfg.NH >= 2 else 1
        self.KR = 128 * self.HPC
        self.nKc = cfg.NH // self.HPC
        self.agK_in = [dt(f"agK_in{c}", [self.KR, TOK], BF16).ap() for c in range(self.nKc)]
        self.agK_out = [dt(f"agK_out{c}", [cfg.G * self.KR, TOK], BF16).ap() for c in range(self.nKc)]
        self.agV_in = [dt(f"agV_in{c}", [TOK, self.KR], BF16).ap() for c in range(self.nKc)]
        self.agV_out = [dt(f"agV_out{c}", [cfg.G * TOK, self.KR], BF16).ap() for c in range(self.nKc)]
        self.agT_in = dt("agT_in", [128, cfg.TL], F32).ap()
        self.agT_out = dt("agT_out", [cfg.G * 128, cfg.TL], F32).ap()

        with ExitStack() as es:
            sb = lambda name, shape, dty: es.enter_context(nc.sbuf_tensor(name, shape, dty))
            self.A = sb("bufA", [128, KD * TOK], BF16)
            self.B = sb("bufB", [128, KD * TOK], BF16)
            self.NWS = 2
            self.W = [sb(f"wslot{i}", [128, KD, cfg.WBW], BF16) for i in range(self.NWS)]
            self.consts = sb("consts_sb", [128, cfg.NCONST], F32)
            self.ones_f = sb("ones_f", [128, 128], F32)
            self.ones_b = sb("ones_b", [128, 128], BF16)
            self.ident_b = sb("ident_b", [128, 128], BF16)
            self.lamv = sb("lamv", [128, 8], F32)
            self.subg = sb("subg", [128, 2], F32)
            self.nscr = 10
            self.scrT = sb("scrT", [128, self.nscr, TT], F32)
            self.scr = [self.scrT[:, i, :] for i in range(self.nscr)]
            self.epsv = sb("epsv", [128, 1], F32)
            self.tailbuf = sb("tailbuf", [128, cfg.TLC], F32)
            r0w = max(2 * cfg.TL, cfg.CK * 128 // 2)
            n_conv = r0w + cfg.TL + 4 * (cfg.HA + TT)
            n_attn = TOK + 4 * TT
            n_x = 3 * cfg.MC * TT // 2
            self.phase_mem = sb("phase_mem", [128, max(n_conv, n_attn, n_x)], F32)
            pm = self.phase_mem
            self.tails2 = [pm[:, i * cfg.TL:(i + 1) * cfg.TL] for i in range(2)]
            self.halo = pm[:, r0w:r0w + cfg.TL]
            self.dg = pm[:, 0:cfg.CK * 128 // 2].bitcast(BF16).rearrange("p (j m) -> p j m", m=128)
            e0 = r0w + cfg.TL
            self.ext = [pm[:, e0 + i * (cfg.HA + TT):e0 + (i + 1) * (cfg.HA + TT)] for i in range(4)]
            n_am = 3 * cfg.SEQ + 4 * cfg.SUB * TT
            if n_am <= KD * TOK:
                self.attn_mem = self.A
            else:
                self.attn_mem = sb("attn_mem", [128, n_am], BF16)
            self.ps = es.enter_context(nc.psum_tensor("ps", [128, 8, 512], F32))
            sem = lambda name: es.enter_context(nc.semaphore(name))
            self.PE = Eng(nc.tensor, sem("s_pe"), "pe")
            self.ACT = Eng(nc.scalar, sem("s_act"), "act")
            self.DVE = Eng(nc.vector, sem("s_dve"), "dve")
            self.POOL = Eng(nc.gpsimd, sem("s_pool"), "pool")
            self.SP = Eng(nc.sync, sem("s_sp"), "sp")
            self.wsem = [DSem(sem(f"s_w{i}"), f"w{i}") for i in range(self.NWS)]
            self.ldsem = [DSem(sem(f"s_ld{i}"), f"ld{i}") for i in range(8)]
            self.stsem = [DSem(sem(f"s_st{i}"), f"st{i}") for i in range(8)]
            self.ccsem = sem("s_cc")
            self.ccn = 0
            self.misc_sem = DSem(sem("s_misc"), "misc")
            self.outsem = DSem(sem("s_out"), "out")
            self.all_dsems = self.ldsem + self.stsem + [self.misc_sem, self.outsem]
            self.pending_bar = []
            self.emit_program()
        return nc

    def Abf(self):
        return self.A[:].rearrange("p (k t) -> p k t", k=self.cfg.KD)

    def Bbf(self):
        return self.B[:].rearrange("p (k t) -> p k t", k=self.cfg.KD)

    def Af32(self):
        return self.A[:].bitcast(F32).rearrange("p (k t) -> p k t", k=self.cfg.KA)

    def Bf32(self):
        return self.B[:].bitcast(F32).rearrange("p (k t) -> p k t", k=self.cfg.KA)

    def cst(self, col, n=1):
        return self.consts[:, col:col + n]

    def barrier(self, engines=None):
        ts = [e.last() for e in (self.PE, self.ACT, self.DVE, self.POOL)]
        ts += [d.last() for d in self.all_dsems]
        if self.ccn > 0:
            ts.append(("cc", self.ccsem, self.ccn))
        ts = [t for t in ts if t is not None]
        for e in (engines or (self.PE, self.ACT, self.DVE, self.SP)):
            e.wait(*ts)
        self.pending_bar = ts

    def pool_sync(self):
        self.POOL.wait(*self.pending_bar)

    def dma(self, q, dsem, out, in_):
        ins = q.e.dma_start(out=out, in_=in_)
        return dsem.issue(ins)

    def emit_program(self):
        cfg = self.cfg
        items = self.items
        items.append(FJob(self.setup))
        x_src = self.xT_in
        stop = False
        for l in range(cfg.depth):
            subs = []
            if l % 2 == 0:
                self.even_layer(l, x_src)
            else:
                self.odd_layer(l, x_src)
            x_src = self.xres
            if self.debug_stop == (l, 0):
                stop = True
                break
            self.xattn_layer(l)
            if self.debug_stop == (l, 1):
                stop = True
                break
            self.mlp_layer(l)
            if self.debug_stop == (l, 2):
                stop = True
                break
        if stop:
            items.append(FJob(self.dump_x))
        else:
            items.append(FJob(lambda: self.norm_phase(x_src, cfg.gain_col("final"), final=True)))
        items.append(FJob(self.finish))
        self.execute()

    def execute(self):
        wjobs = [it for it in self.items if isinstance(it, WJob)]
        for j, w in enumerate(wjobs):
            w.idx = j
        R = self.NWS
        ready = {}
        rel = {}

        def issue(j):
            if j >= len(wjobs):
                return
            slot = j % R
            if j - R >= 0:
                self.POOL.wait(*rel[j - R])
            t = None
            for (src, off, n) in wjobs[j].pieces:
                t = self.dma(self.POOL, self.wsem[slot], self.W[slot][:, :, off:off + n], src)
            ready[j] = t

        for j in range(min(R, len(wjobs))):
            issue(j)
        for it in self.items:
            if isinstance(it, FJob):
                it.fn()
            else:
                j = it.idx
                r = it.fn(self.W[j % R], ready[j])
                rel[j] = r if isinstance(r, list) else [r]
                issue(j + R)

    def wview(self, w2d, row0, c0, n):
        D = self.cfg.D
        return w2d[row0:row0 + D, c0:c0 + n].rearrange("(k p) n -> p k n", p=128)

    def setup(self):
        cfg, nc = self.cfg, self.nc
        SP, DVE, ACT, PE = self.SP, self.DVE, self.ACT, self.PE
        self.dma(SP, self.misc_sem, self.consts[:], self.consts_in[:, :])
        t_c = self.dma(SP, self.misc_sem, self.ident_b[:], self.cmask_in[:, 4 * cfg.SUB * cfg.TT:4 * cfg.SUB * cfg.TT + 128])
        t1 = DVE.tick(nc.vector.memset(self.ones_f[:], 1.0))
        t2 = DVE.tick(nc.vector.memset(self.ones_b[:], 1.0))
        t3 = DVE.tick(nc.vector.memset(self.epsv[:], EPS))
        DVE.wait(t_c)
        for e in range(cfg.NEVEN):
            l = 2 * e
            lam_init = 0.8 - 0.6 * math.exp(-0.3 * l)
            c0 = cfg.c_lam + 4 * e
            prod = self.scr[0][0:64, 0:2]
            ta = DVE.tick(nc.vector.tensor_tensor(out=self.scr[0][0:64, 0:1], in0=self.consts[0:64, c0:c0 + 1],
                                                  in1=self.consts[0:64, c0 + 1:c0 + 2], op=ALU.mult))
            tb = DVE.tick(nc.vector.tensor_tensor(out=self.scr[0][0:64, 1:2], in0=self.consts[0:64, c0 + 2:c0 + 3],
                                                  in1=self.consts[0:64, c0 + 3:c0 + 4], op=ALU.mult))
            PE.wait(ta, tb, t1)
            tp = PE.tick(nc.tensor.matmul(self.ps[:, 0, 0:2], lhsT=self.ones_f[0:64, :], rhs=prod, start=True, stop=True))
            ACT.wait(tp)
            te = ACT.tick(nc.scalar.activation(out=self.scr[1][:, 0:2], in_=self.ps[:, 0, 0:2], func=AF.Exp))
            DVE.wait(te)
            td = DVE.tick(nc.vector.tensor_tensor(out=self.scr[1][:, 2:3], in0=self.scr[1][:, 1:2],
                                                  in1=self.scr[1][:, 0:1], op=ALU.subtract))
            DVE.wait(td)
            tl = DVE.tick(nc.vector.tensor_scalar(out=self.lamv[:, e:e + 1], in0=self.scr[1][:, 2:3],
                                                  scalar1=-lam_init, scalar2=None, op0=ALU.add))
            tg = DVE.tick(nc.vector.tensor_scalar(out=self.subg[:, e:e + 1], in0=self.cst(cfg.c_subg + e),
                                                  scalar1=1.0 - lam_init, scalar2=None, op0=ALU.mult))
            PE.wait(tl)
            ACT.wait(tl)
        self.barrier()
        self.norm_phase(self.memT_in, cfg.gain_col("mem"), mem=True)

    def norm_phase(self, src, gcol, final=False, mem=False):
        cfg, nc = self.cfg, self.nc
        SP, DVE, ACT, PE = self.SP, self.DVE, self.ACT, self.PE
        KD, TT, D = cfg.KD, cfg.TT, cfg.D
        self.barrier()
        ntok = cfg.MEM if mem else cfg.TOK
        W = min(TT, ntok)
        ntile = ntok // W
        srcv = src.rearrange("(k p) t -> p k t", p=128)
        xs_all = self.B[:].bitcast(F32).rearrange("p (b k t) -> p b k t", b=2, k=KD)
        xring = Ring(2)
        psring = Ring(2)
        sqring = Ring(2)
        oring = Ring(2)
        rring = Ring(2)
        Abf = self.Abf()
        for ti in range(ntile):
            off = ti * W
            b, relx = xring.next()
            xs = xs_all[:, b, :, 0:W]
            SP.wait(*relx)
            tld = self.dma(SP, self.ldsem[b], xs, srcv[:, :, off:off + W])
            pb, relp = psring.next()
            pst = self.ps[:, pb, 0:W]
            PE.wait(*relp)
            tpe = None
            for k in range(KD):
                sb_, rels = sqring.next()
                sq = self.scr[sb_][:, 0:W].bitcast(BF16)[:, 0:W]
                ACT.wait(tld, *rels)
                ta = ACT.tick(nc.scalar.activation(out=sq, in_=xs[:, k, :], func=AF.Square))
                PE.wait(ta)
                tpe = PE.tick(nc.tensor.matmul(pst, lhsT=self.ones_b[:], rhs=sq, start=(k == 0), stop=(k == KD - 1)))
                sqring.release(sb_, tpe)
            rb, relr = rring.next()
            sd = self.scr[2 + 4 * rb][:, 0:W]
            rstd = self.scr[3 + 4 * rb][:, 0:W]
            ACT.wait(tpe, *relr)
            ts = ACT.tick(nc.scalar.activation(out=sd, in_=pst, func=AF.Sqrt, bias=self.eps_col(), scale=1.0 / D))
            psring.release(pb, ts)
            DVE.wait(ts)
            tr = DVE.tick(nc.vector.reciprocal(out=rstd, in_=sd))
            DVE.wait(tr)
            tlast = None
            for k in range(KD):
                if final or mem:
                    ob, relo = oring.next()
                    DVE.wait(*relo)
                    if final:
                        dst = self.scr[4 + ob][:, 0:W]
                    else:
                        dst = self.scr[4 + ob][:, 0:W].bitcast(BF16)[:, 0:W]
                else:
                    dst = Abf[:, k, off:off + W]
                tlast = DVE.tick(nc.vector.scalar_tensor_tensor(out=dst, in0=xs[:, k, :], scalar=self.cst(gcol + k),
                                                                in1=rstd, op0=ALU.mult, op1=ALU.mult))
                if final:
                    SP.wait(tlast)
                    tst = self.dma(SP, self.outsem, self.yT[k * 128:(k + 1) * 128, off:off + W], dst)
                    oring.release(ob, tst)
                elif mem:
                    SP.wait(tlast)
                    tst = self.dma(SP, self.stsem[ob], self.memnT[k * 128:(k + 1) * 128, off:off + W], dst)
                    oring.release(ob, tst)
            xring.release(b, tlast)
            rring.release(rb, tlast)
        self.barrier()

    def mm_group(self, pst, wslot, mloc, in_buf, t0, w, ready, extra_wait=()):
        nc, PE, KD = self.nc, self.PE, self.cfg.KD
        PE.wait(ready, *extra_wait)
        ins = None
        for k in range(KD):
            ins = nc.tensor.matmul(pst, lhsT=wslot[:, k, mloc * 128:(mloc + 1) * 128], rhs=in_buf[:, k, t0:t0 + w],
                                   start=(k == 0), stop=(k == KD - 1))
        return PE.tick(ins)

    def gemm_rmw(self, in_buf, w2d, row0, x_src):
        cfg, nc = self.cfg, self.nc
        SP, DVE, PE = self.SP, self.DVE, self.PE
        D, TT, NS, WBW = cfg.D, cfg.TT, cfg.NS, cfg.WBW
        nblk = D // WBW
        mper = WBW // 128
        tiles = [(blk, ml, tt) for blk in range(nblk) for ml in range(mper) for tt in range(NS)]
        st = {"ld": 0, "tld": {}, "inring": Ring(3), "outring": Ring(2), "psring": Ring(8), "slotmap": {}}
        PRE = 2

        def load(n):
            if n >= len(tiles) or n in st["tld"]:
                return
            blk, ml, tt = tiles[n]
            m = blk * mper + ml
            s, rel = st["inring"].next()
            st["slotmap"][n] = s
            SP.wait(*rel)
            st["tld"][n] = self.dma(SP, self.ldsem[s], self.scr[s][:, 0:TT],
                                    x_src[m * 128:(m + 1) * 128, tt * TT:(tt + 1) * TT])

        def job(blk):
            def fn(wslot, ready):
                tp = None
                for n, (b2, ml, tt) in enumerate(tiles):
                    if b2 != blk:
                        continue
                    m = blk * mper + ml
                    for q in range(n, n + PRE + 1):
                        load(q)
                    pb, relp = st["psring"].next()
                    pst = self.ps[:, pb, 0:TT]
                    tp = self.mm_group(pst, wslot, ml, in_buf, tt * TT, TT, ready, relp)
                    ob, relo = st["outring"].next()
                    xo = self.scr[3 + ob][:, 0:TT]
                    s = st["slotmap"][n]
                    DVE.wait(tp, st["tld"][n], *relo)
                    td = DVE.tick(nc.vector.tensor_tensor(out=xo, in0=pst, in1=self.scr[s][:, 0:TT], op=ALU.add))
                    st["psring"].release(pb, td)
                    st["inring"].release(s, td)
                    SP.wait(td)
                    tst = self.dma(SP, self.stsem[ob], self.xres[m * 128:(m + 1) * 128, tt * TT:(tt + 1) * TT], xo)
                    st["outring"].release(ob, tst)
                return tp
            return fn

        self.items.append(FJob(self.barrier))
        for blk in range(nblk):
            self.items.append(WJob([(self.wview(w2d, row0, blk * WBW, WBW), 0, WBW)], job(blk)))
        self.items.append(FJob(self.barrier))

    def gemm_to_sbuf(self, in_buf, w2d, row0, col0, ncols, dst_fn, mode, tok_tiles=None):
        cfg, nc = self.cfg, self.nc
        DVE, ACT = self.DVE, self.ACT
        TT, NS, WBW = cfg.TT, cfg.NS, cfg.WBW
        nblk = ncols // WBW
        mper = WBW // 128
        st = {"psring": Ring(8), "tring": Ring(2), "cnt": 0}

        def job(blk):
            def fn(wslot, ready):
                tp = None
                for ml in range(mper):
                    m = blk * mper + ml
                    for tt, (tk0, tkw) in enumerate(tok_tiles or [(t_ * TT, TT) for t_ in range(NS)]):
                        pb, relp = st["psring"].next()
                        pst = self.ps[:, pb, 0:tkw]
                        tp = self.mm_group(pst, wslot, ml, in_buf, tk0, tkw, ready, relp)
                        dst = dst_fn(m, tt)
                        if mode == "copy":
                            st["cnt"] += 1
                            if st["cnt"] % 2 == 0:
                                ACT.wait(tp)
                                tc_ = ACT.tick(nc.scalar.copy(out=dst, in_=pst))
                            else:
                                DVE.wait(tp)
                                tc_ = DVE.tick(nc.vector.tensor_copy(out=dst, in_=pst))
                            st["psring"].release(pb, tc_)
                        else:
                            tb, relt = st["tring"].next()
                            tmp = self.scr[tb][:, 0:TT]
                            ACT.wait(tp, *relt)
                            ta = ACT.tick(nc.scalar.activation(out=tmp, in_=pst, func=AF.Relu))
                            st["psring"].release(pb, ta)
                            DVE.wait(ta)
                            td = DVE.tick(nc.vector.tensor_tensor(out=dst, in0=tmp, in1=tmp, op=ALU.mult))
                            st["tring"].release(tb, td)
                return tp
            return fn

        self.items.append(FJob(self.barrier))
        for blk in range(nblk):
            self.items.append(WJob([(self.wview(w2d, row0, col0 + blk * WBW, WBW), 0, WBW)], job(blk)))
        self.items.append(FJob(self.barrier))

    def gemm_to_dram(self, in_buf, w2d, row0, col0, ncols, dram_dst):
        cfg, nc = self.cfg, self.nc
        DVE, ACT, SP = self.DVE, self.ACT, self.SP
        TT, NS, WBW = cfg.TT, cfg.NS, cfg.WBW
        nblk = ncols // WBW
        mper = WBW // 128
        st = {"psring": Ring(8), "oring": Ring(4), "cnt": 0}

        def job(blk):
            def fn(wslot, ready):
                tp = None
                for ml in range(mper):
                    m = blk * mper + ml
                    for tt in range(NS):
                        pb, relp = st["psring"].next()
                        pst = self.ps[:, pb, 0:TT]
                        tp = self.mm_group(pst, wslot, ml, in_buf, tt * TT, TT, ready, relp)
                        ob, relo = st["oring"].next()
                        dst = self.scr[ob][:, 0:TT].bitcast(BF16)[:, 0:TT]
                        st["cnt"] += 1
                        if st["cnt"] % 2 == 0:
                            ACT.wait(tp, *relo)
                            tc_ = ACT.tick(nc.scalar.copy(out=dst, in_=pst))
                        else:
                            DVE.wait(tp, *relo)
                            tc_ = DVE.tick(nc.vector.tensor_copy(out=dst, in_=pst))
                        st["psring"].release(pb, tc_)
                        SP.wait(tc_)
                        tst = self.dma(SP, self.stsem[ob], dram_dst(m, tt), dst)
                        st["oring"].release(ob, tst)
                return tp
            return fn

        self.items.append(FJob(self.barrier))
        for blk in range(nblk):
            self.items.append(WJob([(self.wview(w2d, row0, col0 + blk * WBW, WBW), 0, WBW)], job(blk)))
        self.items.append(FJob(self.barrier))

    def gemm_tokmajor(self, in_buf, ntok, w2d, row0, col0, ncols, dst_kind, dst):
        cfg, nc = self.cfg, self.nc
        DVE, ACT, SP, PE = self.DVE, self.ACT, self.SP, self.PE
        KD, WBW = cfg.KD, cfg.WBW
        nblk = ncols // WBW
        ntb = ntok // 128
        st = {"psring": Ring(8), "oring": Ring(2), "cnt": 0}

        def job(blk):
            def fn(wslot, ready):
                tp = None
                for tb in range(ntb):
                    pb, relp = st["psring"].next()
                    pst = self.ps[:, pb, 0:WBW]
                    PE.wait(ready, *relp)
                    ins = None
                    for k in range(KD):
                        ins = nc.tensor.matmul(pst, lhsT=in_buf[:, k, tb * 128:(tb + 1) * 128], rhs=wslot[:, k, 0:WBW],
                                               start=(k == 0), stop=(k == KD - 1))
                    tp = PE.tick(ins)
                    st["cnt"] += 1
                    eng, fnc = (ACT, nc.scalar.copy) if st["cnt"] % 2 == 0 else (DVE, nc.vector.tensor_copy)
                    if dst_kind == "dram":
                        ob, relo = st["oring"].next()
                        o = self.scr[ob][:, 0:WBW // 2].bitcast(BF16)[:, 0:WBW] if WBW // 2 <= cfg.TT else None
                        eng.wait(tp, *relo)
                        tc_ = eng.tick(fnc(out=o, in_=pst))
                        st["psring"].release(pb, tc_)
                        SP.wait(tc_)
                        for (dap, c0, c1) in dst(tb, blk):
                            tst = self.dma(SP, self.stsem[ob], dap, o[:, c0:c1])
                        st["oring"].release(ob, tst)
                    else:
                        eng.wait(tp)
                        tc_ = eng.tick(fnc(out=dst(tb, blk), in_=pst))
                        st["psring"].release(pb, tc_)
                return tp
            return fn

        self.items.append(FJob(self.barrier))
        for blk in range(nblk):
            self.items.append(WJob([(self.wview(w2d, row0, col0 + blk * WBW, WBW), 0, WBW)], job(blk)))
        self.items.append(FJob(self.barrier))

    def gemm_gated(self, in_buf, w2d, row0, colA, colB, ncols, kind, dst):
        cfg, nc = self.cfg, self.nc
        DVE, ACT, SP = self.DVE, self.ACT, self.SP
        TT, NS, WBW = cfg.TT, cfg.NS, cfg.WBW
        half = WBW // 2
        nblk = ncols // half
        mper = half // 128
        st = {"psring": Ring(4), "tring": Ring(2), "oring": Ring(2)}
        tb_view = self.tailbuf[:, 0:cfg.TLC].rearrange("p (k i t) -> p k i t", k=cfg.KD, i=NS)

        def job(blk):
            def fn(wslot, ready):
                tp = None
                for ml in range(mper):
                    m = blk * mper + ml
                    for tt in range(NS):
                        pb, relp = st["psring"].next()
                        pA = self.ps[:, 2 * pb, 0:TT]
                        pB = self.ps[:, 2 * pb + 1, 0:TT]
                        tpa = self.mm_group(pA, wslot, ml, in_buf, tt * TT, TT, ready, relp)
                        tp = self.mm_group(pB, wslot, mper + ml, in_buf, tt * TT, TT, ready)
                        tb, relt = st["tring"].next()
                        tmp = self.scr[tb][:, 0:TT]
                        ACT.wait(tp, *relt)
                        if kind == "glu":
                            ta = ACT.tick(nc.scalar.activation(out=tmp, in_=pB, func=AF.Sigmoid))
                        else:
                            ta = ACT.tick(nc.scalar.copy(out=tmp, in_=pB))
                        DVE.wait(ta, tpa)
                        if kind == "glu":
                            td = DVE.tick(nc.vector.tensor_tensor(out=dst(m, tt), in0=pA, in1=tmp, op=ALU.mult))
                            st["tring"].release(tb, td)
                            st["psring"].release(pb, td)
                        else:
                            ob, relo = st["oring"].next()
                            o = self.scr[2 + ob][:, 0:TT]
                            DVE.wait(*relo)
                            td = DVE.tick(nc.vector.tensor_tensor(out=o, in0=pA, in1=tmp, op=ALU.mult))
                            st["tring"].release(tb, td)
                            st["psring"].release(pb, td)
                            DVE.wait(td)
                            tt2 = DVE.tick(nc.vector.tensor_copy(out=tb_view[:, m, tt, :], in_=o[:, TT - cfg.HC:TT]))
                            SP.wait(td)
                            tst = self.dma(SP, self.stsem[ob], self.cuT[m * 128:(m + 1) * 128, tt * TT:(tt + 1) * TT], o)
                            st["oring"].release(ob, tst, tt2)
                return tp
            return fn

        self.items.append(FJob(self.barrier))
        for blk in range(nblk):
            pieces = [(self.wview(w2d, row0, colA + blk * half, half), 0, half),
                      (self.wview(w2d, row0, colB + blk * half, half), half, half)]
            self.items.append(WJob(pieces, job(blk)))
        self.items.append(FJob(self.barrier))

    def allgather(self, src, dst, wait_tickets):
        nc, POOL, cfg = self.nc, self.POOL, self.cfg
        POOL.wait(*wait_tickets)
        groups = [[b * cfg.G + r for r in range(cfg.G)] for b in range(cfg.NB)]
        ins = nc.gpsimd.collective_compute("AllGather", ALU.bypass, replica_groups=groups,
                                           ins=[src], outs=[dst])
        ins.then_inc(self.ccsem, 1)
        self.ccn += 1
        return ("cc", self.ccsem, self.ccn)

    def even_layer(self, l, x_src):
        cfg = self.cfg
        e = l // 2
        D = cfg.D
        items = self.items
        items.append(FJob(lambda: self.norm_phase(x_src, cfg.gain_col("mix", l))))
        Abf, Bbf, Bf32 = self.Abf(), self.Bbf(), self.Bf32()
        TT = cfg.TT
        w_in = self.w_even_in
        r0 = e * D
        self.gemm_gated(Abf, w_in, r0, 0, cfg.AW, cfg.AW, "glu", lambda m, tt: Bf32[:, m, tt * TT:(tt + 1) * TT])
        items.append(FJob(self.send_tails_even))
        KR, WBW = self.KR, cfg.WBW
        self.gemm_to_dram(Abf, w_in, r0, 2 * cfg.AW, cfg.BW,
                          lambda m, tt: self.qT[m * 128:(m + 1) * 128, tt * TT:(tt + 1) * TT])
        self.gemm_to_dram(Abf, w_in, r0, 2 * cfg.AW + cfg.BW, cfg.BW,
                          lambda m, tt: self.agK_in[(m * 128) // KR][(m * 128) % KR:(m * 128) % KR + 128,
                                                                      tt * TT:(tt + 1) * TT])
        def vdst(tb, blk):
            out = []
            for c0 in range(0, WBW, KR):
                col = blk * WBW + c0
                out.append((self.agV_in[col // KR][tb * 128:(tb + 1) * 128, 0:KR], c0, c0 + KR))
            return out
        self.gemm_tokmajor(Abf, cfg.TOK, w_in, r0, 2 * cfg.AW + 2 * cfg.BW, cfg.BW, "dram", vdst)
        items.append(FJob(lambda: self.even_exchange_conv(e)))
        items.append(FJob(lambda: self.even_ln_silu(e)))
        items.append(FJob(lambda: self.diff_attention(e)))
        self.gemm_rmw(Bbf, self.w_even_out, r0, x_src)

    def conv_taps(self, eng, ne, chains, wcol, bcol, ntap, base):
        TT = self.cfg.TT
        for j in range(ntap):
            for c in chains:
                eng.wait(c["t"])
                src = c["ext"][:, base + j:base + j + TT]
                if j == 0:
                    if bcol is not None:
                        ins = ne.tensor_scalar(out=c["dst"], in0=src, scalar1=self.cst(wcol(c["k"], 0)),
                                               scalar2=self.cst(bcol(c["k"])), op0=ALU.mult, op1=ALU.add)
                    else:
                        ins = ne.tensor_scalar(out=c["dst"], in0=src, scalar1=self.cst(wcol(c["k"], 0)),
                                               scalar2=None, op0=ALU.mult)
                else:
                    ins = ne.scalar_tensor_tensor(out=c["dst"], in0=src, scalar=self.cst(wcol(c["k"], j)),
                                                  in1=c["dst"], op0=ALU.mult, op1=ALU.add)
                c["t"] = eng.tick(ins)

    def exchange_tails(self, tl, per, cT):
        cfg, nc = self.cfg, self.nc
        SP, DVE = self.SP, self.DVE
        NS = cfg.NS
        h = self.halo[:, 0:tl]
        nk = tl // (NS * per)
        hv = h.rearrange("p (k i t) -> p k i t", k=nk, i=NS)
        sel = lambda i: self.cst(cfg.c_sel + i)
        SP.wait(cT)
        t = None
        for r in range(cfg.G):
            tb = self.tails2[r % 2][:, 0:tl]
            SP.wait(t)
            tld = self.dma(SP, self.ldsem[r % 2], tb, self.agT_out[r * 128:(r + 1) * 128, 0:tl])
            DVE.wait(tld, t)
            if r == 0:
                t = DVE.tick(nc.vector.tensor_scalar(out=h, in0=tb, scalar1=sel(0), scalar2=None, op0=ALU.mult))
            elif r < cfg.G - 1:
                t = DVE.tick(nc.vector.scalar_tensor_tensor(out=h, in0=tb, scalar=sel(r), in1=h,
                                                            op0=ALU.mult, op1=ALU.add))
            else:
                tv = tb.rearrange("p (k i t) -> p k i t", k=nk, i=NS)
                t = DVE.tick(nc.vector.scalar_tensor_tensor(out=hv[:, :, 1:NS, :], in0=tv[:, :, 0:NS - 1, :],
                                                            scalar=sel(3), in1=hv[:, :, 1:NS, :],
                                                            op0=ALU.mult, op1=ALU.add))
        return t, hv

    def send_tails_even(self):
        cfg = self.cfg
        TT, NS, KA, HA = cfg.TT, cfg.NS, cfg.KA, cfg.HA
        Bf32 = self.Bf32()
        self.barrier(engines=(self.SP,))
        tin = self.agT_in[:, 0:cfg.TLA].rearrange("p (k i t) -> p k i t", k=KA, i=NS)
        tt_ = None
        for k in range(KA):
            src = Bf32[:, k, :].rearrange("p (i t) -> p i t", t=TT)[:, :, TT - HA:TT]
            tt_ = self.dma(self.SP, self.stsem[0], tin[:, k, :, :], src)
        self.cc_tails = self.allgather(self.agT_in[:, :], self.agT_out[:, :], [tt_])

    def send_tails_odd(self):
        cfg = self.cfg
        self.barrier(engines=(self.SP,))
        tt_ = self.dma(self.SP, self.stsem[0], self.agT_in[:, 0:cfg.TLC], self.tailbuf[:, 0:cfg.TLC])
        self.cc_tails = self.allgather(self.agT_in[:, :], self.agT_out[:, :], [tt_])

    def even_exchange_conv(self, e):
        cfg, nc = self.cfg, self.nc
        SP, DVE, ACT, POOL = self.SP, self.DVE, self.ACT, self.POOL
        TT, NS, KA, HA, CK = cfg.TT, cfg.NS, cfg.KA, cfg.HA, cfg.CK
        Bf32, Af32 = self.Bf32(), self.Af32()
        self.barrier()
        self.pool_sync()
        th, hv = self.exchange_tails(cfg.TLA, HA, self.cc_tails)
        self.cc_pair = []
        for c in range(self.nKc):
            self.allgather(self.agK_in[c][:, :], self.agK_out[c][:, :], [])
            self.allgather(self.agV_in[c][:, :], self.agV_out[c][:, :], [])
            self.cc_pair.append(("cc", self.ccsem, self.ccn))
        PE = self.PE
        wcol = lambda k, j: cfg.c_convaw + (e * KA + k) * CK + j
        bcol = lambda k: cfg.c_convab + e * KA + k
        extb = [x_.bitcast(BF16)[:, 0:HA + TT] for x_ in self.ext]
        extring = Ring(4)
        psring = Ring(4)
        dg_rel = []
        for k in range(KA):
            DVE.wait(th, *dg_rel)
            td = None
            for j in range(CK):
                td = DVE.tick(nc.vector.tensor_scalar(out=self.dg[:, j, :], in0=self.ident_b[:],
                                                      scalar1=self.cst(wcol(k, j)), scalar2=None, op0=ALU.mult))
            tpl = None
            for i in range(NS):
                xb, relx = extring.next()
                ACT.wait(th, *relx)
                ACT.tick(nc.scalar.copy(out=extb[xb][:, 0:HA], in_=hv[:, k, i, :]))
                t2 = ACT.tick(nc.scalar.copy(out=extb[xb][:, HA:HA + TT], in_=Bf32[:, k, i * TT:(i + 1) * TT]))
                pb, relp = psring.next()
                PE.wait(td, t2, *relp)
                ins = None
                for j in range(CK):
                    ins = nc.tensor.matmul(self.ps[:, pb, 0:TT], lhsT=self.dg[:, j, :], rhs=extb[xb][:, j:j + TT],
                                           start=(j == 0), stop=(j == CK - 1))
                tpl = PE.tick(ins)
                extring.release(xb, tpl)
                ACT.wait(tpl)
                ta = ACT.tick(nc.scalar.activation(out=Af32[:, k, i * TT:(i + 1) * TT], in_=self.ps[:, pb, 0:TT],
                                                   func=AF.Identity, bias=self.cst(bcol(k)), scale=1.0))
                psring.release(pb, ta)
            dg_rel = [tpl]
        self.barrier()

    def even_ln_silu(self, e):
        cfg, nc = self.cfg, self.nc
        DVE, ACT, PE = self.DVE, self.ACT, self.PE
        TT, NS, KA, AW = cfg.TT, cfg.NS, cfg.KA, cfg.AW
        Af32, Bbf = self.Af32(), self.Bbf()
        self.barrier()
        psring = Ring(2)
        sqring = Ring(2)
        tring = Ring(2)
        prev = None
        for i in range(NS):
            sl = slice(i * TT, (i + 1) * TT)
            pb, relp = psring.next()
            S1 = self.ps[:, 2 * pb, 0:TT]
            S2 = self.ps[:, 2 * pb + 1, 0:TT]
            PE.wait(*relp)
            tp = None
            for k in range(KA):
                nc.tensor.matmul(S1, lhsT=self.ones_f[:], rhs=Af32[:, k, sl], start=(k == 0), stop=(k == KA - 1))
                sb_, rels = sqring.next()
                sq = self.scr[sb_][:, 0:TT]
                ACT.wait(*rels)
                ta = ACT.tick(nc.scalar.activation(out=sq, in_=Af32[:, k, sl], func=AF.Square))
                PE.wait(ta)
                tp = PE.tick(nc.tensor.matmul(S2, lhsT=self.ones_f[:], rhs=sq, start=(k == 0), stop=(k == KA - 1)))
                sqring.release(sb_, tp)
            mean, msq, sd, rstd = (self.scr[j][:, 0:TT] for j in (4, 5, 6, 7))
            DVE.wait(tp, prev)
            t = DVE.tick(nc.vector.tensor_scalar(out=mean, in0=S1, scalar1=1.0 / AW, scalar2=None, op0=ALU.mult))
            DVE.wait(t)
            t = DVE.tick(nc.vector.tensor_tensor(out=msq, in0=mean, in1=mean, op=ALU.mult))
            DVE.wait(t)
            t = DVE.tick(nc.vector.scalar_tensor_tensor(out=msq, in0=S2, scalar=1.0 / AW, in1=msq,
                                                        op0=ALU.mult, op1=ALU.subtract))
            psring.release(pb, t)
            ACT.wait(t, prev)
            ts = ACT.tick(nc.scalar.activation(out=sd, in_=msq, func=AF.Sqrt, bias=self.eps_col(), scale=1.0))
            DVE.wait(ts)
            tr = DVE.tick(nc.vector.reciprocal(out=rstd, in_=sd))
            for k in range(KA):
                tb, relt = tring.next()
                tmp = self.scr[8 + tb][:, 0:TT]
                DVE.wait(tr, *relt)
                t = DVE.tick(nc.vector.tensor_tensor(out=tmp, in0=Af32[:, k, sl], in1=mean, op=ALU.subtract))
                DVE.wait(t)
                t = DVE.tick(nc.vector.tensor_tensor(out=tmp, in0=tmp, in1=rstd, op=ALU.mult))
                ACT.wait(t)
                ta = ACT.tick(nc.scalar.activation(out=Bbf[:, k, sl], in_=tmp, func=AF.Silu,
                                                   bias=self.cst(cfg.c_lnab + e * KA + k),
                                                   scale=self.cst(cfg.c_lnag + e * KA + k)))
                tring.release(tb, ta)
                prev = ta
        self.barrier()

    def eps_col(self):
        return self.epsv[:, 0:1]

    def diff_attention(self, e):
        cfg, nc = self.cfg, self.nc
        SP, DVE, ACT, PE, POOL = self.SP, self.DVE, self.ACT, self.PE, self.POOL
        TT, NS, G, SUB, NH, KA, TOK, SEQ = cfg.TT, cfg.NS, cfg.G, cfg.SUB, cfg.NH, cfg.KA, cfg.TOK, cfg.SEQ
        Bbf = self.Bbf()
        self.barrier()
        self.pool_sync()
        am = self.attn_mem
        Kb = [am[:, kb * SEQ:(kb + 1) * SEQ] for kb in range(2)]
        Vb = am[:, 2 * SEQ:3 * SEQ].rearrange("p (c e) -> p c e", e=128)
        mask = am[:, 3 * SEQ:3 * SEQ + 4 * SUB * TT].rearrange("p (m t) -> p m t", t=TT)
        pm = self.phase_mem
        qh = [pm[:, j * (TOK // 2):(j + 1) * (TOK // 2)].bitcast(BF16) for j in range(2)]
        e0 = TOK
        Er = [pm[:, e0 + j * TT:e0 + (j + 1) * TT].bitcast(BF16).rearrange("p (c t) -> p c t", c=2) for j in range(4)]
        tmask = self.dma(SP, self.ldsem[7], mask, self.cmask_in[:, 0:4 * SUB * TT].rearrange("p (m t) -> p m t", t=TT))
        scale = 64 ** -0.5
        nkv_tiles = G * NS

        def load_k(h):
            SP.wait(self.cc_pair[(h * 128) // self.KR])
            kb = h % 2
            kv = Kb[kb].rearrange("p (i r t) -> p i r t", i=NS, r=G)
            t = None
            for r in range(G):
                kc_, ko_ = (h * 128) // self.KR, (h * 128) % self.KR
                src = self.agK_out[kc_][r * self.KR + ko_:r * self.KR + ko_ + 128, :].rearrange("p (i t) -> p i t", t=TT)
                t = self.dma(SP, self.ldsem[kb], kv[:, :, r, :], src)
            return t

        def load_q(h):
            return self.dma(SP, self.ldsem[2 + h % 2], qh[h % 2], self.qT[h * 128:(h + 1) * 128, :])

        def load_v(h):
            vv = Vb.rearrange("p (i r s) e -> p i r s e", i=NS, r=G)
            t = None
            for r in range(G):
                for i2 in range(NS):
                    vo_ = (h * 128) % self.KR
                    src = self.agV_out[(h * 128) // self.KR][r * TOK + i2 * TT:r * TOK + (i2 + 1) * TT, vo_:vo_ + 128]
                    src = src.rearrange("(s p) e -> p s e", p=128)
                    t = self.dma(SP, self.ldsem[4], vv[:, i2, r, :, :], src)
            return t

        tk = {0: load_k(0)}
        tq = {0: load_q(0)}
        tv = {0: load_v(0)}
        if NH > 1:
            tk[1] = load_k(1)
            tq[1] = load_q(1)
        spair = Ring(2)
        ering = Ring(len(Er))
        seq = [(h, i, kc, G * (i + 1) * SUB) for h in range(NH) for i in range(NS) for kc in range(G * (i + 1) * SUB)]
        stA = {}
        state = {"oz_rel": [], "eb": 0, "tsum": None}
        esring = Ring(2)
        LOOK = 2

        def stage_a(n):
            h, i, kc, nk = seq[n]
            Kh, Q = Kb[h % 2], qh[h % 2]
            t = kc // SUB
            s = kc % SUB
            sp, rels = spair.next()
            PE.wait(tk[h], tq[h], *rels)
            tps = None
            for c in range(2):
                tps = nc.tensor.matmul(self.ps[:, 2 * sp + c, 0:TT],
                                       lhsT=Kh[c * 64:(c + 1) * 64, t * TT + s * 128:t * TT + (s + 1) * 128],
                                       rhs=Q[c * 64:(c + 1) * 64, i * TT:(i + 1) * TT], start=True, stop=True)
            tps = PE.tick(tps)
            er, rele = ering.next()
            E = Er[er]
            ACT.wait(tps, *rele)
            ta = ACT.tick(nc.scalar.activation(out=E, in_=self.ps[:, 2 * sp:2 * sp + 2, 0:TT], func=AF.Exp, scale=scale))
            spair.release(sp, ta)
            te = ta
            if t >= G * i:
                r = t - G * i
                DVE.wait(ta, tmask)
                for c in range(2):
                    te = DVE.tick(nc.vector.tensor_tensor(out=E[:, c, :], in0=E[:, c, :],
                                                          in1=mask[:, r * SUB + s, :], op=ALU.mult))
            if kc == 0:
                state["eb"], rel_es = esring.next()
                state["rel_es"] = rel_es
                state["tsum"] = [None, None]
            par = kc % 2
            eng, ne = (DVE, nc.vector)
            Es = self.scr[4 + 2 * state["eb"] + par][:, 0:TT]
            if kc < 2:
                eng.wait(te, *state["rel_es"])
                tsum = eng.tick(ne.tensor_copy(out=Es, in_=E[:, 1, :]))
            else:
                eng.wait(te, state["tsum"][par])
                tsum = eng.tick(ne.tensor_tensor(out=Es, in0=Es, in1=E[:, 1, :], op=ALU.add))
            state["tsum"][par] = tsum
            if kc == nk - 1:
                state["tsum_final"] = list(state["tsum"])
            stA[n] = (er, te, tsum, state["eb"], state.get("tsum_final") if kc == nk - 1 else None)

        def stage_b(n):
            h, i, kc, nk = seq[n]
            er, te, tsum, eb, tfin = stA.pop(n)
            E = Er[er]
            PE.wait(te, tv[h])
            if kc == 0:
                PE.wait(*state["oz_rel"])
                state["oz_rel"] = []
            ins = None
            for c in range(2):
                ins = nc.tensor.matmul(self.ps[:, 4 + c, 0:TT], lhsT=Vb[:, kc, :], rhs=E[:, c, :],
                                       start=(kc == 0), stop=(kc == nk - 1))
            ins = nc.tensor.matmul(self.ps[:, 6, 0:TT], lhsT=self.ones_b[:], rhs=E[:, 0, :],
                                   start=(kc == 0), stop=(kc == nk - 1))
            tp_last = PE.tick(ins)
            ering.release(er, tp_last, tsum)
            if kc == nk - 1:
                PE.wait(*tfin)
                nc.tensor.matmul(self.ps[:, 7, 0:TT], lhsT=self.ones_f[:], rhs=self.scr[4 + 2 * eb][:, 0:TT],
                                 start=True, stop=False)
                ins = nc.tensor.matmul(self.ps[:, 7, 0:TT], lhsT=self.ones_f[:], rhs=self.scr[5 + 2 * eb][:, 0:TT],
                                       start=False, stop=True)
                tp_last = PE.tick(ins)
                esring.release(eb, tp_last)
                rz0, t0, rz1, t1 = (self.scr[j][:, 0:TT] for j in (0, 1, 2, 3))
                DVE.wait(tp_last)
                a = DVE.tick(nc.vector.reciprocal(out=rz0, in_=self.ps[:, 6, 0:TT]))
                b_ = DVE.tick(nc.vector.reciprocal(out=rz1, in_=self.ps[:, 7, 0:TT]))
                DVE.wait(a)
                c_ = DVE.tick(nc.vector.tensor_tensor(out=t0, in0=self.ps[:, 4, 0:TT], in1=rz0, op=ALU.mult))
                DVE.wait(b_)
                d = DVE.tick(nc.vector.tensor_tensor(out=t1, in0=self.ps[:, 5, 0:TT], in1=rz1, op=ALU.mult))
                state["oz_rel"] = [d]
                DVE.wait(c_, d)
                DVE.tick(nc.vector.scalar_tensor_tensor(out=Bbf[:, KA + h, i * TT:(i + 1) * TT], in0=t1,
                                                        scalar=self.lamv[:, e:e + 1], in1=t0,
                                                        op0=ALU.mult, op1=ALU.add))
                if i == NS - 1:
                    SP.wait(tp_last)
                    if h + 1 < NH:
                        tv[h + 1] = load_v(h + 1)
                    if h + 2 < NH:
                        tk[h + 2] = load_k(h + 2)
                        tq[h + 2] = load_q(h + 2)

        N = len(seq)
        for n in range(N + LOOK):
            if n < N:
                stage_a(n)
            if n - LOOK >= 0:
                stage_b(n - LOOK)
        self.barrier()
        psring = Ring(2)
        sqring = Ring(2)
        for h in range(NH):
            for i in range(NS):
                o = Bbf[:, KA + h, i * TT:(i + 1) * TT]
                sb_, rels = sqring.next()
                sq = self.scr[sb_][:, 0:TT]
                ACT.wait(*rels)
                ta = ACT.tick(nc.scalar.activation(out=sq, in_=o, func=AF.Square))
                pb, relp = psring.next()
                PE.wait(ta, *relp)
                tp = PE.tick(nc.tensor.matmul(self.ps[:, pb, 0:TT], lhsT=self.ones_f[:], rhs=sq, start=True, stop=True))
                sd = self.scr[2 + sb_][:, 0:TT]
                rstd = self.scr[4 + sb_][:, 0:TT]
                ACT.wait(tp)
                ts = ACT.tick(nc.scalar.activation(out=sd, in_=self.ps[:, pb, 0:TT], func=AF.Sqrt,
                                                   bias=self.eps_col(), scale=1.0 / 128))
                psring.release(pb, ts)
                DVE.wait(ts)
                tr = DVE.tick(nc.vector.reciprocal(out=rstd, in_=sd))
                DVE.wait(tr)
                td = DVE.tick(nc.vector.scalar_tensor_tensor(out=o, in0=o, scalar=self.subg[:, e:e + 1], in1=rstd,
                                                             op0=ALU.mult, op1=ALU.mult))
                sqring.release(sb_, td)
        self.barrier()

    def odd_layer(self, l, x_src):
        cfg = self.cfg
        o = l // 2
        D, TT = cfg.D, cfg.TT
        items = self.items
        items.append(FJob(lambda: self.norm_phase(x_src, cfg.gain_col("mix", l))))
        Abf, Bbf = self.Abf(), self.Bbf()
        r0 = o * D
        self.gemm_gated(Abf, self.w_odd_in, r0, D, 2 * D, D, "mul", None)
        items.append(FJob(self.send_tails_odd))
        self.gemm_to_sbuf(Abf, self.w_odd_in, r0, 0, D, lambda m, tt: Bbf[:, m, tt * TT:(tt + 1) * TT], "copy")
        items.append(FJob(lambda: self.odd_exchange_conv(o)))
        self.gemm_rmw(Abf, self.w_odd_out, r0, x_src)

    def odd_exchange_conv(self, o):
        cfg, nc = self.cfg, self.nc
        SP, DVE, ACT, POOL = self.SP, self.DVE, self.ACT, self.POOL
        TT, NS, KD, HC, HA, SK = cfg.TT, cfg.NS, cfg.KD, cfg.HC, cfg.HA, cfg.SK
        Abf, Bbf = self.Abf(), self.Bbf()
        self.barrier()
        self.pool_sync()
        th, hv = self.exchange_tails(cfg.TLC, HC, self.cc_tails)
        PE = self.PE
        wcol = lambda k, j: cfg.c_convcw + (o * KD + k) * SK + j
        base = HA - HC
        extb = [x_.bitcast(BF16)[:, 0:HA + TT] for x_ in self.ext]
        extring = Ring(4)
        ldring = Ring(4)
        psring = Ring(4)
        dg_rel = [[], []]
        tiles = [(k, i) for k in range(KD) for i in range(NS)]
        tld = {}

        def load(n):
            if n >= len(tiles) or n in tld:
                return
            k, i = tiles[n]
            s, rel = ldring.next()
            SP.wait(*rel)
            tld[n] = (s, self.dma(SP, self.ldsem[s], self.scr[s][:, 0:TT],
                                  self.cuT[k * 128:(k + 1) * 128, i * TT:(i + 1) * TT]))

        for n, (k, i) in enumerate(tiles):
            for q in range(n, n + 3):
                load(q)
            db = k % 2
            if i == 0:
                DVE.wait(th, *dg_rel[db])
                td = None
                for j in range(SK):
                    td = DVE.tick(nc.vector.tensor_scalar(out=self.dg[:, db * SK + j, :], in0=self.ident_b[:],
                                                          scalar1=self.cst(wcol(k, j)), scalar2=None, op0=ALU.mult))
                tdg = td
            s, tl = tld[n]
            xb, relx = extring.next()
            ACT.wait(th, tl, *relx)
            ACT.tick(nc.scalar.copy(out=extb[xb][:, base:HA], in_=hv[:, k, i, :]))
            t2 = ACT.tick(nc.scalar.copy(out=extb[xb][:, HA:HA + TT], in_=self.scr[s][:, 0:TT]))
            ldring.release(s, t2)
            pb, relp = psring.next()
            PE.wait(tdg, t2, *relp)
            ins = None
            for j in range(SK):
                ins = nc.tensor.matmul(self.ps[:, pb, 0:TT], lhsT=self.dg[:, db * SK + j, :],
                                       rhs=extb[xb][:, base + j:base + j + TT], start=(j == 0), stop=(j == SK - 1))
            tpl = PE.tick(ins)
            extring.release(xb, tpl)
            if i == NS - 1:
                dg_rel[db] = [tpl]
            DVE.wait(tpl)
            tg = DVE.tick(nc.vector.tensor_tensor(out=Abf[:, k, i * TT:(i + 1) * TT], in0=self.ps[:, pb, 0:TT],
                                                  in1=Bbf[:, k, i * TT:(i + 1) * TT], op=ALU.mult))
            psring.release(pb, tg)
        self.barrier()

    def xattn_layer(self, l):
        cfg = self.cfg
        D, TT, KD, MEM, MC, WBW = cfg.D, cfg.TT, cfg.KD, cfg.MEM, cfg.MC, cfg.WBW
        items = self.items
        Abf, Bbf = self.Abf(), self.Bbf()
        items.append(FJob(lambda: self.norm_phase(self.xres, cfg.gain_col("xattn", l))))
        self.gemm_to_sbuf(Abf, self.w_xq, l * D, 0, D, lambda m, tt: Bbf[:, m, tt * TT:(tt + 1) * TT], "copy")
        n1 = KD * MEM
        memn = self.A[:, 0:n1].rearrange("p (k t) -> p k t", k=KD)
        kmem = self.A[:, n1:2 * n1].rearrange("p (k t) -> p k t", k=KD)
        vmem = self.A[:, 2 * n1:3 * n1].rearrange("p (c d) -> p c d", c=MC)

        def load_memn():
            self.barrier()
            self.dma(self.SP, self.ldsem[0], memn, self.memnT.rearrange("(k p) t -> p k t", p=128))
            self.barrier()
        items.append(FJob(load_memn))
        self.gemm_to_sbuf(memn, self.w_xk, l * D, 0, D, lambda m, tt: kmem[:, m, :], "copy", tok_tiles=[(0, MEM)])
        self.gemm_tokmajor(memn, MEM, self.w_xv, l * D, 0, D, "sbuf",
                           lambda tb, blk: vmem[:, tb, blk * WBW:(blk + 1) * WBW])
        items.append(FJob(lambda: self.xattn_core(kmem, vmem)))
        self.gemm_rmw(Bbf, self.w_xo, l * D, self.xres)

    def xattn_core(self, kmem, vmem):
        cfg, nc = self.cfg, self.nc
        DVE, ACT, PE = self.DVE, self.ACT, self.PE
        TT, NS, XH, XKC, MC = cfg.TT, cfg.NS, cfg.XH, cfg.XKC, cfg.MC
        Bbf = self.Bbf()
        self.barrier()
        pm = self.phase_mem
        Er = [pm[:, j * (MC * TT // 2):(j + 1) * (MC * TT // 2)].bitcast(BF16).rearrange("p (c t) -> p c t", c=MC)
              for j in range(3)]
        scale = cfg.XHD ** -0.5
        sring = Ring(2)
        ering = Ring(len(Er))
        oring = Ring(2)
        seq = [(h, tt) for h in range(XH) for tt in range(NS)]
        stA = {}
        state = {"zrel": []}

        def stage_a(n):
            h, tt = seq[n]
            sl = slice(tt * TT, (tt + 1) * TT)
            sp, rels = sring.next()
            PE.wait(*rels)
            tps = None
            for mc in range(MC):
                for dc in range(XKC):
                    tps = nc.tensor.matmul(self.ps[:, sp * MC + mc, 0:TT],
                                           lhsT=kmem[:, h * XKC + dc, mc * 128:(mc + 1) * 128],
                                           rhs=Bbf[:, h * XKC + dc, sl], start=(dc == 0), stop=(dc == XKC - 1))
            tps = PE.tick(tps)
            er, rele = ering.next()
            E = Er[er]
            ACT.wait(tps, *rele)
            ta = ACT.tick(nc.scalar.activation(out=E, in_=self.ps[:, sp * MC:(sp + 1) * MC, 0:TT], func=AF.Exp,
                                               scale=scale))
            sring.release(sp, ta)
            stA[n] = (er, ta)

        def stage_b(n):
            h, tt = seq[n]
            sl = slice(tt * TT, (tt + 1) * TT)
            er, ta = stA.pop(n)
            E = Er[er]
            PE.wait(ta, *state["zrel"])
            tz = None
            for mc in range(MC):
                tz = nc.tensor.matmul(self.ps[:, 4, 0:TT], lhsT=self.ones_b[:], rhs=E[:, mc, :],
                                      start=(mc == 0), stop=(mc == MC - 1))
            tz = PE.tick(tz)
            rz = self.scr[er][:, 0:TT]
            DVE.wait(tz)
            trz = DVE.tick(nc.vector.reciprocal(out=rz, in_=self.ps[:, 4, 0:TT]))
            state["zrel"] = [trz]
            tpo = td = None
            for ec in range(XKC):
                ob, relo = oring.next()
                PE.wait(*relo)
                for mc in range(MC):
                    tpo = nc.tensor.matmul(self.ps[:, 5 + ob, 0:TT],
                                           lhsT=vmem[:, mc, (h * XKC + ec) * 128:(h * XKC + ec + 1) * 128],
                                           rhs=E[:, mc, :], start=(mc == 0), stop=(mc == MC - 1))
                tpo = PE.tick(tpo)
                DVE.wait(tpo, trz)
                td = DVE.tick(nc.vector.tensor_tensor(out=Bbf[:, h * XKC + ec, sl], in0=self.ps[:, 5 + ob, 0:TT],
                                                      in1=rz, op=ALU.mult))
                oring.release(ob, td)
            ering.release(er, tpo, td)

        N = len(seq)
        LOOK = 2
        for n in range(N + LOOK):
            if n < N:
                stage_a(n)
            if n - LOOK >= 0:
                stage_b(n - LOOK)
        self.barrier()

    def mlp_layer(self, l):
        cfg = self.cfg
        D, TT = cfg.D, cfg.TT
        Abf, Bbf = self.Abf(), self.Bbf()
        self.items.append(FJob(lambda: self.norm_phase(self.xres, cfg.gain_col("mlp", l))))
        for g in range(cfg.NG):
            self.gemm_to_sbuf(Abf, self.w_m1, l * D, g * D, D, lambda m, tt: Bbf[:, m, tt * TT:(tt + 1) * TT], "relu2")
            self.gemm_rmw(Bbf, self.w_m2, l * cfg.DFF + g * D, self.xres)

    def dump_x(self):
        self.barrier()
        n = self.cfg.D // 128
        for k in range(n):
            self.dma(self.SP, self.outsem, self.yT[k * 128:(k + 1) * 128, :], self.xres[k * 128:(k + 1) * 128, :])

    def finish(self):
        self.barrier()
        self.POOL.wait(*self.pending_bar)


def pack_consts(cfg, p, j):
    c = np.zeros((128, cfg.NCONST), np.float32)
    KD, KA = cfg.KD, cfg.KA

    def put_vec(col, v):
        v = np.asarray(v, np.float32)
        n = v.shape[0] // 128
        c[:, col:col + n] = v.reshape(n, 128).T

    put_vec(cfg.gain_col("mem"), p["mem_norm_g"])
    put_vec(cfg.gain_col("final"), p["final_norm_g"])
    for l in range(cfg.depth):
        put_vec(cfg.gain_col("mix", l), p["mix_norm_g"][l])
        put_vec(cfg.gain_col("xattn", l), p["xattn_norm_g"][l])
        put_vec(cfg.gain_col("mlp", l), p["mlp_norm_g"][l])
    for e in range(cfg.NEVEN):
        w = np.asarray(p["conv_a_w"][e], np.float32)
        blk = w.T.reshape(KA, 128, cfg.CK).transpose(1, 0, 2).reshape(128, KA * cfg.CK)
        c[:, cfg.c_convaw + e * KA * cfg.CK: cfg.c_convaw + (e + 1) * KA * cfg.CK] = blk
        put_vec(cfg.c_convab + e * KA, p["conv_a_b"][e])
        put_vec(cfg.c_lnag + e * KA, p["ln_a_g"][e])
        put_vec(cfg.c_lnab + e * KA, p["ln_a_b"][e])
        c[:, cfg.c_subg + e] = np.asarray(p["subln_g"][e], np.float32)
        for n, name in enumerate(("lambda_q1", "lambda_k1", "lambda_q2", "lambda_k2")):
            c[0:64, cfg.c_lam + 4 * e + n] = np.asarray(p[name][e], np.float32)
    for o in range(cfg.NODD):
        w = np.asarray(p["conv_c_w"][o], np.float32)
        blk = w.T.reshape(KD, 128, cfg.SK).transpose(1, 0, 2).reshape(128, KD * cfg.SK)
        c[:, cfg.c_convcw + o * KD * cfg.SK: cfg.c_convcw + (o + 1) * KD * cfg.SK] = blk
    sel = np.zeros(4, np.float32)
    if j > 0:
        sel[j - 1] = 1.0
    else:
        sel[3] = 1.0
    c[:, cfg.c_sel:cfg.c_sel + 4] = sel[None, :]
    return c


def make_mask(cfg, j):
    TT, SUB = cfg.TT, cfg.SUB
    m = np.zeros((128, 4 * SUB, TT), np.float32)
    pidx = np.arange(128)[:, None]
    cidx = np.arange(TT)[None, :]
    for r in range(4):
        for s in range(SUB):
            m[:, r * SUB + s, :] = ((r * TT + s * 128 + pidx) <= (j * TT + cidx)).astype(np.float32)
    m = np.concatenate([m.reshape(128, 4 * SUB * TT), np.eye(128, dtype=np.float32)], axis=1)
    return m.astype(ml_dtypes.bfloat16)


_PROG_CACHE = {}


def run(cfg, inputs, debug_stop=None):
    key = (cfg.D, cfg.TT, cfg.depth, debug_stop)
    if key not in _PROG_CACHE:
        _PROG_CACHE[key] = Prog(cfg, debug_stop).build()
    nc = _PROG_CACHE[key]
    p = {k: np.asarray(v) for k, v in inputs.items()}
    D, TT, NS, G = cfg.D, cfg.TT, cfg.NS, cfg.G
    x, mem = p["x"], p["mem"]
    shared = {
        "even_w_in": p["even_w_in"].reshape(-1, cfg.EIN), "even_w_out": p["even_w_out"].reshape(-1, D),
        "odd_w_in": p["odd_w_in"].reshape(-1, cfg.OIN), "odd_w_out": p["odd_w_out"].reshape(-1, D),
        "xq_w": p["xq_w"].reshape(-1, D), "xk_w": p["xk_w"].reshape(-1, D),
        "xv_w": p["xv_w"].reshape(-1, D), "xo_w": p["xo_w"].reshape(-1, D),
        "mlp_w1": p["mlp_w1"].reshape(-1, cfg.DFF), "mlp_w2": p["mlp_w2"].reshape(-1, D),
    }
    in_maps = []
    for c in range(cfg.NCORES):
        b, j = divmod(c, G)
        xb = x[b].reshape(NS, G, TT, D)[:, j].reshape(NS * TT, D)
        m = dict(shared)
        m["xT"] = np.ascontiguousarray(xb.T)
        m["memT"] = np.ascontiguousarray(mem[b].T)
        m["consts"] = pack_consts(cfg, p, j)
        m["cmask"] = make_mask(cfg, j)
        in_maps.append(m)
    res = run_bass_kernel_spmd(nc, in_maps, core_ids=list(range(cfg.NCORES)))
    out = np.empty((cfg.NB, cfg.SEQ, D), np.float32)
    ov = out.reshape(cfg.NB, NS, G, TT, D)
    for c in range(cfg.NCORES):
        b, j = divmod(c, G)
        yT = np.asarray(res.results[c]["yT"])
        ov[b, :, j] = yT.T.reshape(NS, TT, D)
    return out


def kernel(**inputs):
    return run(BIG, inputs)
```

```python
import math
from contextlib import ExitStack

import numpy as np
import ml_dtypes

import concourse.bass as bass
import concourse.mybir as mybir
from concourse.bass_utils import run_bass_kernel_spmd

F32 = mybir.dt.float32
BF16 = mybir.dt.bfloat16
AF = mybir.ActivationFunctionType
ALU = mybir.AluOpType
EPS = 1e-6


class Cfg:
    def __init__(self, D=2048, TT=512, MEM=256, WBW=512, depth=4):
        self.D = D
        self.TT = TT
        self.MEM = MEM
        self.WBW = WBW
        self.depth = depth
        self.NS = 4
        self.G = 4
        self.NB = 2
        self.NCORES = self.G * self.NB
        self.KD = D // 128
        self.TOK = self.NS * TT
        self.SEQ = self.G * self.NS * TT
        self.AW = D // 2
        self.BW = D - self.AW
        self.KA = self.AW // 128
        self.NH = self.BW // 128
        self.XH = 4
        self.XHD = D // 4
        self.XKC = self.XHD // 128
        self.DFF = 4 * D
        self.NG = self.DFF // D
        self.SUB = TT // 128
        self.MC = MEM // 128
        self.CK = 31
        self.HA = 30
        self.SK = 3
        self.HC = 2
        self.EIN = 2 * self.AW + 3 * self.BW
        self.OIN = 3 * D
        self.NEVEN = (depth + 1) // 2
        self.NODD = depth // 2
        self.TLA = self.KA * self.NS * self.HA
        self.TLC = self.KD * self.NS * self.HC
        self.TL = max(self.TLA, self.TLC)
        c = 0
        self.c_gain = c; c += 14 * self.KD
        self.c_convaw = c; c += 2 * self.KA * self.CK
        self.c_convab = c; c += 2 * self.KA
        self.c_lnag = c; c += 2 * self.KA
        self.c_lnab = c; c += 2 * self.KA
        self.c_subg = c; c += 2
        self.c_lam = c; c += 2 * 4
        self.c_convcw = c; c += 2 * self.KD * self.SK
        self.c_sel = c; c += 4
        self.NCONST = c

    def gain_col(self, kind, l=0):
        idx = {"mem": 0, "final": 1, "mix": 2, "xattn": 6, "mlp": 10}[kind] + (l if kind in ("mix", "xattn", "mlp") else 0)
        return self.c_gain + idx * self.KD


BIG = Cfg()


class Eng:
    def __init__(self, e, sem, name):
        self.e = e
        self.sem = sem
        self.name = name
        self.n = 0
        self.seen = {}

    def wait(self, *tickets):
        for t in tickets:
            if t is None:
                continue
            if isinstance(t, (list, tuple)) and len(t) > 0 and isinstance(t[0], (list, tuple)):
                self.wait(*t)
                continue
            if isinstance(t, list):
                self.wait(*t)
                continue
            key, sem, val = t
            if self.seen.get(key, 0) >= val:
                continue
            self.e.wait_ge(sem, val)
            self.seen[key] = val

    def tick(self, ins):
        self.n += 1
        ins.then_inc(self.sem, 1)
        return (self.name, self.sem, self.n)

    def last(self):
        return (self.name, self.sem, self.n) if self.n > 0 else None


class DSem:
    def __init__(self, sem, name):
        self.sem = sem
        self.name = name
        self.n = 0

    def issue(self, ins):
        self.n += 16
        ins.then_inc(self.sem, 16)
        return (self.name, self.sem, self.n)

    def last(self):
        return (self.name, self.sem, self.n) if self.n > 0 else None


class Ring:
    def __init__(self, n):
        self.n = n
        self.i = 0
        self.rel = [[] for _ in range(n)]

    def next(self):
        k = self.i % self.n
        self.i += 1
        rel = self.rel[k]
        self.rel[k] = []
        return k, rel

    def release(self, k, *tickets):
        self.rel[k].extend([t for t in tickets if t is not None])


class WJob:
    def __init__(self, pieces, fn):
        self.pieces = pieces
        self.fn = fn


class FJob:
    def __init__(self, fn):
        self.fn = fn


class Prog:
    def __init__(self, cfg, debug_stop=None):
        self.cfg = cfg
        self.debug_stop = debug_stop
        self.items = []

    def build(self):
        cfg = self.cfg
        nc = bass.Bass("TRN2", target_bir_lowering=False)
        self.nc = nc
        D, TOK, TT, KD, MEM = cfg.D, cfg.TOK, cfg.TT, cfg.KD, cfg.MEM
        dt = nc.dram_tensor
        self.xT_in = dt("xT", [D, TOK], F32, kind="ExternalInput").ap()
        self.memT_in = dt("memT", [D, MEM], F32, kind="ExternalInput").ap()
        self.consts_in = dt("consts", [128, cfg.NCONST], F32, kind="ExternalInput").ap()
        self.cmask_in = dt("cmask", [128, 4 * cfg.SUB * TT + 128], BF16, kind="ExternalInput").ap()
        ne, no, dp = cfg.NEVEN, cfg.NODD, cfg.depth
        self.w_even_in = dt("even_w_in", [ne * D, cfg.EIN], F32, kind="ExternalInput").ap()
        self.w_even_out = dt("even_w_out", [ne * D, D], F32, kind="ExternalInput").ap()
        self.w_odd_in = dt("odd_w_in", [max(no, 1) * D, cfg.OIN], F32, kind="ExternalInput").ap()
        self.w_odd_out = dt("odd_w_out", [max(no, 1) * D, D], F32, kind="ExternalInput").ap()
        self.w_xq = dt("xq_w", [dp * D, D], F32, kind="ExternalInput").ap()
        self.w_xk = dt("xk_w", [dp * D, D], F32, kind="ExternalInput").ap()
        self.w_xv = dt("xv_w", [dp * D, D], F32, kind="ExternalInput").ap()
        self.w_xo = dt("xo_w", [dp * D, D], F32, kind="ExternalInput").ap()
        self.w_m1 = dt("mlp_w1", [dp * D, cfg.DFF], F32, kind="ExternalInput").ap()
        self.w_m2 = dt("mlp_w2", [dp * cfg.DFF, D], F32, kind="ExternalInput").ap()
        self.yT = dt("yT", [D, TOK], F32, kind="ExternalOutput").ap()
        self.xres = dt("xres", [D, TOK], F32).ap()
        self.qT = dt("qT_s", [cfg.BW, TOK], BF16).ap()
        self.cuT = dt("cuT_s", [D, TOK], F32).ap()
        self.memnT = dt("memnT_s", [D, MEM], BF16).ap()
        self.HPC = 2 if cfg.NH >= 2 else 1
        self.KR = 128 * self.HPC
        self.nKc = cfg.NH // self.HPC
        self.agK_in = [dt(f"agK_in{c}", [self.KR, TOK], BF16).ap() for c in range(self.nKc)]
        self.agK_out = [dt(f"agK_out{c}", [cfg.G * self.KR, TOK], BF16).ap() for c in range(self.nKc)]
        self.agV_in = [dt(f"agV_in{c}", [TOK, self.KR], BF16).ap() for c in range(self.nKc)]
        self.agV_out = [dt(f"agV_out{c}", [cfg.G * TOK, self.KR], BF16).ap() for c in range(self.nKc)]
        self.agT_in = dt("agT_in", [128, cfg.TL], F32).ap()
        self.agT_out = dt("agT_out", [cfg.G * 128, cfg.TL], F32).ap()

        with ExitStack() as es:
            sb = lambda name, shape, dty: es.enter_context(nc.sbuf_tensor(name, shape, dty))
            self.A = sb("bufA", [128, KD * TOK], BF16)
            self.B = sb("bufB", [128, KD * TOK], BF16)
            self.NWS = 2
            self.W = [sb(f"wslot{i}", [128, KD, cfg.WBW], BF16) for i in range(self.NWS)]
            self.consts = sb("consts_sb", [128, cfg.NCONST], F32)
            self.ones_f = sb("ones_f", [128, 128], F32)
            self.ones_b = sb("ones_b", [128, 128], BF16)
            self.ident_b = sb("ident_b", [128, 128], BF16)
            self.lamv = sb("lamv", [128, 8], F32)
            self.subg = sb("subg", [128, 2], F32)
            self.nscr = 10
            self.scrT = sb("scrT", [128, self.nscr, TT], F32)
            self.scr = [self.scrT[:, i, :] for i in range(self.nscr)]
            self.epsv = sb("epsv", [128, 1], F32)
            self.tailbuf = sb("tailbuf", [128, cfg.TLC], F32)
            r0w = max(2 * cfg.TL, cfg.CK * 128 // 2)
            n_conv = r0w + cfg.TL + 4 * (cfg.HA + TT)
            n_attn = TOK + 4 * TT
            n_x = 3 * cfg.MC * TT // 2
            self.phase_mem = sb("phase_mem", [128, max(n_conv, n_attn, n_x)], F32)
            pm = self.phase_mem
            self.tails2 = [pm[:, i * cfg.TL:(i + 1) * cfg.TL] for i in range(2)]
            self.halo = pm[:, r0w:r0w + cfg.TL]
            self.dg = pm[:, 0:cfg.CK * 128 // 2].bitcast(BF16).rearrange("p (j m) -> p j m", m=128)
            e0 = r0w + cfg.TL
            self.ext = [pm[:, e0 + i * (cfg.HA + TT):e0 + (i + 1) * (cfg.HA + TT)] for i in range(4)]
            n_am = 3 * cfg.SEQ + 4 * cfg.SUB * TT
            if n_am <= KD * TOK:
                self.attn_mem = self.A
            else:
                self.attn_mem = sb("attn_mem", [128, n_am], BF16)
            self.ps = es.enter_context(nc.psum_tensor("ps", [128, 8, 512], F32))
            sem = lambda name: es.enter_context(nc.semaphore(name))
            self.PE = Eng(nc.tensor, sem("s_pe"), "pe")
            self.ACT = Eng(nc.scalar, sem("s_act"), "act")
            self.DVE = Eng(nc.vector, sem("s_dve"), "dve")
            self.POOL = Eng(nc.gpsimd, sem("s_pool"), "pool")
            self.SP = Eng(nc.sync, sem("s_sp"), "sp")
            self.wsem = [DSem(sem(f"s_w{i}"), f"w{i}") for i in range(self.NWS)]
            self.ldsem = [DSem(sem(f"s_ld{i}"), f"ld{i}") for i in range(8)]
            self.stsem = [DSem(sem(f"s_st{i}"), f"st{i}") for i in range(8)]
            self.ccsem = sem("s_cc")
            self.ccn = 0
            self.misc_sem = DSem(sem("s_misc"), "misc")
            self.outsem = DSem(sem("s_out"), "out")
            self.all_dsems = self.ldsem + self.stsem + [self.misc_sem, self.outsem]
            self.pending_bar = []
            self.emit_program()
        return nc

    def Abf(self):
        return self.A[:].rearrange("p (k t) -> p k t", k=self.cfg.KD)

    def Bbf(self):
        return self.B[:].rearrange("p (k t) -> p k t", k=self.cfg.KD)

    def Af32(self):
        return self.A[:].bitcast(F32).rearrange("p (k t) -> p k t", k=self.cfg.KA)

    def Bf32(self):
        return self.B[:].bitcast(F32).rearrange("p (k t) -> p k t", k=self.cfg.KA)

    def cst(self, col, n=1):
        return self.consts[:, col:col + n]

    def barrier(self, engines=None):
        ts = [e.last() for e in (self.PE, self.ACT, self.DVE, self.POOL)]
        ts += [d.last() for d in self.all_dsems]
        if self.ccn > 0:
            ts.append(("cc", self.ccsem, self.ccn))
        ts = [t for t in ts if t is not None]
        for e in (engines or (self.PE, self.ACT, self.DVE, self.SP)):
            e.wait(*ts)
        self.pending_bar = ts

    def pool_sync(self):
        self.POOL.wait(*self.pending_bar)

    def dma(self, q, dsem, out, in_):
        ins = q.e.dma_start(out=out, in_=in_)
        return dsem.issue(ins)

    def emit_program(self):
        cfg = self.cfg
        items = self.items
        items.append(FJob(self.setup))
        x_src = self.xT_in
        stop = False
        for l in range(cfg.depth):
            subs = []
            if l % 2 == 0:
                self.even_layer(l, x_src)
            else:
                self.odd_layer(l, x_src)
            x_src = self.xres
            if self.debug_stop == (l, 0):
                stop = True
                break
            self.xattn_layer(l)
            if self.debug_stop == (l, 1):
                stop = True
                break
            self.mlp_layer(l)
            if self.debug_stop == (l, 2):
                stop = True
                break
        if stop:
            items.append(FJob(self.dump_x))
        else:
            items.append(FJob(lambda: self.norm_phase(x_src, cfg.gain_col("final"), final=True)))
        items.append(FJob(self.finish))
        self.execute()

    def execute(self):
        wjobs = [it for it in self.items if isinstance(it, WJob)]
        for j, w in enumerate(wjobs):
            w.idx = j
        R = self.NWS
        ready = {}
        rel = {}

        def issue(j):
            if j >= len(wjobs):
                return
            slot = j % R
            if j - R >= 0:
                self.POOL.wait(*rel[j - R])
            t = None
            for (src, off, n) in wjobs[j].pieces:
                t = self.dma(self.POOL, self.wsem[slot], self.W[slot][:, :, off:off + n], src)
            ready[j] = t

        for j in range(min(R, len(wjobs))):
            issue(j)
        for it in self.items:
            if isinstance(it, FJob):
                it.fn()
            else:
                j = it.idx
                r = it.fn(self.W[j % R], ready[j])
                rel[j] = r if isinstance(r, list) else [r]
                issue(j + R)

    def wview(self, w2d, row0, c0, n):
        D = self.cfg.D
        return w2d[row0:row0 + D, c0:c0 + n].rearrange("(k p) n -> p k n", p=128)

    def setup(self):
        cfg, nc = self.cfg, self.nc
        SP, DVE, ACT, PE = self.SP, self.DVE, self.ACT, self.PE
        self.dma(SP, self.misc_sem, self.consts[:], self.consts_in[:, :])
        t_c = self.dma(SP, self.misc_sem, self.ident_b[:], self.cmask_in[:, 4 * cfg.SUB * cfg.TT:4 * cfg.SUB * cfg.TT + 128])
        t1 = DVE.tick(nc.vector.memset(self.ones_f[:], 1.0))
        t2 = DVE.tick(nc.vector.memset(self.ones_b[:], 1.0))
        t3 = DVE.tick(nc.vector.memset(self.epsv[:], EPS))
        DVE.wait(t_c)
        for e in range(cfg.NEVEN):
            l = 2 * e
            lam_init = 0.8 - 0.6 * math.exp(-0.3 * l)
            c0 = cfg.c_lam + 4 * e
            prod = self.scr[0][0:64, 0:2]
            ta = DVE.tick(nc.vector.tensor_tensor(out=self.scr[0][0:64, 0:1], in0=self.consts[0:64, c0:c0 + 1],
                                                  in1=self.consts[0:64, c0 + 1:c0 + 2], op=ALU.mult))
            tb = DVE.tick(nc.vector.tensor_tensor(out=self.scr[0][0:64, 1:2], in0=self.consts[0:64, c0 + 2:c0 + 3],
                                                  in1=self.consts[0:64, c0 + 3:c0 + 4], op=ALU.mult))
            PE.wait(ta, tb, t1)
            tp = PE.tick(nc.tensor.matmul(self.ps[:, 0, 0:2], lhsT=self.ones_f[0:64, :], rhs=prod, start=True, stop=True))
            ACT.wait(tp)
            te = ACT.tick(nc.scalar.activation(out=self.scr[1][:, 0:2], in_=self.ps[:, 0, 0:2], func=AF.Exp))
            DVE.wait(te)
            td = DVE.tick(nc.vector.tensor_tensor(out=self.scr[1][:, 2:3], in0=self.scr[1][:, 1:2],
                                                  in1=self.scr[1][:, 0:1], op=ALU.subtract))
            DVE.wait(td)
            tl = DVE.tick(nc.vector.tensor_scalar(out=self.lamv[:, e:e + 1], in0=self.scr[1][:, 2:3],
                                                  scalar1=-lam_init, scalar2=None, op0=ALU.add))
            tg = DVE.tick(nc.vector.tensor_scalar(out=self.subg[:, e:e + 1], in0=self.cst(cfg.c_subg + e),
                                                  scalar1=1.0 - lam_init, scalar2=None, op0=ALU.mult))
            PE.wait(tl)
            ACT.wait(tl)
        self.barrier()
        self.norm_phase(self.memT_in, cfg.gain_col("mem"), mem=True)

    def norm_phase(self, src, gcol, final=False, mem=False):
        cfg, nc = self.cfg, self.nc
        SP, DVE, ACT, PE = self.SP, self.DVE, self.ACT, self.PE
        KD, TT, D = cfg.KD, cfg.TT, cfg.D
        self.barrier()
        ntok = cfg.MEM if mem else cfg.TOK
        W = min(TT, ntok)
        ntile = ntok // W
        srcv = src.rearrange("(k p) t -> p k t", p=128)
        xs_all = self.B[:].bitcast(F32).rearrange("p (b k t) -> p b k t", b=2, k=KD)
        xring = Ring(2)
        psring = Ring(2)
        sqring = Ring(2)
        oring = Ring(2)
        rring = Ring(2)
        Abf = self.Abf()
        for ti in range(ntile):
            off = ti * W
            b, relx = xring.next()
            xs = xs_all[:, b, :, 0:W]
            SP.wait(*relx)
            tld = self.dma(SP, self.ldsem[b], xs, srcv[:, :, off:off + W])
            pb, relp = psring.next()
            pst = self.ps[:, pb, 0:W]
            PE.wait(*relp)
            tpe = None
            for k in range(KD):
                sb_, rels = sqring.next()
                sq = self.scr[sb_][:, 0:W].bitcast(BF16)[:, 0:W]
                ACT.wait(tld, *rels)
                ta = ACT.tick(nc.scalar.activation(out=sq, in_=xs[:, k, :], func=AF.Square))
                PE.wait(ta)
                tpe = PE.tick(nc.tensor.matmul(pst, lhsT=self.ones_b[:], rhs=sq, start=(k == 0), stop=(k == KD - 1)))
                sqring.release(sb_, tpe)
            rb, relr = rring.next()
            sd = self.scr[2 + 4 * rb][:, 0:W]
            rstd = self.scr[3 + 4 * rb][:, 0:W]
            ACT.wait(tpe, *relr)
            ts = ACT.tick(nc.scalar.activation(out=sd, in_=pst, func=AF.Sqrt, bias=self.eps_col(), scale=1.0 / D))
            psring.release(pb, ts)
            DVE.wait(ts)
            tr = DVE.tick(nc.vector.reciprocal(out=rstd, in_=sd))
            DVE.wait(tr)
            tlast = None
            for k in range(KD):
                if final or mem:
                    ob, relo = oring.next()
                    DVE.wait(*relo)
                    if final:
                        dst = self.scr[4 + ob][:, 0:W]
                    else:
                        dst = self.scr[4 + ob][:, 0:W].bitcast(BF16)[:, 0:W]
                else:
                    dst = Abf[:, k, off:off + W]
                tlast = DVE.tick(nc.vector.scalar_tensor_tensor(out=dst, in0=xs[:, k, :], scalar=self.cst(gcol + k),
                                                                in1=rstd, op0=ALU.mult, op1=ALU.mult))
                if final:
                    SP.wait(tlast)
                    tst = self.dma(SP, self.stsem[ob], self.yT[k * 128:(k + 1) * 128, off:off + W], dst)
                    oring.release(ob, tst)
                elif mem:
                    SP.wait(tlast)
                    tst = self.dma(SP, self.stsem[ob], self.memnT[k * 128:(k + 1) * 128, off:off + W], dst)
                    oring.release(ob, tst)
            xring.release(b, tlast)
            rring.release(rb, tlast)
        self.barrier()

    def mm_group(self, pst, wslot, mloc, in_buf, t0, w, ready, extra_wait=()):
        nc, PE, KD = self.nc, self.PE, self.cfg.KD
        PE.wait(ready, *extra_wait)
        ins = None
        for k in range(KD):
            ins = nc.tensor.matmul(pst, lhsT=wslot[:, k, mloc * 128:(mloc + 1) * 128], rhs=in_buf[:, k, t0:t0 + w],
                                   start=(k == 0), stop=(k == KD - 1))
        return PE.tick(ins)

    def gemm_rmw(self, in_buf, w2d, row0, x_src):
        cfg, nc = self.cfg, self.nc
        SP, DVE, PE = self.SP, self.DVE, self.PE
        D, TT, NS, WBW = cfg.D, cfg.TT, cfg.NS, cfg.WBW
        nblk = D // WBW
        mper = WBW // 128
        tiles = [(blk, ml, tt) for blk in range(nblk) for ml in range(mper) for tt in range(NS)]
        st = {"ld": 0, "tld": {}, "inring": Ring(3), "outring": Ring(2), "psring": Ring(8), "slotmap": {}}
        PRE = 2

        def load(n):
            if n >= len(tiles) or n in st["tld"]:
                return
            blk, ml, tt = tiles[n]
            m = blk * mper + ml
            s, rel = st["inring"].next()
            st["slotmap"][n] = s
            SP.wait(*rel)
            st["tld"][n] = self.dma(SP, self.ldsem[s], self.scr[s][:, 0:TT],
                                    x_src[m * 128:(m + 1) * 128, tt * TT:(tt + 1) * TT])

        def job(blk):
            def fn(wslot, ready):
                tp = None
                for n, (b2, ml, tt) in enumerate(tiles):
                    if b2 != blk:
                        continue
                    m = blk * mper + ml
                    for q in range(n, n + PRE + 1):
                        load(q)
                    pb, relp = st["psring"].next()
                    pst = self.ps[:, pb, 0:TT]
                    tp = self.mm_group(pst, wslot, ml, in_buf, tt * TT, TT, ready, relp)
                    ob, relo = st["outring"].next()
                    xo = self.scr[3 + ob][:, 0:TT]
                    s = st["slotmap"][n]
                    DVE.wait(tp, st["tld"][n], *relo)
                    td = DVE.tick(nc.vector.tensor_tensor(out=xo, in0=pst, in1=self.scr[s][:, 0:TT], op=ALU.add))
                    st["psring"].release(pb, td)
                    st["inring"].release(s, td)
                    SP.wait(td)
                    tst = self.dma(SP, self.stsem[ob], self.xres[m * 128:(m + 1) * 128, tt * TT:(tt + 1) * TT], xo)
                    st["outring"].release(ob, tst)
                return tp
            return fn

        self.items.append(FJob(self.barrier))
        for blk in range(nblk):
            self.items.append(WJob([(self.wview(w2d, row0, blk * WBW, WBW), 0, WBW)], job(blk)))
        self.items.append(FJob(self.barrier))

    def gemm_to_sbuf(self, in_buf, w2d, row0, col0, ncols, dst_fn, mode, tok_tiles=None):
        cfg, nc = self.cfg, self.nc
        DVE, ACT = self.DVE, self.ACT
        TT, NS, WBW = cfg.TT, cfg.NS, cfg.WBW
        nblk = ncols // WBW
        mper = WBW // 128
        st = {"psring": Ring(8), "tring": Ring(2), "cnt": 0}

        def job(blk):
            def fn(wslot, ready):
                tp = None
                for ml in range(mper):
                    m = blk * mper + ml
                    for tt, (tk0, tkw) in enumerate(tok_tiles or [(t_ * TT, TT) for t_ in range(NS)]):
                        pb, relp = st["psring"].next()
                        pst = self.ps[:, pb, 0:tkw]
                        tp = self.mm_group(pst, wslot, ml, in_buf, tk0, tkw, ready, relp)
                        dst = dst_fn(m, tt)
                        if mode == "copy":
                            st["cnt"] += 1
                            if st["cnt"] % 2 == 0:
                                ACT.wait(tp)
                                tc_ = ACT.tick(nc.scalar.copy(out=dst, in_=pst))
                            else:
                                DVE.wait(tp)
                                tc_ = DVE.tick(nc.vector.tensor_copy(out=dst, in_=pst))
                            st["psring"].release(pb, tc_)
                        else:
                            tb, relt = st["tring"].next()
                            tmp = self.scr[tb][:, 0:TT]
                            ACT.wait(tp, *relt)
                            ta = ACT.tick(nc.scalar.activation(out=tmp, in_=pst, func=AF.Relu))
                            st["psring"].release(pb, ta)
                            DVE.wait(ta)
                            td = DVE.tick(nc.vector.tensor_tensor(out=dst, in0=tmp, in1=tmp, op=ALU.mult))
                            st["tring"].release(tb, td)
                return tp
            return fn

        self.items.append(FJob(self.barrier))
        for blk in range(nblk):
            self.items.append(WJob([(self.wview(w2d, row0, col0 + blk * WBW, WBW), 0, WBW)], job(blk)))
        self.items.append(FJob(self.barrier))

    def gemm_to_dram(self, in_buf, w2d, row0, col0, ncols, dram_dst):
        cfg, nc = self.cfg, self.nc
        DVE, ACT, SP = self.DVE, self.ACT, self.SP
        TT, NS, WBW = cfg.TT, cfg.NS, cfg.WBW
        nblk = ncols // WBW
        mper = WBW // 128
        st = {"psring": Ring(8), "oring": Ring(4), "cnt": 0}

        def job(blk):
            def fn(wslot, ready):
                tp = None
                for ml in range(mper):
                    m = blk * mper + ml
                    for tt in range(NS):
                        pb, relp = st["psring"].next()
                        pst = self.ps[:, pb, 0:TT]
                        tp = self.mm_group(pst, wslot, ml, in_buf, tt * TT, TT, ready, relp)
                        ob, relo = st["oring"].next()
                        dst = self.scr[ob][:, 0:TT].bitcast(BF16)[:, 0:TT]
                        st["cnt"] += 1
                        if st["cnt"] % 2 == 0:
                            ACT.wait(tp, *relo)
                            tc_ = ACT.tick(nc.scalar.copy(out=dst, in_=pst))
                        else:
                            DVE.wait(tp, *relo)
                            tc_ = DVE.tick(nc.vector.tensor_copy(out=dst, in_=pst))
                        st["psring"].release(pb, tc_)
                        SP.wait(tc_)
                        tst = self.dma(SP, self.stsem[ob], dram_dst(m, tt), dst)
                        st["oring"].release(ob, tst)
                return tp
            return fn

        self.items.append(FJob(self.barrier))
        for blk in range(nblk):
            self.items.append(WJob([(self.wview(w2d, row0, col0 + blk * WBW, WBW), 0, WBW)], job(blk)))
        self.items.append(FJob(self.barrier))

    def gemm_tokmajor(self, in_buf, ntok, w2d, row0, col0, ncols, dst_kind, dst):
        cfg, nc = self.cfg, self.nc
        DVE, ACT, SP, PE = self.DVE, self.ACT, self.SP, self.PE
        KD, WBW = cfg.KD, cfg.WBW
        nblk = ncols // WBW
        ntb = ntok // 128
        st = {"psring": Ring(8), "oring": Ring(2), "cnt": 0}

        def job(blk):
            def fn(wslot, ready):
                tp = None
                for tb in range(ntb):
                    pb, relp = st["psring"].next()
                    pst = self.ps[:, pb, 0:WBW]
                    PE.wait(ready, *relp)
                    ins = None
                    for k in range(KD):
                        ins = nc.tensor.matmul(pst, lhsT=in_buf[:, k, tb * 128:(tb + 1) * 128], rhs=wslot[:, k, 0:WBW],
                                               start=(k == 0), stop=(k == KD - 1))
                    tp = PE.tick(ins)
                    st["cnt"] += 1
                    eng, fnc = (ACT, nc.scalar.copy) if st["cnt"] % 2 == 0 else (DVE, nc.vector.tensor_copy)
                    if dst_kind == "dram":
                        ob, relo = st["oring"].next()
                        o = self.scr[ob][:, 0:WBW // 2].bitcast(BF16)[:, 0:WBW] if WBW // 2 <= cfg.TT else None
                        eng.wait(tp, *relo)
                        tc_ = eng.tick(fnc(out=o, in_=pst))
                        st["psring"].release(pb, tc_)
                        SP.wait(tc_)
                        for (dap, c0, c1) in dst(tb, blk):
                            tst = self.dma(SP, self.stsem[ob], dap, o[:, c0:c1])
                        st["oring"].release(ob, tst)
                    else:
                        eng.wait(tp)
                        tc_ = eng.tick(fnc(out=dst(tb, blk), in_=pst))
                        st["psring"].release(pb, tc_)
                return tp
            return fn

        self.items.append(FJob(self.barrier))
        for blk in range(nblk):
            self.items.append(WJob([(self.wview(w2d, row0, col0 + blk * WBW, WBW), 0, WBW)], job(blk)))
        self.items.append(FJob(self.barrier))

    def gemm_gated(self, in_buf, w2d, row0, colA, colB, ncols, kind, dst):
        cfg, nc = self.cfg, self.nc
        DVE, ACT, SP = self.DVE, self.ACT, self.SP
        TT, NS, WBW = cfg.TT, cfg.NS, cfg.WBW
        half = WBW // 2
        nblk = ncols // half
        mper = half // 128
        st = {"psring": Ring(4), "tring": Ring(2), "oring": Ring(2)}
        tb_view = self.tailbuf[:, 0:cfg.TLC].rearrange("p (k i t) -> p k i t", k=cfg.KD, i=NS)

        def job(blk):
            def fn(wslot, ready):
                tp = None
                for ml in range(mper):
                    m = blk * mper + ml
                    for tt in range(NS):
                        pb, relp = st["psring"].next()
                        pA = self.ps[:, 2 * pb, 0:TT]
                        pB = self.ps[:, 2 * pb + 1, 0:TT]
                        tpa = self.mm_group(pA, wslot, ml, in_buf, tt * TT, TT, ready, relp)
                        tp = self.mm_group(pB, wslot, mper + ml, in_buf, tt * TT, TT, ready)
                        tb, relt = st["tring"].next()
                        tmp = self.scr[tb][:, 0:TT]
                        ACT.wait(tp, *relt)
                        if kind == "glu":
                            ta = ACT.tick(nc.scalar.activation(out=tmp, in_=pB, func=AF.Sigmoid))
                        else:
                            ta = ACT.tick(nc.scalar.copy(out=tmp, in_=pB))
                        DVE.wait(ta, tpa)
                        if kind == "glu":
                            td = DVE.tick(nc.vector.tensor_tensor(out=dst(m, tt), in0=pA, in1=tmp, op=ALU.mult))
                            st["tring"].release(tb, td)
                            st["psring"].release(pb, td)
                        else:
                            ob, relo = st["oring"].next()
                            o = self.scr[2 + ob][:, 0:TT]
                            DVE.wait(*relo)
                            td = DVE.tick(nc.vector.tensor_tensor(out=o, in0=pA, in1=tmp, op=ALU.mult))
                            st["tring"].release(tb, td)
                            st["psring"].release(pb, td)
                            DVE.wait(td)
                            tt2 = DVE.tick(nc.vector.tensor_copy(out=tb_view[:, m, tt, :], in_=o[:, TT - cfg.HC:TT]))
                            SP.wait(td)
                            tst = self.dma(SP, self.stsem[ob], self.cuT[m * 128:(m + 1) * 128, tt * TT:(tt + 1) * TT], o)
                            st["oring"].release(ob, tst, tt2)
                return tp
            return fn

        self.items.append(FJob(self.barrier))
        for blk in range(nblk):
            pieces = [(self.wview(w2d, row0, colA + blk * half, half), 0, half),
                      (self.wview(w2d, row0, colB + blk * half, half), half, half)]
            self.items.append(WJob(pieces, job(blk)))
        self.items.append(FJob(self.barrier))

    def allgather(self, src, dst, wait_tickets):
        nc, POOL, cfg = self.nc, self.POOL, self.cfg
        POOL.wait(*wait_tickets)
        groups = [[b * cfg.G + r for r in range(cfg.G)] for b in range(cfg.NB)]
        ins = nc.gpsimd.collective_compute("AllGather", ALU.bypass, replica_groups=groups,
                                           ins=[src], outs=[dst])
        ins.then_inc(self.ccsem, 1)
        self.ccn += 1
        return ("cc", self.ccsem, self.ccn)

    def even_layer(self, l, x_src):
        cfg = self.cfg
        e = l // 2
        D = cfg.D
        items = self.items
        items.append(FJob(lambda: self.norm_phase(x_src, cfg.gain_col("mix", l))))
        Abf, Bbf, Bf32 = self.Abf(), self.Bbf(), self.Bf32()
        TT = cfg.TT
        w_in = self.w_even_in
        r0 = e * D
        self.gemm_gated(Abf, w_in, r0, 0, cfg.AW, cfg.AW, "glu", lambda m, tt: Bf32[:, m, tt * TT:(tt + 1) * TT])
        items.append(FJob(self.send_tails_even))
        KR, WBW = self.KR, cfg.WBW
        self.gemm_to_dram(Abf, w_in, r0, 2 * cfg.AW, cfg.BW,
                          lambda m, tt: self.qT[m * 128:(m + 1) * 128, tt * TT:(tt + 1) * TT])
        self.gemm_to_dram(Abf, w_in, r0, 2 * cfg.AW + cfg.BW, cfg.BW,
                          lambda m, tt: self.agK_in[(m * 128) // KR][(m * 128) % KR:(m * 128) % KR + 128,
                                                                      tt * TT:(tt + 1) * TT])
        def vdst(tb, blk):
            out = []
            for c0 in range(0, WBW, KR):
                col = blk * WBW + c0
                out.append((self.agV_in[col // KR][tb * 128:(tb + 1) * 128, 0:KR], c0, c0 + KR))
            return out
        self.gemm_tokmajor(Abf, cfg.TOK, w_in, r0, 2 * cfg.AW + 2 * cfg.BW, cfg.BW, "dram", vdst)
        items.append(FJob(lambda: self.even_exchange_conv(e)))
        items.append(FJob(lambda: self.even_ln_silu(e)))
        items.append(FJob(lambda: self.diff_attention(e)))
        self.gemm_rmw(Bbf, self.w_even_out, r0, x_src)

    def conv_taps(self, eng, ne, chains, wcol, bcol, ntap, base):
        TT = self.cfg.TT
        for j in range(ntap):
            for c in chains:
                eng.wait(c["t"])
                src = c["ext"][:, base + j:base + j + TT]
                if j == 0:
                    if bcol is not None:
                        ins = ne.tensor_scalar(out=c["dst"], in0=src, scalar1=self.cst(wcol(c["k"], 0)),
                                               scalar2=self.cst(bcol(c["k"])), op0=ALU.mult, op1=ALU.add)
                    else:
                        ins = ne.tensor_scalar(out=c["dst"], in0=src, scalar1=self.cst(wcol(c["k"], 0)),
                                               scalar2=None, op0=ALU.mult)
                else:
                    ins = ne.scalar_tensor_tensor(out=c["dst"], in0=src, scalar=self.cst(wcol(c["k"], j)),
                                                  in1=c["dst"], op0=ALU.mult, op1=ALU.add)
                c["t"] = eng.tick(ins)

    def exchange_tails(self, tl, per, cT):
        cfg, nc = self.cfg, self.nc
        SP, DVE = self.SP, self.DVE
        NS = cfg.NS
        h = self.halo[:, 0:tl]
        nk = tl // (NS * per)
        hv = h.rearrange("p (k i t) -> p k i t", k=nk, i=NS)
        sel = lambda i: self.cst(cfg.c_sel + i)
        SP.wait(cT)
        t = None
        for r in range(cfg.G):
            tb = self.tails2[r % 2][:, 0:tl]
            SP.wait(t)
            tld = self.dma(SP, self.ldsem[r % 2], tb, self.agT_out[r * 128:(r + 1) * 128, 0:tl])
            DVE.wait(tld, t)
            if r == 0:
                t = DVE.tick(nc.vector.tensor_scalar(out=h, in0=tb, scalar1=sel(0), scalar2=None, op0=ALU.mult))
            elif r < cfg.G - 1:
                t = DVE.tick(nc.vector.scalar_tensor_tensor(out=h, in0=tb, scalar=sel(r), in1=h,
                                                            op0=ALU.mult, op1=ALU.add))
            else:
                tv = tb.rearrange("p (k i t) -> p k i t", k=nk, i=NS)
                t = DVE.tick(nc.vector.scalar_tensor_tensor(out=hv[:, :, 1:NS, :], in0=tv[:, :, 0:NS - 1, :],
                                                            scalar=sel(3), in1=hv[:, :, 1:NS, :],
                                                            op0=ALU.mult, op1=ALU.add))
        return t, hv

    def send_tails_even(self):
        cfg = self.cfg
        TT, NS, KA, HA = cfg.TT, cfg.NS, cfg.KA, cfg.HA
        Bf32 = self.Bf32()
        self.barrier(engines=(self.SP,))
        tin = self.agT_in[:, 0:cfg.TLA].rearrange("p (k i t) -> p k i t", k=KA, i=NS)
        tt_ = None
        for k in range(KA):
            src = Bf32[:, k, :].rearrange("p (i t) -> p i t", t=TT)[:, :, TT - HA:TT]
            tt_ = self.dma(self.SP, self.stsem[0], tin[:, k, :, :], src)
        self.cc_tails = self.allgather(self.agT_in[:, :], self.agT_out[:, :], [tt_])

    def send_tails_odd(self):
        cfg = self.cfg
        self.barrier(engines=(self.SP,))
        tt_ = self.dma(self.SP, self.stsem[0], self.agT_in[:, 0:cfg.TLC], self.tailbuf[:, 0:cfg.TLC])
        self.cc_tails = self.allgather(self.agT_in[:, :], self.agT_out[:, :], [tt_])

    def even_exchange_conv(self, e):
        cfg, nc = self.cfg, self.nc
        SP, DVE, ACT, POOL = self.SP, self.DVE, self.ACT, self.POOL
        TT, NS, KA, HA, CK = cfg.TT, cfg.NS, cfg.KA, cfg.HA, cfg.CK
        Bf32, Af32 = self.Bf32(), self.Af32()
        self.barrier()
        self.pool_sync()
        th, hv = self.exchange_tails(cfg.TLA, HA, self.cc_tails)
        self.cc_pair = []
        for c in range(self.nKc):
            self.allgather(self.agK_in[c][:, :], self.agK_out[c][:, :], [])
            self.allgather(self.agV_in[c][:, :], self.agV_out[c][:, :], [])
            self.cc_pair.append(("cc", self.ccsem, self.ccn))
        PE = self.PE
        wcol = lambda k, j: cfg.c_convaw + (e * KA + k) * CK + j
        bcol = lambda k: cfg.c_convab + e * KA + k
        extb = [x_.bitcast(BF16)[:, 0:HA + TT] for x_ in self.ext]
        extring = Ring(4)
        psring = Ring(4)
        dg_rel = []
        for k in range(KA):
            DVE.wait(th, *dg_rel)
            td = None
            for j in range(CK):
                td = DVE.tick(nc.vector.tensor_scalar(out=self.dg[:, j, :], in0=self.ident_b[:],
                                                      scalar1=self.cst(wcol(k, j)), scalar2=None, op0=ALU.mult))
            tpl = None
            for i in range(NS):
                xb, relx = extring.next()
                ACT.wait(th, *relx)
                ACT.tick(nc.scalar.copy(out=extb[xb][:, 0:HA], in_=hv[:, k, i, :]))
                t2 = ACT.tick(nc.scalar.copy(out=extb[xb][:, HA:HA + TT], in_=Bf32[:, k, i * TT:(i + 1) * TT]))
                pb, relp = psring.next()
                PE.wait(td, t2, *relp)
                ins = None
                for j in range(CK):
                    ins = nc.tensor.matmul(self.ps[:, pb, 0:TT], lhsT=self.dg[:, j, :], rhs=extb[xb][:, j:j + TT],
                                           start=(j == 0), stop=(j == CK - 1))
                tpl = PE.tick(ins)
                extring.release(xb, tpl)
                ACT.wait(tpl)
                ta = ACT.tick(nc.scalar.activation(out=Af32[:, k, i * TT:(i + 1) * TT], in_=self.ps[:, pb, 0:TT],
                                                   func=AF.Identity, bias=self.cst(bcol(k)), scale=1.0))
                psring.release(pb, ta)
            dg_rel = [tpl]
        self.barrier()

    def even_ln_silu(self, e):
        cfg, nc = self.cfg, self.nc
        DVE, ACT, PE = self.DVE, self.ACT, self.PE
        TT, NS, KA, AW = cfg.TT, cfg.NS, cfg.KA, cfg.AW
        Af32, Bbf = self.Af32(), self.Bbf()
        self.barrier()
        psring = Ring(2)
        sqring = Ring(2)
        tring = Ring(2)
        prev = None
        for i in range(NS):
            sl = slice(i * TT, (i + 1) * TT)
            pb, relp = psring.next()
            S1 = self.ps[:, 2 * pb, 0:TT]
            S2 = self.ps[:, 2 * pb + 1, 0:TT]
            PE.wait(*relp)
            tp = None
            for k in range(KA):
                nc.tensor.matmul(S1, lhsT=self.ones_f[:], rhs=Af32[:, k, sl], start=(k == 0), stop=(k == KA - 1))
                sb_, rels = sqring.next()
                sq = self.scr[sb_][:, 0:TT]
                ACT.wait(*rels)
                ta = ACT.tick(nc.scalar.activation(out=sq, in_=Af32[:, k, sl], func=AF.Square))
                PE.wait(ta)
                tp = PE.tick(nc.tensor.matmul(S2, lhsT=self.ones_f[:], rhs=sq, start=(k == 0), stop=(k == KA - 1)))
                sqring.release(sb_, tp)
            mean, msq, sd, rstd = (self.scr[j][:, 0:TT] for j in (4, 5, 6, 7))
            DVE.wait(tp, prev)
            t = DVE.tick(nc.vector.tensor_scalar(out=mean, in0=S1, scalar1=1.0 / AW, scalar2=None, op0=ALU.mult))
            DVE.wait(t)
            t = DVE.tick(nc.vector.tensor_tensor(out=msq, in0=mean, in1=mean, op=ALU.mult))
            DVE.wait(t)
            t = DVE.tick(nc.vector.scalar_tensor_tensor(out=msq, in0=S2, scalar=1.0 / AW, in1=msq,
                                                        op0=ALU.mult, op1=ALU.subtract))
            psring.release(pb, t)
            ACT.wait(t, prev)
            ts = ACT.tick(nc.scalar.activation(out=sd, in_=msq, func=AF.Sqrt, bias=self.eps_col(), scale=1.0))
            DVE.wait(ts)
            tr = DVE.tick(nc.vector.reciprocal(out=rstd, in_=sd))
            for k in range(KA):
                tb, relt = tring.next()
                tmp = self.scr[8 + tb][:, 0:TT]
                DVE.wait(tr, *relt)
                t = DVE.tick(nc.vector.tensor_tensor(out=tmp, in0=Af32[:, k, sl], in1=mean, op=ALU.subtract))
                DVE.wait(t)
                t = DVE.tick(nc.vector.tensor_tensor(out=tmp, in0=tmp, in1=rstd, op=ALU.mult))
                ACT.wait(t)
                ta = ACT.tick(nc.scalar.activation(out=Bbf[:, k, sl], in_=tmp, func=AF.Silu,
                                                   bias=self.cst(cfg.c_lnab + e * KA + k),
                                                   scale=self.cst(cfg.c_lnag + e * KA + k)))
                tring.release(tb, ta)
                prev = ta
        self.barrier()

    def eps_col(self):
        return self.epsv[:, 0:1]

    def diff_attention(self, e):
        cfg, nc = self.cfg, self.nc
        SP, DVE, ACT, PE, POOL = self.SP, self.DVE, self.ACT, self.PE, self.POOL
        TT, NS, G, SUB, NH, KA, TOK, SEQ = cfg.TT, cfg.NS, cfg.G, cfg.SUB, cfg.NH, cfg.KA, cfg.TOK, cfg.SEQ
        Bbf = self.Bbf()
        self.barrier()
        self.pool_sync()
        am = self.attn_mem
        Kb = [am[:, kb * SEQ:(kb + 1) * SEQ] for kb in range(2)]
        Vb = am[:, 2 * SEQ:3 * SEQ].rearrange("p (c e) -> p c e", e=128)
        mask = am[:, 3 * SEQ:3 * SEQ + 4 * SUB * TT].rearrange("p (m t) -> p m t", t=TT)
        pm = self.phase_mem
        qh = [pm[:, j * (TOK // 2):(j + 1) * (TOK // 2)].bitcast(BF16) for j in range(2)]
        e0 = TOK
        Er = [pm[:, e0 + j * TT:e0 + (j + 1) * TT].bitcast(BF16).rearrange("p (c t) -> p c t", c=2) for j in range(4)]
        tmask = self.dma(SP, self.ldsem[7], mask, self.cmask_in[:, 0:4 * SUB * TT].rearrange("p (m t) -> p m t", t=TT))
        scale = 64 ** -0.5
        nkv_tiles = G * NS

        def load_k(h):
            SP.wait(self.cc_pair[(h * 128) // self.KR])
            kb = h % 2
            kv = Kb[kb].rearrange("p (i r t) -> p i r t", i=NS, r=G)
            t = None
            for r in range(G):
                kc_, ko_ = (h * 128) // self.KR, (h * 128) % self.KR
                src = self.agK_out[kc_][r * self.KR + ko_:r * self.KR + ko_ + 128, :].rearrange("p (i t) -> p i t", t=TT)
                t = self.dma(SP, self.ldsem[kb], kv[:, :, r, :], src)
            return t

        def load_q(h):
            return self.dma(SP, self.ldsem[2 + h % 2], qh[h % 2], self.qT[h * 128:(h + 1) * 128, :])

        def load_v(h):
            vv = Vb.rearrange("p (i r s) e -> p i r s e", i=NS, r=G)
            t = None
            for r in range(G):
                for i2 in range(NS):
                    vo_ = (h * 128) % self.KR
                    src = self.agV_out[(h * 128) // self.KR][r * TOK + i2 * TT:r * TOK + (i2 + 1) * TT, vo_:vo_ + 128]
                    src = src.rearrange("(s p) e -> p s e", p=128)
                    t = self.dma(SP, self.ldsem[4], vv[:, i2, r, :, :], src)
            return t

        tk = {0: load_k(0)}
        tq = {0: load_q(0)}
        tv = {0: load_v(0)}
        if NH > 1:
            tk[1] = load_k(1)
            tq[1] = load_q(1)
        spair = Ring(2)
        ering = Ring(len(Er))
        seq = [(h, i, kc, G * (i + 1) * SUB) for h in range(NH) for i in range(NS) for kc in range(G * (i + 1) * SUB)]
        stA = {}
        state = {"oz_rel": [], "eb": 0, "tsum": None}
        esring = Ring(2)
        LOOK = 2

        def stage_a(n):
            h, i, kc, nk = seq[n]
            Kh, Q = Kb[h % 2], qh[h % 2]
            t = kc // SUB
            s = kc % SUB
            sp, rels = spair.next()
            PE.wait(tk[h], tq[h], *rels)
            tps = None
            for c in range(2):
                tps = nc.tensor.matmul(self.ps[:, 2 * sp + c, 0:TT],
                                       lhsT=Kh[c * 64:(c + 1) * 64, t * TT + s * 128:t * TT + (s + 1) * 128],
                                       rhs=Q[c * 64:(c + 1) * 64, i * TT:(i + 1) * TT], start=True, stop=True)
            tps = PE.tick(tps)
            er, rele = ering.next()
            E = Er[er]
            ACT.wait(tps, *rele)
            ta = ACT.tick(nc.scalar.activation(out=E, in_=self.ps[:, 2 * sp:2 * sp + 2, 0:TT], func=AF.Exp, scale=scale))
            spair.release(sp, ta)
            te = ta
            if t >= G * i:
                r = t - G * i
                DVE.wait(ta, tmask)
                for c in range(2):
                    te = DVE.tick(nc.vector.tensor_tensor(out=E[:, c, :], in0=E[:, c, :],
                                                          in1=mask[:, r * SUB + s, :], op=ALU.mult))
            if kc == 0:
                state["eb"], rel_es = esring.next()
                state["rel_es"] = rel_es
                state["tsum"] = [None, None]
            par = kc % 2
            eng, ne = (DVE, nc.vector)
            Es = self.scr[4 + 2 * state["eb"] + par][:, 0:TT]
            if kc < 2:
                eng.wait(te, *state["rel_es"])
                tsum = eng.tick(ne.tensor_copy(out=Es, in_=E[:, 1, :]))
            else:
                eng.wait(te, state["tsum"][par])
                tsum = eng.tick(ne.tensor_tensor(out=Es, in0=Es, in1=E[:, 1, :], op=ALU.add))
            state["tsum"][par] = tsum
            if kc == nk - 1:
                state["tsum_final"] = list(state["tsum"])
            stA[n] = (er, te, tsum, state["eb"], state.get("tsum_final") if kc == nk - 1 else None)

        def stage_b(n):
            h, i, kc, nk = seq[n]
            er, te, tsum, eb, tfin = stA.pop(n)
            E = Er[er]
            PE.wait(te, tv[h])
            if kc == 0:
                PE.wait(*state["oz_rel"])
                state["oz_rel"] = []
            ins = None
            for c in range(2):
                ins = nc.tensor.matmul(self.ps[:, 4 + c, 0:TT], lhsT=Vb[:, kc, :], rhs=E[:, c, :],
                                       start=(kc == 0), stop=(kc == nk - 1))
            ins = nc.tensor.matmul(self.ps[:, 6, 0:TT], lhsT=self.ones_b[:], rhs=E[:, 0, :],
                                   start=(kc == 0), stop=(kc == nk - 1))
            tp_last = PE.tick(ins)
            ering.release(er, tp_last, tsum)
            if kc == nk - 1:
                PE.wait(*tfin)
                nc.tensor.matmul(self.ps[:, 7, 0:TT], lhsT=self.ones_f[:], rhs=self.scr[4 + 2 * eb][:, 0:TT],
                                 start=True, stop=False)
                ins = nc.tensor.matmul(self.ps[:, 7, 0:TT], lhsT=self.ones_f[:], rhs=self.scr[5 + 2 * eb][:, 0:TT],
                                       start=False, stop=True)
                tp_last = PE.tick(ins)
                esring.release(eb, tp_last)
                rz0, t0, rz1, t1 = (self.scr[j][:, 0:TT] for j in (0, 1, 2, 3))
                DVE.wait(tp_last)
                a = DVE.tick(nc.vector.reciprocal(out=rz0, in_=self.ps[:, 6, 0:TT]))
                b_ = DVE.tick(nc.vector.reciprocal(out=rz1, in_=self.ps[:, 7, 0:TT]))
                DVE.wait(a)
                c_ = DVE.tick(nc.vector.tensor_tensor(out=t0, in0=self.ps[:, 4, 0:TT], in1=rz0, op=ALU.mult))
                DVE.wait(b_)
                d = DVE.tick(nc.vector.tensor_tensor(out=t1, in0=self.ps[:, 5, 0:TT], in1=rz1, op=ALU.mult))
                state["oz_rel"] = [d]
                DVE.wait(c_, d)
                DVE.tick(nc.vector.scalar_tensor_tensor(out=Bbf[:, KA + h, i * TT:(i + 1) * TT], in0=t1,
                                                        scalar=self.lamv[:, e:e + 1], in1=t0,
                                                        op0=ALU.mult, op1=ALU.add))
                if i == NS - 1:
                    SP.wait(tp_last)
                    if h + 1 < NH:
                        tv[h + 1] = load_v(h + 1)
                    if h + 2 < NH:
                        tk[h + 2] = load_k(h + 2)
                        tq[h + 2] = load_q(h + 2)

        N = len(seq)
        for n in range(N + LOOK):
            if n < N:
                stage_a(n)
            if n - LOOK >= 0:
                stage_b(n - LOOK)
        self.barrier()
        psring = Ring(2)
        sqring = Ring(2)
        for h in range(NH):
            for i in range(NS):
                o = Bbf[:, KA + h, i * TT:(i + 1) * TT]
                sb_, rels = sqring.next()
                sq = self.scr[sb_][:, 0:TT]
                ACT.wait(*rels)
                ta = ACT.tick(nc.scalar.activation(out=sq, in_=o, func=AF.Square))
                pb, relp = psring.next()
                PE.wait(ta, *relp)
                tp = PE.tick(nc.tensor.matmul(self.ps[:, pb, 0:TT], lhsT=self.ones_f[:], rhs=sq, start=True, stop=True))
                sd = self.scr[2 + sb_][:, 0:TT]
                rstd = self.scr[4 + sb_][:, 0:TT]
                ACT.wait(tp)
                ts = ACT.tick(nc.scalar.activation(out=sd, in_=self.ps[:, pb, 0:TT], func=AF.Sqrt,
                                                   bias=self.eps_col(), scale=1.0 / 128))
                psring.release(pb, ts)
                DVE.wait(ts)
                tr = DVE.tick(nc.vector.reciprocal(out=rstd, in_=sd))
                DVE.wait(tr)
                td = DVE.tick(nc.vector.scalar_tensor_tensor(out=o, in0=o, scalar=self.subg[:, e:e + 1], in1=rstd,
                                                             op0=ALU.mult, op1=ALU.mult))
                sqring.release(sb_, td)
        self.barrier()

    def odd_layer(self, l, x_src):
        cfg = self.cfg
        o = l // 2
        D, TT = cfg.D, cfg.TT
        items = self.items
        items.append(FJob(lambda: self.norm_phase(x_src, cfg.gain_col("mix", l))))
        Abf, Bbf = self.Abf(), self.Bbf()
        r0 = o * D
        self.gemm_gated(Abf, self.w_odd_in, r0, D, 2 * D, D, "mul", None)
        items.append(FJob(self.send_tails_odd))
        self.gemm_to_sbuf(Abf, self.w_odd_in, r0, 0, D, lambda m, tt: Bbf[:, m, tt * TT:(tt + 1) * TT], "copy")
        items.append(FJob(lambda: self.odd_exchange_conv(o)))
        self.gemm_rmw(Abf, self.w_odd_out, r0, x_src)

    def odd_exchange_conv(self, o):
        cfg, nc = self.cfg, self.nc
        SP, DVE, ACT, POOL = self.SP, self.DVE, self.ACT, self.POOL
        TT, NS, KD, HC, HA, SK = cfg.TT, cfg.NS, cfg.KD, cfg.HC, cfg.HA, cfg.SK
        Abf, Bbf = self.Abf(), self.Bbf()
        self.barrier()
        self.pool_sync()
        th, hv = self.exchange_tails(cfg.TLC, HC, self.cc_tails)
        PE = self.PE
        wcol = lambda k, j: cfg.c_convcw + (o * KD + k) * SK + j
        base = HA - HC
        extb = [x_.bitcast(BF16)[:, 0:HA + TT] for x_ in self.ext]
        extring = Ring(4)
        ldring = Ring(4)
        psring = Ring(4)
        dg_rel = [[], []]
        tiles = [(k, i) for k in range(KD) for i in range(NS)]
        tld = {}

        def load(n):
            if n >= len(tiles) or n in tld:
                return
            k, i = tiles[n]
            s, rel = ldring.next()
            SP.wait(*rel)
            tld[n] = (s, self.dma(SP, self.ldsem[s], self.scr[s][:, 0:TT],
                                  self.cuT[k * 128:(k + 1) * 128, i * TT:(i + 1) * TT]))

        for n, (k, i) in enumerate(tiles):
            for q in range(n, n + 3):
                load(q)
            db = k % 2
            if i == 0:
                DVE.wait(th, *dg_rel[db])
                td = None
                for j in range(SK):
                    td = DVE.tick(nc.vector.tensor_scalar(out=self.dg[:, db * SK + j, :], in0=self.ident_b[:],
                                                          scalar1=self.cst(wcol(k, j)), scalar2=None, op0=ALU.mult))
                tdg = td
            s, tl = tld[n]
            xb, relx = extring.next()
            ACT.wait(th, tl, *relx)
            ACT.tick(nc.scalar.copy(out=extb[xb][:, base:HA], in_=hv[:, k, i, :]))
            t2 = ACT.tick(nc.scalar.copy(out=extb[xb][:, HA:HA + TT], in_=self.scr[s][:, 0:TT]))
            ldring.release(s, t2)
            pb, relp = psring.next()
            PE.wait(tdg, t2, *relp)
            ins = None
            for j in range(SK):
                ins = nc.tensor.matmul(self.ps[:, pb, 0:TT], lhsT=self.dg[:, db * SK + j, :],
                                       rhs=extb[xb][:, base + j:base + j + TT], start=(j == 0), stop=(j == SK - 1))
            tpl = PE.tick(ins)
            extring.release(xb, tpl)
            if i == NS - 1:
                dg_rel[db] = [tpl]
            DVE.wait(tpl)
            tg = DVE.tick(nc.vector.tensor_tensor(out=Abf[:, k, i * TT:(i + 1) * TT], in0=self.ps[:, pb, 0:TT],
                                                  in1=Bbf[:, k, i * TT:(i + 1) * TT], op=ALU.mult))
            psring.release(pb, tg)
        self.barrier()

    def xattn_layer(self, l):
        cfg = self.cfg
        D, TT, KD, MEM, MC, WBW = cfg.D, cfg.TT, cfg.KD, cfg.MEM, cfg.MC, cfg.WBW
        items = self.items
        Abf, Bbf = self.Abf(), self.Bbf()
        items.append(FJob(lambda: self.norm_phase(self.xres, cfg.gain_col("xattn", l))))
        self.gemm_to_sbuf(Abf, self.w_xq, l * D, 0, D, lambda m, tt: Bbf[:, m, tt * TT:(tt + 1) * TT], "copy")
        n1 = KD * MEM
        memn = self.A[:, 0:n1].rearrange("p (k t) -> p k t", k=KD)
        kmem = self.A[:, n1:2 * n1].rearrange("p (k t) -> p k t", k=KD)
        vmem = self.A[:, 2 * n1:3 * n1].rearrange("p (c d) -> p c d", c=MC)

        def load_memn():
            self.barrier()
            self.dma(self.SP, self.ldsem[0], memn, self.memnT.rearrange("(k p) t -> p k t", p=128))
            self.barrier()
        items.append(FJob(load_memn))
        self.gemm_to_sbuf(memn, self.w_xk, l * D, 0, D, lambda m, tt: kmem[:, m, :], "copy", tok_tiles=[(0, MEM)])
        self.gemm_tokmajor(memn, MEM, self.w_xv, l * D, 0, D, "sbuf",
                           lambda tb, blk: vmem[:, tb, blk * WBW:(blk + 1) * WBW])
        items.append(FJob(lambda: self.xattn_core(kmem, vmem)))
        self.gemm_rmw(Bbf, self.w_xo, l * D, self.xres)

    def xattn_core(self, kmem, vmem):
        cfg, nc = self.cfg, self.nc
        DVE, ACT, PE = self.DVE, self.ACT, self.PE
        TT, NS, XH, XKC, MC = cfg.TT, cfg.NS, cfg.XH, cfg.XKC, cfg.MC
        Bbf = self.Bbf()
        self.barrier()
        pm = self.phase_mem
        Er = [pm[:, j * (MC * TT // 2):(j + 1) * (MC * TT // 2)].bitcast(BF16).rearrange("p (c t) -> p c t", c=MC)
              for j in range(3)]
        scale = cfg.XHD ** -0.5
        sring = Ring(2)
        ering = Ring(len(Er))
        oring = Ring(2)
        seq = [(h, tt) for h in range(XH) for tt in range(NS)]
        stA = {}
        state = {"zrel": []}

        def stage_a(n):
            h, tt = seq[n]
            sl = slice(tt * TT, (tt + 1) * TT)
            sp, rels = sring.next()
            PE.wait(*rels)
            tps = None
            for mc in range(MC):
                for dc in range(XKC):
                    tps = nc.tensor.matmul(self.ps[:, sp * MC + mc, 0:TT],
                                           lhsT=kmem[:, h * XKC + dc, mc * 128:(mc + 1) * 128],
                                           rhs=Bbf[:, h * XKC + dc, sl], start=(dc == 0), stop=(dc == XKC - 1))
            tps = PE.tick(tps)
            er, rele = ering.next()
            E = Er[er]
            ACT.wait(tps, *rele)
            ta = ACT.tick(nc.scalar.activation(out=E, in_=self.ps[:, sp * MC:(sp + 1) * MC, 0:TT], func=AF.Exp,
                                               scale=scale))
            sring.release(sp, ta)
            stA[n] = (er, ta)

        def stage_b(n):
            h, tt = seq[n]
            sl = slice(tt * TT, (tt + 1) * TT)
            er, ta = stA.pop(n)
            E = Er[er]
            PE.wait(ta, *state["zrel"])
            tz = None
            for mc in range(MC):
                tz = nc.tensor.matmul(self.ps[:, 4, 0:TT], lhsT=self.ones_b[:], rhs=E[:, mc, :],
                                      start=(mc == 0), stop=(mc == MC - 1))
            tz = PE.tick(tz)
            rz = self.scr[er][:, 0:TT]
            DVE.wait(tz)
            trz = DVE.tick(nc.vector.reciprocal(out=rz, in_=self.ps[:, 4, 0:TT]))
            state["zrel"] = [trz]
            tpo = td = None
            for ec in range(XKC):
                ob, relo = oring.next()
                PE.wait(*relo)
                for mc in range(MC):
                    tpo = nc.tensor.matmul(self.ps[:, 5 + ob, 0:TT],
                                           lhsT=vmem[:, mc, (h * XKC + ec) * 128:(h * XKC + ec + 1) * 128],
                                           rhs=E[:, mc, :], start=(mc == 0), stop=(mc == MC - 1))
                tpo = PE.tick(tpo)
                DVE.wait(tpo, trz)
                td = DVE.tick(nc.vector.tensor_tensor(out=Bbf[:, h * XKC + ec, sl], in0=self.ps[:, 5 + ob, 0:TT],
                                                      in1=rz, op=ALU.mult))
                oring.release(ob, td)
            ering.release(er, tpo, td)

        N = len(seq)
        LOOK = 2
        for n in range(N + LOOK):
            if n < N:
                stage_a(n)
            if n - LOOK >= 0:
                stage_b(n - LOOK)
        self.barrier()

    def mlp_layer(self, l):
        cfg = self.cfg
        D, TT = cfg.D, cfg.TT
        Abf, Bbf = self.Abf(), self.Bbf()
        self.items.append(FJob(lambda: self.norm_phase(self.xres, cfg.gain_col("mlp", l))))
        for g in range(cfg.NG):
            self.gemm_to_sbuf(Abf, self.w_m1, l * D, g * D, D, lambda m, tt: Bbf[:, m, tt * TT:(tt + 1) * TT], "relu2")
            self.gemm_rmw(Bbf, self.w_m2, l * cfg.DFF + g * D, self.xres)

    def dump_x(self):
        self.barrier()
        n = self.cfg.D // 128
        for k in range(n):
            self.dma(self.SP, self.outsem, self.yT[k * 128:(k + 1) * 128, :], self.xres[k * 128:(k + 1) * 128, :])

    def finish(self):
        self.barrier()
        self.POOL.wait(*self.pending_bar)


def pack_consts(cfg, p, j):
    c = np.zeros((128, cfg.NCONST), np.float32)
    KD, KA = cfg.KD, cfg.KA

    def put_vec(col, v):
        v = np.asarray(v, np.float32)
        n = v.shape[0] // 128
        c[:, col:col + n] = v.reshape(n, 128).T

    put_vec(cfg.gain_col("mem"), p["mem_norm_g"])
    put_vec(cfg.gain_col("final"), p["final_norm_g"])
    for l in range(cfg.depth):
        put_vec(cfg.gain_col("mix", l), p["mix_norm_g"][l])
        put_vec(cfg.gain_col("xattn", l), p["xattn_norm_g"][l])
        put_vec(cfg.gain_col("mlp", l), p["mlp_norm_g"][l])
    for e in range(cfg.NEVEN):
        w = np.asarray(p["conv_a_w"][e], np.float32)
        blk = w.T.reshape(KA, 128, cfg.CK).transpose(1, 0, 2).reshape(128, KA * cfg.CK)
        c[:, cfg.c_convaw + e * KA * cfg.CK: cfg.c_convaw + (e + 1) * KA * cfg.CK] = blk
        put_vec(cfg.c_convab + e * KA, p["conv_a_b"][e])
        put_vec(cfg.c_lnag + e * KA, p["ln_a_g"][e])
        put_vec(cfg.c_lnab + e * KA, p["ln_a_b"][e])
        c[:, cfg.c_subg + e] = np.asarray(p["subln_g"][e], np.float32)
        for n, name in enumerate(("lambda_q1", "lambda_k1", "lambda_q2", "lambda_k2")):
            c[0:64, cfg.c_lam + 4 * e + n] = np.asarray(p[name][e], np.float32)
    for o in range(cfg.NODD):
        w = np.asarray(p["conv_c_w"][o], np.float32)
        blk = w.T.reshape(KD, 128, cfg.SK).transpose(1, 0, 2).reshape(128, KD * cfg.SK)
        c[:, cfg.c_convcw + o * KD * cfg.SK: cfg.c_convcw + (o + 1) * KD * cfg.SK] = blk
    sel = np.zeros(4, np.float32)
    if j > 0:
        sel[j - 1] = 1.0
    else:
        sel[3] = 1.0
    c[:, cfg.c_sel:cfg.c_sel + 4] = sel[None, :]
    return c


def make_mask(cfg, j):
    TT, SUB = cfg.TT, cfg.SUB
    m = np.zeros((128, 4 * SUB, TT), np.float32)
    pidx = np.arange(128)[:, None]
    cidx = np.arange(TT)[None, :]
    for r in range(4):
        for s in range(SUB):
            m[:, r * SUB + s, :] = ((r * TT + s * 128 + pidx) <= (j * TT + cidx)).astype(np.float32)
    m = np.concatenate([m.reshape(128, 4 * SUB * TT), np.eye(128, dtype=np.float32)], axis=1)
    return m.astype(ml_dtypes.bfloat16)


_PROG_CACHE = {}


def run(cfg, inputs, debug_stop=None):
    key = (cfg.D, cfg.TT, cfg.depth, debug_stop)
    if key not in _PROG_CACHE:
        _PROG_CACHE[key] = Prog(cfg, debug_stop).build()
    nc = _PROG_CACHE[key]
    p = {k: np.asarray(v) for k, v in inputs.items()}
    D, TT, NS, G = cfg.D, cfg.TT, cfg.NS, cfg.G
    x, mem = p["x"], p["mem"]
    shared = {
        "even_w_in": p["even_w_in"].reshape(-1, cfg.EIN), "even_w_out": p["even_w_out"].reshape(-1, D),
        "odd_w_in": p["odd_w_in"].reshape(-1, cfg.OIN), "odd_w_out": p["odd_w_out"].reshape(-1, D),
        "xq_w": p["xq_w"].reshape(-1, D), "xk_w": p["xk_w"].reshape(-1, D),
        "xv_w": p["xv_w"].reshape(-1, D), "xo_w": p["xo_w"].reshape(-1, D),
        "mlp_w1": p["mlp_w1"].reshape(-1, cfg.DFF), "mlp_w2": p["mlp_w2"].reshape(-1, D),
    }
    in_maps = []
    for c in range(cfg.NCORES):
        b, j = divmod(c, G)
        xb = x[b].reshape(NS, G, TT, D)[:, j].reshape(NS * TT, D)
        m = dict(shared)
        m["xT"] = np.ascontiguousarray(xb.T)
        m["memT"] = np.ascontiguousarray(mem[b].T)
        m["consts"] = pack_consts(cfg, p, j)
        m["cmask"] = make_mask(cfg, j)
        in_maps.append(m)
    res = run_bass_kernel_spmd(nc, in_maps, core_ids=list(range(cfg.NCORES)))
    out = np.empty((cfg.NB, cfg.SEQ, D), np.float32)
    ov = out.reshape(cfg.NB, NS, G, TT, D)
    for c in range(cfg.NCORES):
        b, j = divmod(c, G)
        yT = np.asarray(res.results[c]["yT"])
        ov[b, :, j] = yT.T.reshape(NS, TT, D)
    return out


def kernel(**inputs):
    return run(BIG, inputs)
```
